# Optimizing a Trainium2 kernel written in Bass

```python
import math
import jax, jax.numpy as jnp
from jax import lax
import numpy as np

D_MODEL = 1024
BATCH = 4
SEQ = 8192
DEPTH = 1
DEC_BATCH = 128
DEC_SEQ = 4
PAST_LEN = 16384
PAGE_SIZE = 128

D_MIX = D_MODEL
SSM_WIDTH = D_MIX // 2
SSM_HEAD_DIM = 64
SSM_HEADS = SSM_WIDTH // SSM_HEAD_DIM
SSM_GROUPS = 2
SSM_HPG = SSM_HEADS // SSM_GROUPS
D_STATE = 128
CONV_W = 4
CONV_DIM = SSM_WIDTH + 2 * SSM_GROUPS * D_STATE
SSD_CHUNK = 128
ATT_WIDTH = D_MIX - SSM_WIDTH
HEAD_DIM = 64
N_Q = ATT_WIDTH // HEAD_DIM
N_KV = 2
Q_PER_KV = N_Q // N_KV
WINDOW = 128
ROT_DIM = HEAD_DIM // 4
ROPE_THETA = 500000.0
D_FF = 4 * D_MODEL
LN_EPS = 1e-5
RMS_EPS = 1e-5
ALPHA = (2 * DEPTH) ** 0.25
BETA = (8 * DEPTH) ** -0.25

Z_END = SSM_WIDTH
XBC_END = Z_END + CONV_DIM
DT_END = XBC_END + SSM_HEADS
Q_END = DT_END + N_Q * HEAD_DIM
K_END = Q_END + N_KV * HEAD_DIM
D_IN_PROJ = K_END + N_KV * HEAD_DIM

kernel_name = 'hymba_ssd_swa_sink_deepnorm_adaln_step'


def window_rows():
    return min(WINDOW, PAST_LEN)


def layer_norm(x, g, b):
    xf = x.astype(jnp.float32)
    mu = jnp.mean(xf, -1, keepdims=True)
    var = jnp.mean(jnp.square(xf - mu), -1, keepdims=True)
    return ((xf - mu) * lax.rsqrt(var + LN_EPS) * g.astype(jnp.float32) + b.astype(jnp.float32)).astype(x.dtype)


def rotary(x, pos):
    half = ROT_DIM // 2
    inv = ROPE_THETA ** (-jnp.arange(half, dtype=jnp.float32) / half)
    ang = pos.astype(jnp.float32)[:, None] * inv[None, :]
    cos = jnp.cos(ang)[None, :, None, :]
    sin = jnp.sin(ang)[None, :, None, :]
    xr = x[..., :ROT_DIM].astype(jnp.float32)
    x1, x2 = xr[..., :half], xr[..., half:]
    rot = jnp.concatenate([x1 * cos - x2 * sin, x2 * cos + x1 * sin], -1)
    return jnp.concatenate([rot.astype(x.dtype), x[..., ROT_DIM:]], -1)


def causal_conv(u, buf, w, b):
    t = u.shape[1]
    up = jnp.concatenate([buf.astype(u.dtype), u], 1)
    y = up[:, 0:t] * w[0]
    for j in range(1, CONV_W):
        y = y + up[:, j:j + t] * w[j]
    return jax.nn.silu(y + b), up[:, t:]


def segsum(a):
    cs = jnp.cumsum(a, -1)
    n = a.shape[-1]
    mask = jnp.tril(jnp.ones((n, n), dtype=bool))
    return jnp.where(mask, cs[..., :, None] - cs[..., None, :], -jnp.inf)


def ssd_scan(x, dt, A, B, C, h0):
    b, t = x.shape[:2]
    l = math.gcd(t, SSD_CHUNK)
    c = t // l
    G, R, P, N = SSM_GROUPS, SSM_HPG, SSM_HEAD_DIM, D_STATE
    X = (x * dt[..., None]).reshape(b, c, l, G, R, P)
    a = (dt * A).reshape(b, c, l, G, R).transpose(0, 3, 4, 1, 2)
    Bc = B.reshape(b, c, l, G, N)
    Cc = C.reshape(b, c, l, G, N)
    a_cs = jnp.cumsum(a, -1)
    Lmat = jnp.exp(segsum(a))
    cb = jnp.einsum('bclgn,bcsgn->bgcls', Cc, Bc)
    y_diag = jnp.einsum('bgcls,bgrcls,bcsgrp->bclgrp', cb, Lmat, X)
    decay_states = jnp.exp(a_cs[..., -1:] - a_cs)
    states = jnp.einsum('bcsgn,bgrcs,bcsgrp->bcgrpn', Bc, decay_states, X)
    states = jnp.concatenate([h0.reshape(b, G, R, P, N)[:, None], states], 1)
    chunk_tot = jnp.pad(a_cs[..., -1], ((0, 0), (0, 0), (0, 0), (1, 0)))
    decay_chunk = jnp.exp(segsum(chunk_tot))
    states = jnp.einsum('bgrzc,bcgrpn->bzgrpn', decay_chunk, states)
    prev, final = states[:, :-1], states[:, -1]
    y_off = jnp.einsum('bclgn,bcgrpn,bgrcl->bclgrp', Cc, prev, jnp.exp(a_cs))
    y = (y_diag + y_off).reshape(b, t, SSM_HEADS, P)
    return y, final.reshape(b, SSM_HEADS, P, N)


def ssd_branch(z, xbc, dt_raw, conv_buf, h0, conv_w, conv_b, dt_bias, A_log, D_skip, norm_w):
    b, t = z.shape[:2]
    xbc, conv_new = causal_conv(xbc, conv_buf, conv_w, conv_b)
    xbc = xbc.astype(jnp.float32)
    xs = xbc[..., :SSM_WIDTH].reshape(b, t, SSM_HEADS, SSM_HEAD_DIM)
    Bm = xbc[..., SSM_WIDTH:SSM_WIDTH + SSM_GROUPS * D_STATE].reshape(b, t, SSM_GROUPS, D_STATE)
    Cm = xbc[..., SSM_WIDTH + SSM_GROUPS * D_STATE:].reshape(b, t, SSM_GROUPS, D_STATE)
    dt = jax.nn.softplus(dt_raw.astype(jnp.float32) + dt_bias.astype(jnp.float32))
    A = -jnp.exp(A_log.astype(jnp.float32))
    y, h_new = ssd_scan(xs, dt, A, Bm, Cm, h0.astype(jnp.float32))
    y = y + xs * D_skip.astype(jnp.float32)[:, None]
    y = y.reshape(b, t, SSM_WIDTH) * jax.nn.silu(z.astype(jnp.float32))
    yg = y.reshape(b, t, SSM_GROUPS, SSM_WIDTH // SSM_GROUPS)
    yg = yg * lax.rsqrt(jnp.mean(jnp.square(yg), -1, keepdims=True) + RMS_EPS)
    y = yg.reshape(b, t, SSM_WIDTH) * norm_w.astype(jnp.float32)
    return y.astype(z.dtype), conv_new, h_new


def band_mask(qpos, kpos):
    return (kpos <= qpos) & (kpos > qpos - WINDOW) & (kpos >= 0)


def sink_softmax(s, sink):
    m = jnp.maximum(jnp.max(s, -1, keepdims=True), sink)
    p = jnp.exp(s - m)
    return p / (jnp.sum(p, -1, keepdims=True) + jnp.exp(sink - m))


def swa_prompt(q, k, v, sinks):
    b, t = q.shape[:2]
    nb = t // WINDOW
    qb = q.astype(jnp.float32).reshape(b, nb, WINDOW, N_KV, Q_PER_KV, HEAD_DIM)
    kb = k.astype(jnp.float32).reshape(b, nb, WINDOW, N_KV, HEAD_DIM)
    vb = v.astype(jnp.float32).reshape(b, nb, WINDOW, N_KV, HEAD_DIM)
    pad = ((0, 0), (1, 0), (0, 0), (0, 0), (0, 0))
    kk = jnp.concatenate([jnp.pad(kb, pad)[:, :-1], kb], 2)
    vv = jnp.concatenate([jnp.pad(vb, pad)[:, :-1], vb], 2)
    qpos = jnp.arange(t).reshape(nb, WINDOW)
    kpos = qpos[:, :1] - WINDOW + jnp.arange(2 * WINDOW)[None, :]
    mask = band_mask(qpos[:, :, None], kpos[:, None, :])
    s = jnp.einsum('bnqgrd,bnkgd->bngrqk', qb, kk) * (HEAD_DIM ** -0.5)
    s = jnp.where(mask[None, :, None, None], s, -jnp.inf)
    p = sink_softmax(s, sinks.astype(jnp.float32).reshape(N_KV, Q_PER_KV)[:, :, None, None])
    o = jnp.einsum('bngrqk,bnkgd->bnqgrd', p, vv)
    return o.reshape(b, t, ATT_WIDTH).astype(q.dtype)


def swa_decode(q, k, v, k_buf, v_buf, pos, sinks):
    b, t = q.shape[:2]
    wb = k_buf.shape[1]
    kk = jnp.concatenate([k_buf.astype(k.dtype), k], 1)
    vv = jnp.concatenate([v_buf.astype(v.dtype), v], 1)
    kpos = pos[0] - wb + jnp.arange(wb + t)
    mask = band_mask(pos[:, None], kpos[None, :])
    qg = q.astype(jnp.float32).reshape(b, t, N_KV, Q_PER_KV, HEAD_DIM)
    s = jnp.einsum('bqgrd,bkgd->bgrqk', qg, kk.astype(jnp.float32)) * (HEAD_DIM ** -0.5)
    s = jnp.where(mask, s, -jnp.inf)
    p = sink_softmax(s, sinks.astype(jnp.float32).reshape(N_KV, Q_PER_KV)[:, :, None, None])
    o = jnp.einsum('bgrqk,bkgd->bqgrd', p, vv.astype(jnp.float32))
    return o.reshape(b, t, ATT_WIDTH).astype(q.dtype), kk[:, t:], vv[:, t:]


def decoder_layer(x, c, pos, conv_buf, h0, k_buf, v_buf, w_ada, b_ada, w_in, conv_w, conv_b,
                  dt_bias, A_log, D_skip, ssm_norm_w, sinks, w_out, ln1_g, ln1_b,
                  w_up, w_down, ln2_g, ln2_b):
    b, t, _ = x.shape
    ada = jnp.einsum('bd,de->be', jax.nn.silu(c), w_ada) + b_ada
    sh_m, sc_m, g_m, sh_f, sc_f, g_f = jnp.split(ada[:, None, :], 6, axis=-1)
    h = x * (1.0 + sc_m) + sh_m
    p = jnp.einsum('btd,de->bte', h, w_in)
    z = p[..., :Z_END]
    xbc = p[..., Z_END:XBC_END]
    dt_raw = p[..., XBC_END:DT_END]
    q = rotary(p[..., DT_END:Q_END].reshape(b, t, N_Q, HEAD_DIM), pos)
    k = rotary(p[..., Q_END:K_END].reshape(b, t, N_KV, HEAD_DIM), pos)
    v = p[..., K_END:].reshape(b, t, N_KV, HEAD_DIM)
    if conv_buf is None:
        conv_buf = jnp.zeros((b, CONV_W - 1, CONV_DIM), x.dtype)
        h0 = jnp.zeros((b, SSM_HEADS, SSM_HEAD_DIM, D_STATE), jnp.float32)
    y_ssm, conv_new, h_new = ssd_branch(z, xbc, dt_raw, conv_buf, h0, conv_w, conv_b,
                                        dt_bias, A_log, D_skip, ssm_norm_w)
    if k_buf is None:
        wb = window_rows()
        y_att = swa_prompt(q, k, v, sinks)
        k_new, v_new = k[:, t - wb:], v[:, t - wb:]
    else:
        y_att, k_new, v_new = swa_decode(q, k, v, k_buf, v_buf, pos, sinks)
    mixed = jnp.einsum('bte,ed->btd', jnp.concatenate([y_ssm, y_att], -1), w_out)
    x = layer_norm(ALPHA * x + g_m * mixed, ln1_g, ln1_b)
    h = x * (1.0 + sc_f) + sh_f
    u = jnp.square(jax.nn.relu(jnp.einsum('btd,df->btf', h, w_up)))
    f = jnp.einsum('btf,fd->btd', u, w_down)
    x = layer_norm(ALPHA * x + g_f * f, ln2_g, ln2_b)
    return x, conv_new, h_new, k_new, v_new


def setup_inputs(seed: int = 0) -> dict:
    key = jax.random.key(seed)
    ks = jax.random.split(key, 32)
    f32 = jnp.float32

    def nrm(k, shape, s=1.0):
        return jax.random.normal(k, shape, f32) * s

    wb = window_rows()
    dt0 = jnp.exp(jax.random.uniform(ks[20], (DEPTH, SSM_HEADS), f32, math.log(1e-3), math.log(1e-1)))
    return {
        'x_prompt': nrm(ks[0], (BATCH, SEQ, D_MODEL)),
        'x_sample': nrm(ks[1], (DEC_BATCH, DEC_SEQ, D_MODEL)),
        'state_conv': nrm(ks[2], (DEPTH, DEC_BATCH, CONV_W - 1, CONV_DIM)),
        'state_ssm': nrm(ks[3], (DEPTH, DEC_BATCH, SSM_HEADS, SSM_HEAD_DIM, D_STATE), 0.3),
        'cache_k': nrm(ks[4], (DEPTH, DEC_BATCH, wb, N_KV, HEAD_DIM)),
        'cache_v': nrm(ks[5], (DEPTH, DEC_BATCH, wb, N_KV, HEAD_DIM)),
        'c_prompt': nrm(ks[6], (BATCH, D_MODEL)),
        'c_sample': nrm(ks[7], (DEC_BATCH, D_MODEL)),
        'ln_in_g': 1.0 + nrm(ks[8], (D_MODEL,), 0.02),
        'ln_in_b': nrm(ks[9], (D_MODEL,), 0.02),
        'w_ada': nrm(ks[10], (DEPTH, D_MODEL, 6 * D_MODEL), 0.5 * D_MODEL ** -0.5),
        'b_ada': nrm(ks[11], (DEPTH, 6 * D_MODEL), 0.02),
        'w_in': nrm(ks[12], (DEPTH, D_MODEL, D_IN_PROJ), D_MODEL ** -0.5),
        'conv_w': nrm(ks[13], (DEPTH, CONV_W, CONV_DIM), CONV_W ** -0.5),
        'conv_b': nrm(ks[14], (DEPTH, CONV_DIM), 0.02),
        'dt_bias': dt0 + jnp.log(-jnp.expm1(-dt0)),
        'A_log': jnp.log(jax.random.uniform(ks[15], (DEPTH, SSM_HEADS), f32, 1.0, 16.0)),
        'D_skip': 1.0 + nrm(ks[16], (DEPTH, SSM_HEADS), 0.02),
        'ssm_norm_w': 1.0 + nrm(ks[17], (DEPTH, SSM_WIDTH), 0.02),
        'sinks': nrm(ks[18], (DEPTH, N_Q)),
        'w_out': nrm(ks[19], (DEPTH, D_MIX, D_MODEL), BETA * D_MIX ** -0.5),
        'ln1_g': 1.0 + nrm(ks[21], (DEPTH, D_MODEL), 0.02),
        'ln1_b': nrm(ks[22], (DEPTH, D_MODEL), 0.02),
        'w_up': nrm(ks[23], (DEPTH, D_MODEL, D_FF), D_MODEL ** -0.5),
        'w_down': nrm(ks[24], (DEPTH, D_FF, D_MODEL), BETA * D_FF ** -0.5),
        'ln2_g': 1.0 + nrm(ks[25], (DEPTH, D_MODEL), 0.02),
        'ln2_b': nrm(ks[26], (DEPTH, D_MODEL), 0.02),
    }


def reference(x_prompt, x_sample, state_conv, state_ssm, cache_k, cache_v, c_prompt, c_sample,
              ln_in_g, ln_in_b, w_ada, b_ada, w_in, conv_w, conv_b, dt_bias, A_log, D_skip,
              ssm_norm_w, sinks, w_out, ln1_g, ln1_b, w_up, w_down, ln2_g, ln2_b):
    pos_p = jnp.arange(x_prompt.shape[1], dtype=jnp.int32)
    pos_s = PAST_LEN + jnp.arange(x_sample.shape[1], dtype=jnp.int32)
    xp = layer_norm(x_prompt, ln_in_g, ln_in_b)
    xs = layer_norm(x_sample, ln_in_g, ln_in_b)
    conv_p, ssm_p, k_p, v_p = [], [], [], []
    conv_s, ssm_s, k_s, v_s = [], [], [], []
    for l in range(DEPTH):
        lw = (w_ada[l], b_ada[l], w_in[l], conv_w[l], conv_b[l], dt_bias[l], A_log[l], D_skip[l],
              ssm_norm_w[l], sinks[l], w_out[l], ln1_g[l], ln1_b[l], w_up[l], w_down[l], ln2_g[l], ln2_b[l])
        xp, cp, hp, kp, vp = decoder_layer(xp, c_prompt, pos_p, None, None, None, None, *lw)
        xs, cs, hs, kq, vq = decoder_layer(xs, c_sample, pos_s, state_conv[l], state_ssm[l],
                                           cache_k[l], cache_v[l], *lw)
        conv_p.append(cp); ssm_p.append(hp); k_p.append(kp); v_p.append(vp)
        conv_s.append(cs); ssm_s.append(hs); k_s.append(kq); v_s.append(vq)
    return (xp, xs, jnp.stack(conv_p), jnp.stack(ssm_p), jnp.stack(k_p), jnp.stack(v_p),
            jnp.stack(conv_s), jnp.stack(ssm_s), jnp.stack(k_s), jnp.stack(v_s))
```

```python
import numpy as np
from contextlib import ExitStack
import concourse.bass as bass
import concourse.mybir as mybir
from concourse.bass_utils import run_bass_kernel_spmd

F32 = mybir.dt.float32
BF16 = mybir.dt.bfloat16
AF = mybir.ActivationFunctionType
ALU = mybir.AluOpType
AX = mybir.AxisListType

D = 1024
NQ = 8
NKV = 2
HD = 64
NH = 8
HP = 64
NST = 128
DFF = 4096
PAST = 16384
ALPHA = 2.0 ** 0.25
LN_EPS = 1e-5
RMS_EPS = 1e-5
NEG = -30000.0
C_Z, C_XBC, C_Q, C_K, C_V, C_DT, C_END = 0, 512, 1536, 2048, 2176, 2304, 2312


class Buf:
    __slots__ = ("name", "w", "r", "multi")

    def __init__(self, name, multi=False):
        self.name = name
        self.w = {}
        self.r = {}
        self.multi = multi


class K:
    def __init__(self, nc, es):
        self.nc = nc
        self.es = es
        self.eng = {"pe": nc.tensor, "dve": nc.vector, "act": nc.scalar, "pool": nc.gpsimd, "sp": nc.sync}
        self.sem = {}
        self.cnt = {}
        self.waited = {k: {} for k in self.eng}
        self.semobj = {}
        for k in ("pe", "dve", "act", "pool"):
            s = es.enter_context(nc.semaphore("sem_" + k))
            self.sem[k] = s
            self.semobj[id(s)] = s
            self.cnt[k] = 0
        self.dsems = []
        self._sites = {}
        self.ninstr = 0

    def new_dsem(self, name):
        s = self.es.enter_context(self.nc.semaphore("d_" + name))
        self.semobj[id(s)] = s
        h = [s, 0]
        self.dsems.append(h)
        return h

    def _need(self, ek, R, W):
        need = {}
        for b in R:
            for s, v in b.w.items():
                if need.get(s, 0) < v:
                    need[s] = v
        for b in W:
            if b.multi:
                continue
            for dd in (b.w, b.r):
                for s, v in dd.items():
                    if need.get(s, 0) < v:
                        need[s] = v
        return need

    def _waits(self, ek, need):
        e = self.eng[ek]
        own = id(self.sem[ek]) if ek in self.sem else None
        wd = self.waited[ek]
        for s, v in need.items():
            if s == own:
                if ek == "pe":
                    continue
                if v < self.cnt[ek] - 8:
                    continue
            if wd.get(s, 0) >= v:
                continue
            e.wait_ge(self.semobj[s], v)
            self.ninstr += 1
            wd[s] = v

    def _record(self, R, W, s, v):
        for b in W:
            if b.multi:
                if b.w.get(s, 0) < v:
                    b.w[s] = v
            else:
                b.w = {s: v}
                b.r = {}
        for b in R:
            if b in W:
                continue
            if b.r.get(s, 0) < v:
                b.r[s] = v

    def op(self, ek, fn, R=(), W=()):
        need = self._need(ek, R, W)
        self._waits(ek, need)
        ins = fn(self.eng[ek])
        self.cnt[ek] += 1
        ins.then_inc(self.sem[ek], 1)
        self.ninstr += 1
        self._record(R, W, id(self.sem[ek]), self.cnt[ek])

    def dma(self, qk, outs_ins, R, W, dsem):
        need = self._need(qk, R, W)
        self._waits(qk, need)
        for (o, i) in outs_ins:
            self.eng[qk].dma_start(out=o, in_=i, allow_slow_non_contiguous=True).then_inc(dsem[0], 16)
            dsem[1] += 16
            self.ninstr += 1
        self._record(R, W, id(dsem[0]), dsem[1])

    def site(self, name):
        if name not in self._sites:
            self._sites[name] = self.new_dsem(name)
        return self._sites[name]

    def finish(self, ek="sp"):
        for h in self.dsems:
            if h[1] > 0:
                self.eng[ek].wait_ge(h[0], h[1])
        for k2 in ("pe", "dve", "act", "pool"):
            if self.cnt[k2] > 0:
                self.eng[ek].wait_ge(self.sem[k2], self.cnt[k2])


def _consts(T, nseq):
    tl = T // nseq
    idx = np.arange(T)
    seq = idx // tl
    same = seq[:, None] == seq[None, :]
    U = (same & (idx[:, None] <= idx[None, :])).astype(np.float32)
    SL = (same & (idx[:, None] > idx[None, :])).astype(np.float32)
    BD = same.astype(np.float32)
    return U, SL, BD


def build(LH, NTS=64, NSEQ=16, with_sample=True, NSUB=2):
    import os
    with_sample = with_sample and os.environ.get('NO_SAMPLE') is None
    DBG = os.environ.get('DBG', '')
    nc = bass.Bass("TRN2", target_bir_lowering=False)
    NT = LH // 128
    assert NT % NSUB == 0

    def din(name, shape, dt=F32):
        return nc.dram_tensor(name, list(shape), dt, kind="ExternalInput").ap()

    def dout(name, shape, dt=F32):
        return nc.dram_tensor(name, list(shape), dt, kind="ExternalOutput").ap()

    def dscr(name, shape, dt):
        return nc.dram_tensor(name, list(shape), dt, kind="Internal").ap()
    assert NTS == 64 and NSEQ == 16

    x_prev = din("x_prev", [LH, D])
    x_main = din("x_main", [LH, D])
    c_p = din("c_p", [1, D])
    flag = din("flag", [128, 2])
    m0 = din("m0", [128, 256])
    m2 = din("m2", [128, 256])
    rope_main = din("rope_main", [LH, 16])
    rope_prev = din("rope_prev", [128, 16])
    cmask = din("cmask", [128, 5, 128])
    ln_in_g = din("ln_in_g", [1, D]); ln_in_b = din("ln_in_b", [1, D])
    w_ada = din("w_ada", [D, 6 * D]); b_ada = din("b_ada", [1, 6 * D])
    w_in = din("w_in", [D, C_END])
    conv_w = din("conv_w", [4, D]); conv_b = din("conv_b", [1, D])
    dt_bias = din("dt_bias", [1, NH]); A_log = din("A_log", [1, NH]); D_skip = din("D_skip", [1, NH])
    ssm_norm_w = din("ssm_norm_w", [1, 512])
    sinks_p = din("sinks_p", [1, NQ])
    w_out = din("w_out", [D, D])
    ln1_g = din("ln1_g", [1, D]); ln1_b = din("ln1_b", [1, D])
    w_up = din("w_up", [D, DFF]); w_down = din("w_down", [DFF, D])
    ln2_g = din("ln2_g", [1, D]); ln2_b = din("ln2_b", [1, D])

    x_s = din("x_s", [NTS, D]); c_s = din("c_s", [NTS, D]); rope_s = din("rope_s", [NTS, 16])
    sconv = din("sconv", [NSEQ * 3, D]); sssm = din("sssm", [NSEQ, 512, 128])
    ck = din("ck", [NSEQ, 128, 128]); cv = din("cv", [NSEQ, 128, 128])
    cmask_s = din("cmask_s", [128, 3, 128]); smk_d = din("smk", [128, NSEQ]); sink16_d = din("sink16", [16, 2])
    dmask = din("dmask", [16, 128 + NSEQ * 64])
    y_s = dout("y_s", [NTS, D]); conv_s = dout("conv_s", [NSEQ, 3, D]); ssm_s = dout("ssm_s", [NSEQ, 512, 128])
    k_s = dout("k_s", [NSEQ, 128, 128]); v_s = dout("v_s", [NSEQ, 128, 128])
    osc = dscr("osc", [16, NSEQ * 128], BF16)
    y_main = dout("y_main", [LH, D])
    conv_p = dout("conv_p", [3, D])
    ssm_p = dout("ssm_p", [512, 128])
    k_p = dout("k_p", [128, 128])
    v_p = dout("v_p", [128, 128])

    wup_s = dscr("wup_s", [8, 128, 4096], BF16)
    wdn_s = dscr("wdn_s", [2, 4, 128, 4096], BF16)
    wout_s = dscr("wout_s", [4, 128, 2048], BF16)

    es = ExitStack()
    with es:
        k = K(nc, es)

        def sb(name, shape, dt=F32):
            return es.enter_context(nc.sbuf_tensor(name, list(shape), dt))

        w_in_b = sb("w_in_b", [128, 8, C_END], BF16); B_win = Buf("w_in_b", multi=True)
        B_wout_s = Buf("wout_s", multi=True)
        NSLOT = 6
        wslot = [sb(f"wslot{i}", [128, 1024], BF16) for i in range(NSLOT)]
        B_wslot = [Buf(f"wslot{i}") for i in range(NSLOT)]
        D_wslot = [k.new_dsem(f"wslot{i}") for i in range(NSLOT)]
        NOS = 4
        oslot = [sb(f"oslot{i}", [128, 512], BF16) for i in range(NOS)]
        B_oslot = [Buf(f"oslot{i}") for i in range(NOS)]
        D_oslot = [k.new_dsem(f"oslot{i}") for i in range(NOS)]
        oq = {"n": 0}

        def oload(c_):
            kk, n = c_ // 2, c_ % 2
            i = oq["n"] % NOS
            oq["n"] += 1
            k.dma("sp", [(oslot[i][:, :], wout_s[kk // 2, :, (kk % 2) * 1024 + n * 512:(kk % 2) * 1024 + (n + 1) * 512])], R=[B_wout_s], W=[B_oslot[i]], dsem=D_oslot[i])
            return i
        D_stgf = [k.new_dsem(f"stgf{i}") for i in range(2)]
        B_stgb = [Buf(f"stgb{i}") for i in range(2)]
        D_stgb = [k.new_dsem(f"stgb{i}") for i in range(2)]
        B_wup_s = Buf("wup_s", multi=True); B_wdn_s = Buf("wdn_s", multi=True)

        ada = sb("ada", [128, 6, D], F32); B_ada = Buf("ada")
        gb = sb("gb", [128, 6, D], F32); B_gb = Buf("gb")
        normw = sb("normw", [128, 512], F32)
        smallc = sb("smallc", [128, 5, 8], F32)
        B_small = Buf("smallc")
        cw = sb("cw", [128, 8, 5], F32); B_cw = Buf("cw")
        cm = sb("cm", [128, 5, 128], F32); B_cm = Buf("cm")
        identb = sb("identb", [128, 128], BF16)
        am = sb("am", [128, 2, 256], F32); B_am = Buf("am")
        flg = sb("flg", [128, 2], F32)
        D_c = k.new_dsem("consts")

        x_in = sb("x_in", [128, D], F32); B_xin = Buf("x_in"); D_xin = k.new_dsem("x_in")
        rope = sb("rope", [128, 16], F32); B_rope = Buf("rope"); D_rope = k.new_dsem("rope")
        stats0 = sb("stats", [128, 2, 6], F32); B_stats0 = Buf("stats")
        mv0 = sb("mv", [128, 8], F32); B_mv0 = Buf("mv")
        xpb = sb("xpb", [128, 2, D], F32); B_xpb = [Buf("xp0"), Buf("xp1")]
        xp = xpb[:, 0, :]; B_xp = B_xpb[0]
        hbP = sb("hbP", [128, D], BF16); B_hbP = Buf("hbP")
        hTP = sb("hTP", [128, 8, 128], BF16); B_hTP = Buf("hTP")
        hT = sb("hT", [128, 8, 128], BF16); B_hT = Buf("hT")
        cT = hT; B_cT = B_hT
        xbcT = sb("xbcT", [128, 8, 131], F32); B_xbcT = Buf("xbcT")
        xacc = sb("xacc", [128, 8, 128], F32); B_xacc = Buf("xacc")
        xcT = xacc[:, 0:4, :]; B_xcT = B_xacc
        B_xc = [Buf(f"xc{i}") for i in range(9)]
        bcT = sb("bcT", [128, 4, 128], BF16); B_bcT = Buf("bcT")
        xtok = sb("xtok", [128, 512], F32); B_xtok = Buf("xtok")
        btok = sb("btok", [128, 256], BF16); B_btok = Buf("btok")
        sm8 = sb("sm8", [128, 8, 8], F32); B_sm8 = Buf("sm8")
        xdt = sb("xdt", [128, 512], BF16); B_xdt = Buf("xdt")
        xdec = sb("xdec", [128, 512], BF16); B_xdec = Buf("xdec")
        Hst = sb("Hst", [128, 512], F32); B_H = Buf("H")
        Hbf = sb("Hbf", [128, 512], BF16); B_Hb = Buf("Hb")
        big1 = sb("big1", [128, 8, 128], F32); B_big1 = Buf("big1")
        big2 = big1; B_big2 = B_big1
        Gb = sb("Gb", [128, 8, 128], BF16); B_G = Buf("G")
        cbm = sb("cbm", [128, 2, 128], F32); B_cbm = Buf("cbm")
        ysb = sb("ysb", [128, 512], F32); B_y = Buf("y")
        qs = sb("qs", [128, 512], F32); B_qs = Buf("qs")
        kvs = sb("kvs", [128, 256], F32); B_kvs = Buf("kvs")
        rt = sb("rt", [128, 4, 64], F32); B_rt = Buf("rt")
        qb = sb("qb", [128, 640], BF16); B_qb = Buf("qb")
        qT = sb("qT", [128, 4, 128], BF16); B_qT = Buf("qT")
        kTb = [sb(f"kTb{i}", [128, 128], BF16) for i in range(2)]; B_kT = [Buf(f"kT{i}") for i in range(2)]
        vb = [sb(f"vb{i}", [128, 128], BF16) for i in range(2)]; B_vb = [Buf(f"vb{i}") for i in range(2)]
        PT = sb("PT", [128, 8, 128], BF16); B_PT = Buf("PT")
        att8 = sb("att8", [128, 8, 8], F32); B_att8 = Buf("att8")
        rs8 = sb("rs8", [128, 8], F32); B_rs8 = Buf("rs8")
        ymix = sb("ymix", [128, D], BF16); B_ymix = Buf("ymix")
        hb = ymix; B_hb = B_ymix
        ymixT = hT; B_ymixT = B_hT
        tmpf = sb("tmpf", [128, D], F32); B_tmpf = Buf("tmpf")
        zs = sb("zs", [128, 512], F32); B_zs = Buf("zs")
        TT = NSUB * 128
        NXB = 1
        X1 = sb("X1", [128, NXB, NSUB, D], F32); B_X1 = [[Buf(f"X1_{b}_{i}") for i in range(NSUB)] for b in range(NXB)]
        h2T = sb("h2T", [128, NXB, 8, TT], BF16); B_h2T = [[Buf(f"h2T_{b}_{i}") for i in range(NSUB)] for b in range(NXB)]
        smvb = sb("smvb", [128, 4, 256], F32); B_smvb = Buf("smvb")
        Pb = sb("Pb", [128, 4, 256], BF16); B_Pb = Buf("Pb")
        uT = sb("uT", [128, 32, TT], BF16); B_uT = [Buf(f"uT_{i}") for i in range(32)]
        urelu = [sb(f"urelu{i}", [128, TT], BF16) for i in range(2)]; B_urelu = [Buf(f"urelu{i}") for i in range(2)]
        D_yo = k.new_dsem("yo")
        stg_f = [tmpf, X1[:, NXB - 1, NSUB - 1, :]]
        B_stgf = [B_tmpf, B_X1[NXB - 1][NSUB - 1]]
        stg_b = [uT[:, 0:16, :].rearrange("p a b -> p (a b)"), uT[:, 16:32, :].rearrange("p a b -> p (a b)")]
        assert TT == 256

        def W_stgb(j):
            return [B_stgb[j]] + B_uT[16 * j:16 * j + 16]
        uflat = uT[:, :, :].rearrange("p a b -> p (a b)")

        def scr(lo, hi, dt=None):
            v = uflat[:, lo:hi]
            return v.bitcast(dt) if dt is not None else v
        hn = [scr(0, 1024, F32), scr(1024, 2048, F32)]; B_hn = [Buf("hn0"), Buf("hn1")]; D_hn = [k.new_dsem("hn0"), k.new_dsem("hn1")]
        hnb = scr(2048, 2560); B_hnb = Buf("hnb")
        h0T = scr(2560, 3072); B_h0T = Buf("h0T")
        xdm = scr(3072, 3584); B_xdm = Buf("xdm")
        ckf = scr(3584, 3840, F32); cvf = scr(3840, 4096, F32); B_ckf = Buf("ckf"); B_cvf = Buf("cvf"); D_ckv = k.new_dsem("ckv")
        ckb = scr(4096, 4224); cvb = scr(4224, 4352); kcT = scr(4352, 4480); B_ckb = Buf("ckb"); B_cvb = Buf("cvb"); B_kcT = Buf("kcT")
        dmk = scr(4480, 6784, F32); B_dmk = Buf("dmk")
        cms = scr(6784, 7552, F32).rearrange("p (a b) -> p a b", a=3); B_cms = Buf("cms")
        yoT = scr(7552, 8064, F32); B_yoT = Buf("yoT")
        dcol = scr(8064, 8192, F32).rearrange("p (a b) -> p a b", a=4); B_dcol = Buf("dcol")
        smk = sb("smk_sb", [128, NSEQ], F32); sink16 = sb("sink16_sb", [16, 2], F32); B_smk = Buf("smk")
        fdummy = sb("fdummy", [128, 2], F32)
        qTs = sb("qTs", [64, 2, NSEQ, 16], BF16); B_qTs = Buf("qTs")
        knT1 = sb("knT1", [64, 64], BF16)
        kcT2 = sb("kcT2", [64, 2, 128], BF16)
        SCR_BUFS = B_hn + [B_hnb, B_h0T, B_xdm, B_ckf, B_cvf, B_ckb, B_cvb, B_kcT, B_dmk, B_cms, B_yoT, B_dcol]
        osb = X1[:, NXB - 1, 0, :].bitcast(BF16)

        def fence():
            allb = SCR_BUFS + B_stgb + B_uT
            k.op("dve", lambda e: e.memset(fdummy[:, 0:1], 0.0), R=[], W=allb)
        D_misc = k.new_dsem("misc")
        B_dummy = Buf("dram_out", multi=True)

        ps = [es.enter_context(nc.psum_tensor(f"ps{i}", [128, 512], F32)) for i in range(8)]
        B_ps = [Buf(f"ps{i}") for i in range(8)]
        B_ps7h = [Buf("ps7a"), Buf("ps7b")]

        def psb(i, dt=BF16):
            return ps[i][:, :].bitcast(dt)

        def bc_row(ap_row, n):
            return ap_row.broadcast_to([128, n])

        k.dma("sp", [(gb[:, 0, :], bc_row(ln_in_g, D)), (gb[:, 1, :], bc_row(ln_in_b, D)),
                     (gb[:, 2, :], bc_row(ln1_g, D)), (gb[:, 3, :], bc_row(ln1_b, D)),
                     (gb[:, 4, :], bc_row(ln2_g, D)), (gb[:, 5, :], bc_row(ln2_b, D)),
                     (normw[:, :], bc_row(ssm_norm_w, 512)),
                     (smallc[:, 0, :], bc_row(dt_bias, NH)), (smallc[:, 1, :], bc_row(A_log, NH)),
                     (smallc[:, 2, :], bc_row(D_skip, NH)), (smallc[:, 3, :], bc_row(sinks_p, NQ)),
                     (cm[:, :, :], cmask), (am[:, 0, :], m0), (am[:, 1, :], m2), (flg[:, :], flag)],
              R=[], W=[B_gb, B_small, B_cm, B_am], dsem=D_c)
        k.dma("sp", [(cw[:, :, j], conv_w[j, :].rearrange("(c p) -> p c", p=128)) for j in range(4)]
              + [(cw[:, :, 4], conv_b[0, :].rearrange("(c p) -> p c", p=128))],
              R=[], W=[B_cw], dsem=k.site("cw"))
        k.op("act", lambda e: e.activation(out=smallc[:, 1, :], in_=smallc[:, 1, :], func=AF.Exp), R=[B_small], W=[B_small])
        k.op("dve", lambda e: e.tensor_scalar(out=smallc[:, 1, :], in0=smallc[:, 1, :], scalar1=-1.0, scalar2=None, op0=ALU.mult), R=[B_small], W=[B_small])
        k.op("dve", lambda e: e.tensor_copy(out=identb[:, :], in_=cm[:, 3, :]), R=[B_cm], W=[B_cm])
        identf = cm[:, 3, :]

        cast_engs = ["dve", "pool", "act"]
        state = {"stg": 0, "ce": 0}

        def cast(out_ap, in_ap, R, W):
            ek = cast_engs[state["ce"] % 3]
            state["ce"] += 1
            if ek == "act":
                k.op("act", lambda e: e.copy(out=out_ap, in_=in_ap), R=R, W=W)
            else:
                k.op(ek, lambda e: e.tensor_copy(out=out_ap, in_=in_ap), R=R, W=W)

        def load_stage(dram_ap_3d, shape3):
            i = state["stg"] % 2
            state["stg"] += 1
            view = stg_f[i][:, 0:shape3[0] * shape3[1]].rearrange("p (a b) -> p a b", a=shape3[0])
            k.dma("sp", [(view, dram_ap_3d)], R=[], W=[B_stgf[i]], dsem=D_stgf[i])
            return i, view

        D_wprep = k.new_dsem("wprep")
        k.dma("pool", [(w_in_b[:, kk, :], w_in[kk * 128:(kk + 1) * 128, :]) for kk in range(8)], R=[], W=[B_win], dsem=k.new_dsem("wp_in"))
        def late_weight_prep():
            k.dma("pool", [(wout_s[q_, :, :].rearrange("p (a n) -> p a n", a=2), w_out[q_ * 256:(q_ + 1) * 256, :].rearrange("(a p) n -> p a n", p=128)) for q_ in range(4)],
                  R=[], W=[B_wout_s], dsem=k.new_dsem("wp_out"))
            k.dma("pool", [(wup_s[g, :, fc * 1024:(fc + 1) * 1024].rearrange("p (kk f) -> p kk f", kk=8), w_up[:, (g * 4 + fc) * 128:(g * 4 + fc + 1) * 128].rearrange("(kk p) f -> p kk f", p=128))
                           for g in range(8) for fc in range(4)], R=[], W=[B_wup_s], dsem=k.new_dsem("wp_up"))
            k.dma("pool", [(wdn_s[n, g, :, :].rearrange("p (fk d) -> p fk d", fk=8), w_down[g * 1024:(g + 1) * 1024, n * 512:(n + 1) * 512].rearrange("(fk p) d -> p fk d", p=128))
                           for n in range(2) for g in range(4)], R=[], W=[B_wdn_s], dsem=k.new_dsem("wp_dn"))

        sbi = 0

        def compute_ada(c_rows_ap, T):
            k.dma("sp", [(xp[:T, :], c_rows_ap)], R=[], W=[B_xp], dsem=k.site("ada_c"))
            k.op("act", lambda e: e.activation(out=hb[:T, :], in_=xp[:T, :], func=AF.Silu), R=[B_xp], W=[B_hb])
            for kk in range(8):
                k.op("pe", lambda e, kk=kk: e.transpose(out=psb(0)[:, kk * 128:kk * 128 + T], in_=hb[:T, kk * 128:(kk + 1) * 128], identity=identb[:T, :T]),
                     R=[B_hb, B_cm], W=[B_ps[0]])
            k.op("act", lambda e: e.copy(out=cT[:, :, :T], in_=psb(0)[:, 0:1024].rearrange("p (a b) -> p a b", a=8)[:, :, :T]), R=[B_ps[0]], W=[B_cT])
            k.dma("sp", [(ada[:T, :, :].rearrange("p a b -> p (a b)"), b_ada.broadcast_to([T, 6 * D]))], R=[], W=[B_ada], dsem=k.site("ada_b"))
            for nn in range(12):
                pb = 1 + (nn % 2)
                j = sbi_box[0] % 2
                sbi_box[0] += 1
                wv = stg_b[j].rearrange("p (a n) -> p a n", a=8)
                k.dma("pool", [(wv, w_ada[:, nn * 512:(nn + 1) * 512].rearrange("(a p) n -> p a n", p=128))], R=[], W=W_stgb(j), dsem=D_stgb[j])
                for kk in range(8):
                    k.op("pe", lambda e, kk=kk, wv=wv, pb=pb: e.matmul(out=ps[pb][:T, :], lhsT=cT[:, kk, :T], rhs=wv[:, kk, :], start=(kk == 0), stop=(kk == 7)),
                         R=[B_cT, B_stgb[j]], W=[B_ps[pb]])
                av = ada[:T, :, :].rearrange("p a b -> p (a b)")[:, nn * 512:(nn + 1) * 512]
                k.op("dve", lambda e, av=av, pb=pb: e.tensor_tensor(out=av, in0=ps[pb][:T, :], in1=av, op=ALU.add), R=[B_ps[pb], B_ada], W=[B_ada])
            for s_, gi_ in ((1, 0), (4, 2)):
                k.op("pool", lambda e, s_=s_: e.tensor_scalar(out=ada[:T, s_, :], in0=ada[:T, s_, :], scalar1=1.0, scalar2=None, op0=ALU.add), R=[B_ada], W=[B_ada])
                k.op("dve", lambda e, s_=s_, gi_=gi_: e.tensor_tensor(out=tmpf[:T, :], in0=ada[:T, s_, :], in1=gb[:T, gi_ + 1, :], op=ALU.mult), R=[B_ada, B_gb], W=[B_tmpf])
                k.op("dve", lambda e, s_=s_: e.tensor_tensor(out=ada[:T, s_ - 1, :], in0=ada[:T, s_ - 1, :], in1=tmpf[:T, :], op=ALU.add), R=[B_ada, B_tmpf], W=[B_ada])
                k.op("pool", lambda e, s_=s_, gi_=gi_: e.tensor_tensor(out=ada[:T, s_, :], in0=ada[:T, s_, :], in1=gb[:T, gi_, :], op=ALU.mult), R=[B_ada, B_gb], W=[B_ada])

        sbi_box = [sbi]

        statsP = sb("statsP", [128, 2, 6], F32); B_statsP = Buf("statsP")
        mvP = sb("mvP", [128, 8], F32); B_mvP = Buf("mvP")

        def layer_norm_rows(src, B_src, T, gi, dst, B_dst, scr=None):
            if scr is None:
                stats, B_stats, mv, B_mv = stats0, B_stats0, mv0, B_mv0
            else:
                stats, B_stats, mv, B_mv = scr
            for c in range(2):
                k.op("dve", lambda e, c=c: e.bn_stats(out=stats[:T, c, :], in_=src[:T, c * 512:(c + 1) * 512]), R=[B_src], W=[B_stats])
            k.op("dve", lambda e: e.bn_aggr(out=mv[:T, 0:2], in_=stats[:T, :, :].rearrange('p a b -> p (a b)')), R=[B_stats], W=[B_mv])
            k.op("dve", lambda e: e.tensor_scalar(out=mv[:T, 2:3], in0=mv[:T, 1:2], scalar1=LN_EPS, scalar2=None, op0=ALU.add), R=[B_mv], W=[B_mv])
            k.op("act", lambda e: e.activation(out=mv[:T, 2:3], in_=mv[:T, 2:3], func=AF.Ln), R=[B_mv], W=[B_mv])
            k.op("act", lambda e: e.activation(out=mv[:T, 2:3], in_=mv[:T, 2:3], func=AF.Exp, scale=-0.5), R=[B_mv], W=[B_mv])
            k.op("dve", lambda e: e.tensor_scalar(out=mv[:T, 3:4], in0=mv[:T, 0:1], scalar1=mv[:T, 2:3], scalar2=-1.0, op0=ALU.mult, op1=ALU.mult), R=[B_mv], W=[B_mv])
            k.op("act", lambda e: e.activation(out=dst[:T, :], in_=src[:T, :], func=AF.Identity, bias=mv[:T, 3:4], scale=mv[:T, 2:3]), R=[B_src, B_mv], W=[B_dst])
            if gi is not None:
                affine(dst, B_dst, T, gi, dst, B_dst, ("dve", "pool"))

        def affine(src, B_src, T, gi, dst, B_dst, engs):
            k.op(engs[0], lambda e: e.tensor_tensor(out=dst[:T, :], in0=src[:T, :], in1=gb[:T, gi, :], op=ALU.mult), R=[B_src, B_gb], W=[B_dst])
            k.op(engs[1], lambda e: e.tensor_tensor(out=dst[:T, :], in0=dst[:T, :], in1=gb[:T, gi + 1, :], op=ALU.add), R=[B_dst, B_gb], W=[B_dst])

        def modulate_T(src, B_src, T, si, dstT, B_dstT, col0, hb=None, B_hb=None, tmp=None, B_tmp=None, pbk=0):
            if hb is None:
                hb, B_hb = ymix, B_ymix
            if tmp is None:
                tmp, B_tmp = tmpf, B_tmpf
            k.op("dve", lambda e: e.tensor_tensor(out=tmp[:T, :], in0=src[:T, :], in1=ada[:T, si, :], op=ALU.mult), R=[B_src, B_ada], W=[B_tmp])
            k.op("dve", lambda e: e.tensor_tensor(out=hb[:T, :], in0=tmp[:T, :], in1=ada[:T, si - 1, :], op=ALU.add), R=[B_tmp, B_ada], W=[B_hb])
            for kk in range(8):
                k.op("pe", lambda e, kk=kk: e.transpose(out=psb(pbk)[:, kk * 128:kk * 128 + T], in_=hb[:T, kk * 128:(kk + 1) * 128], identity=identb[:T, :T]),
                     R=[B_hb, B_cm], W=[B_ps[pbk]])
            k.op("act", lambda e: e.copy(out=dstT[:, :, col0:col0 + T], in_=psb(pbk)[:, 0:1024].rearrange("p (a b) -> p a b", a=8)[:, :, :T]), R=[B_ps[pbk]], W=[B_dstT])

        def bc8(ap_t8, T, n):
            return ap_t8.unsqueeze(2).broadcast_to([T, 8, n])

        def mixer_tile(x_rows, rope_rows, T, par, mode, mask_ap, sub, last_conv_out=False, smp=False, xb=0):
            full = mode == "full"
            xb = xb % NXB
            X1v = X1[:, xb]; B_X1v = B_X1[xb]; h2Tv = h2T[:, xb]; B_h2Tv = B_h2T[xb]
            if smp:
                U_ap = cms[:, 0, :]; SL_ap = cms[:, 1, :]; BD_ap = cms[:, 2, :]; CAU_ap = cms[:, 0, :]; B_cmx = B_cms
            else:
                U_ap = cm[:, 0, :]; SL_ap = cm[:, 1, :]; BD_ap = cm[:, 2, :]; CAU_ap = cm[:, 4, :]; B_cmx = B_cm
            xbcS = xbcT[:, :, :].rearrange("p a b -> p (a b)")[:, 0:896].rearrange("p (c s u) -> p c s u", c=8, s=16)
            kv = mode in ("full", "statekv")
            k.dma("act", [(x_in[:T, :], x_rows)], R=[], W=[B_xin], dsem=D_xin)
            if kv:
                k.dma("act", [(rope[:T, :], rope_rows)], R=[], W=[B_rope], dsem=D_rope)
            xp = xpb[:, par % 2, :]; B_xp = B_xpb[par % 2]
            hT = hTP; B_hT = B_hTP
            layer_norm_rows(x_in, B_xin, T, None, x_in, B_xin, scr=(statsP, B_statsP, mvP, B_mvP))
            modulate_T(x_in, B_xin, T, 1, hT, B_hT, 0, hbP, B_hbP, xp, B_xp, 5)
            if full:
                affine(x_in, B_xin, T, 0, xp, B_xp, ("pool", "pool"))
            yield
            nch = 6 if mode == 'state' else 8
            for c in range(nch):
                pb = 6 + c // 4
                for kk in range(8):
                    k.op("pe", lambda e, c=c, kk=kk, pb=pb: e.matmul(out=ps[pb][:, (c % 4) * 128:(c % 4) * 128 + T], lhsT=w_in_b[:, kk, C_XBC + c * 128:C_XBC + (c + 1) * 128],
                                                                   rhs=hT[:, kk, :T], start=(kk == 0), stop=(kk == 7)), R=[B_win, B_hT], W=[B_ps[pb]])
            n2 = nch - 4
            if not smp:
                k.op("act", lambda e: e.copy(out=xbcT[:, 0:4, 3:3 + T], in_=ps[6][:, :].rearrange("p (a b) -> p a b", a=4)[:, :, :T]), R=[B_ps[6]], W=[B_xbcT])
                k.op("dve", lambda e: e.tensor_copy(out=xbcT[:, 4:4 + n2, 3:3 + T], in_=ps[7][:, :].rearrange("p (a b) -> p a b", a=4)[:, 0:n2, :T]), R=[B_ps[7]], W=[B_xbcT])
            else:
                for hh in range(2):
                    k.op("act" if hh == 0 else "dve", lambda e, hh=hh: (e.copy if hh == 0 else e.tensor_copy)(out=xbcS[:, hh * 4:(hh + 1) * 4, :, 3:7],
                         in_=ps[6 + hh][:, :].rearrange("p (a b) -> p a b", a=4)[:, :, :64].rearrange("p a (s t) -> p a s t", t=4)), R=[B_ps[6 + hh]], W=[B_xbcT])
                k.dma("sp", [(x_in[:48, :], sconv)], R=[], W=[B_xin], dsem=D_xin)
                for c in range(8):
                    k.op("pe", lambda e, c=c: e.transpose(out=ps[4][:, c * 48:(c + 1) * 48], in_=x_in[:48, c * 128:(c + 1) * 128], identity=identf[:48, :48]), R=[B_xin, B_cm], W=[B_ps[4]])
                k.op("act", lambda e: e.copy(out=xbcS[:, :, :, 0:3], in_=ps[4][:, 0:384].rearrange("p (c s u) -> p c s u", c=8, s=16)), R=[B_ps[4]], W=[B_xbcT])
                for n in range(2):
                    for kk in range(8):
                        k.op("pe", lambda e, n=n, kk=kk: e.matmul(out=ps[1 + n][:T, :], lhsT=hT[:, kk, :T], rhs=w_in_b[:, kk, C_XBC + n * 512:C_XBC + (n + 1) * 512], start=(kk == 0), stop=(kk == 7)),
                             R=[B_win, B_hT], W=[B_ps[1 + n]])
                k.op("act", lambda e: e.copy(out=tmpf[:T, 0:512], in_=ps[1][:T, :]), R=[B_ps[1]], W=[B_tmpf])
                k.op("dve", lambda e: e.tensor_copy(out=tmpf[:T, 512:1024], in_=ps[2][:T, :]), R=[B_ps[2]], W=[B_tmpf])
                k.dma("sp", [(conv_s[:, t_ - 1, :], tmpf[t_:64:4, :]) for t_ in range(1, 4)], R=[B_tmpf], W=[B_dummy], dsem=k.site("conv_s"))
            c0 = C_K if kv else C_DT
            ncol = C_END - c0
            for kk in range(8):
                k.op("pe", lambda e, kk=kk: e.matmul(out=ps[3][:T, 0:ncol], lhsT=hT[:, kk, :T], rhs=w_in_b[:, kk, c0:C_END], start=(kk == 0), stop=(kk == 7)),
                     R=[B_win, B_hT], W=[B_ps[3]])
            dtcol = C_DT - c0
            if full:
                for kk in range(8):
                    k.op("pe", lambda e, kk=kk: e.matmul(out=ps[4][:T, :], lhsT=hT[:, kk, :T], rhs=w_in_b[:, kk, C_Z:C_Z + 512], start=(kk == 0), stop=(kk == 7)),
                         R=[B_win, B_hT], W=[B_ps[4]])
                for kk in range(8):
                    k.op("pe", lambda e, kk=kk: e.matmul(out=ps[7][:T, :], lhsT=hT[:, kk, :T], rhs=w_in_b[:, kk, C_Q:C_Q + 512], start=(kk == 0), stop=(kk == 7)),
                         R=[B_win, B_hT], W=[B_ps[7]])
                k.op("act", lambda e: e.activation(out=zs[:T, :], in_=ps[4][:T, :], func=AF.Silu), R=[B_ps[4]], W=[B_zs])
                k.op("act", lambda e: e.activation(out=qs[:T, :], in_=ps[7][:T, :], func=AF.Copy, scale=0.125), R=[B_ps[7]], W=[B_qs])
            yield "P2"
            DT, A_, ACS, EACS, DEC, EATOT, DTDEC, TMP = [sm8[:, i, :] for i in range(8)]
            k.op("dve", lambda e: e.tensor_tensor(out=DT[:T], in0=ps[3][:T, dtcol:dtcol + 8], in1=smallc[:T, 0, :], op=ALU.add), R=[B_ps[3], B_small], W=[B_sm8])
            k.op("act", lambda e: e.activation(out=DT[:T], in_=DT[:T], func=AF.Exp), R=[B_sm8], W=[B_sm8])
            k.op("act", lambda e: e.activation(out=DT[:T], in_=DT[:T], func=AF.Ln, bias=1.0), R=[B_sm8], W=[B_sm8])
            k.op("dve", lambda e: e.tensor_tensor(out=A_[:T], in0=DT[:T], in1=smallc[:T, 1, :], op=ALU.mult), R=[B_sm8, B_small], W=[B_sm8])
            if kv:
                k.op("act", lambda e: e.copy(out=kvs[:T, :], in_=ps[3][:T, 0:256]), R=[B_ps[3]], W=[B_kvs])
            k.op("pe", lambda e: e.matmul(out=ps[4][:T, 0:8], lhsT=U_ap[:T, :T], rhs=A_[:T], start=True, stop=True), R=[B_cmx, B_sm8], W=[B_ps[4]])
            NB = T if smp else 128
            k.op("pe", lambda e: e.matmul(out=ps[4][:NB, 8:16], lhsT=BD_ap[:T, :NB], rhs=A_[:T], start=True, stop=True), R=[B_cmx, B_sm8], W=[B_ps[4]])
            k.op("dve", lambda e: e.tensor_copy(out=ACS[:T], in_=ps[4][:T, 0:8]), R=[B_ps[4]], W=[B_sm8])
            k.op("act", lambda e: e.activation(out=EACS[:T], in_=ps[4][:T, 0:8], func=AF.Exp), R=[B_ps[4]], W=[B_sm8])
            k.op("dve", lambda e: e.tensor_tensor(out=TMP[:T], in0=ps[4][:T, 8:16], in1=ACS[:T], op=ALU.subtract), R=[B_ps[4], B_sm8], W=[B_sm8])
            k.op("act", lambda e: e.activation(out=DEC[:T], in_=TMP[:T], func=AF.Exp), R=[B_sm8], W=[B_sm8])
            if not smp:
                k.op("act", lambda e: e.activation(out=EATOT, in_=ps[4][:, 8:16], func=AF.Exp), R=[B_ps[4]], W=[B_sm8])
            k.op("dve", lambda e: e.tensor_tensor(out=DTDEC[:T], in0=DT[:T], in1=DEC[:T], op=ALU.mult), R=[B_sm8], W=[B_sm8])
            yield
            if smp:
                def cwb(j):
                    return cw[:, 0:8, j:j + 1].unsqueeze(3).broadcast_to([128, 8, 16, 4])

                def v4(ap3):
                    return ap3.rearrange("p c (s t) -> p c s t", t=4)
                k.op("pool", lambda e: e.tensor_tensor(out=v4(xacc[:, 0:8, :64]), in0=xbcS[:, :, :, 0:4], in1=cwb(0), op=ALU.mult), R=[B_xbcT, B_cw], W=[B_xacc])
                for j in range(1, 4):
                    k.op("pool", lambda e, j=j: e.tensor_tensor(out=v4(big1[:, 0:8, :64]), in0=xbcS[:, :, :, j:j + 4], in1=cwb(j), op=ALU.mult), R=[B_xbcT, B_cw], W=[B_big1])
                    k.op("pool", lambda e: e.tensor_tensor(out=xacc[:, 0:8, :64], in0=xacc[:, 0:8, :64], in1=big1[:, 0:8, :64], op=ALU.add), R=[B_xacc, B_big1], W=[B_xacc])
                k.op("pool", lambda e: e.tensor_tensor(out=v4(xacc[:, 0:8, :64]), in0=v4(xacc[:, 0:8, :64]), in1=cwb(4), op=ALU.add), R=[B_xacc, B_cw], W=[B_xacc])

            if not smp:
                nD = nch - 2
                for b_ in B_xc:
                    b_.w = {}
                    b_.r = {}
                    for dd in (B_xacc.w, B_xacc.r):
                        for s__, v__ in dd.items():
                            if b_.w.get(s__, 0) < v__:
                                b_.w[s__] = v__
                for j in range(4):
                    for c in range(nD):
                        if j == 0:
                            k.op("dve", lambda e, c=c: e.tensor_scalar(out=xacc[:, c, :T], in0=xbcT[:, c, 0:T], scalar1=cw[:, c, 0:1], scalar2=cw[:, c, 4:5], op0=ALU.mult, op1=ALU.add),
                                 R=[B_xbcT, B_cw], W=[B_xc[c]])
                        else:
                            k.op("dve", lambda e, c=c, j=j: e.scalar_tensor_tensor(out=xacc[:, c, :T], in0=xbcT[:, c, j:j + T], scalar=cw[:, c, j:j + 1], in1=xacc[:, c, :T], op0=ALU.mult, op1=ALU.add),
                                 R=[B_xbcT, B_cw], W=[B_xc[c]])

                def cwb(j):
                    return cw[:, nD:nch, j:j + 1].broadcast_to([128, nch - nD, T])
                k.op("pool", lambda e: e.tensor_tensor(out=xacc[:, nD:nch, :T], in0=xbcT[:, nD:nch, 0:T], in1=cwb(0), op=ALU.mult), R=[B_xbcT, B_cw], W=[B_xc[8]])
                for j in range(1, 4):
                    k.op("pool", lambda e, j=j: e.tensor_tensor(out=big1[:, nD:nch, :T], in0=xbcT[:, nD:nch, j:j + T], in1=cwb(j), op=ALU.mult), R=[B_xbcT, B_cw], W=[B_big1])
                    k.op("pool", lambda e: e.tensor_tensor(out=xacc[:, nD:nch, :T], in0=xacc[:, nD:nch, :T], in1=big1[:, nD:nch, :T], op=ALU.add), R=[B_big1], W=[B_xc[8]])
                k.op("pool", lambda e: e.tensor_tensor(out=xacc[:, nD:nch, :T], in0=xacc[:, nD:nch, :T], in1=cwb(4), op=ALU.add), R=[B_cw], W=[B_xc[8]])
            if last_conv_out:
                pass
            if not smp:
                k.op("pool", lambda e: e.tensor_copy(out=xbcT[:, :, 0:3], in_=xbcT[:, :, T:T + 3]), R=[B_xbcT], W=[B_xbcT])
            k.op("act", lambda e: e.activation(out=xcT[:, :, :T], in_=xacc[:, 0:4, :T], func=AF.Silu), R=[B_xacc] + B_xc, W=[B_xcT])
            k.op("act", lambda e: e.activation(out=bcT[:, 0:n2, :T], in_=xacc[:, 4:4 + n2, :T], func=AF.Silu), R=[B_xacc] + B_xc, W=[B_bcT])
            yield
            for c in range(4):
                k.op("pe", lambda e, c=c: e.transpose(out=ps[4][:T, c * 128:(c + 1) * 128], in_=xcT[:, c, :T], identity=identf), R=[B_xcT, B_cm], W=[B_ps[4]])
            k.op("act", lambda e: e.copy(out=xtok[:T, :], in_=ps[4][:T, :]), R=[B_ps[4]], W=[B_xtok])
            for g in range(2):
                k.op("pe", lambda e, g=g: e.transpose(out=psb(5)[:T, g * 128:(g + 1) * 128], in_=bcT[:, g, :T], identity=identb[:, :]), R=[B_bcT, B_cm], W=[B_ps[5]])
            k.op("dve", lambda e: e.tensor_copy(out=btok[:T, :], in_=psb(5)[:T, 0:256]), R=[B_ps[5]], W=[B_btok])
            x3 = xtok[:T, :].rearrange("p (h d) -> p h d", h=8)
            k.op("dve", lambda e: e.tensor_tensor(out=xdec[:T, :].rearrange("p (h d) -> p h d", h=8), in0=x3, in1=bc8(DTDEC[:T], T, 64), op=ALU.mult), R=[B_xtok, B_sm8], W=[B_xdec])
            def chainA():
                if full:
                    k.op("pool", lambda e: e.tensor_tensor(out=xdt[:T, :].rearrange("p (h d) -> p h d", h=8), in0=x3, in1=bc8(DT[:T], T, 64), op=ALU.mult), R=[B_xtok, B_sm8], W=[B_xdt])
                    k.op("dve", lambda e: e.tensor_tensor(out=big1[:T, :, :T], in0=A_[:T].unsqueeze(2).broadcast_to([T, 8, T]),
                                                          in1=SL_ap[:T, :T].unsqueeze(1).broadcast_to([T, 8, T]), op=ALU.mult), R=[B_sm8, B_cmx], W=[B_big1])
                    for h in range(8):
                        pb = 1 + h // 4
                        k.op("pe", lambda e, h=h, pb=pb: e.matmul(out=ps[pb][:T, (h % 4) * 128:(h % 4) * 128 + T], lhsT=big1[:T, h, :T], rhs=U_ap[:T, :T], start=True, stop=True),
                             R=[B_big1, B_cmx], W=[B_ps[pb]])
                    for hh in range(2):
                        k.op("act", lambda e, hh=hh: e.activation(out=big2[:T, hh * 4:(hh + 1) * 4, :T], in_=ps[1 + hh][:T, :].rearrange("p (a b) -> p a b", a=4)[:, :, :T], func=AF.Exp),
                             R=[B_ps[1 + hh]], W=[B_big2])
                    yield
                    for g in range(2):
                        k.op("pe", lambda e, g=g: e.matmul(out=ps[4][:T, g * 128:g * 128 + T], lhsT=bcT[:, g, :T], rhs=bcT[:, 2 + g, :T], start=True, stop=True), R=[B_bcT], W=[B_ps[4]])
                    k.op("dve", lambda e: e.tensor_tensor(out=cbm[:T, :, :T], in0=ps[4][:T, 0:256].rearrange("p (a b) -> p a b", a=2)[:, :, :T],
                                                          in1=CAU_ap[:T, :T].unsqueeze(1).broadcast_to([T, 2, T]), op=ALU.mult), R=[B_ps[4], B_cmx], W=[B_cbm])
                    for g in range(2):
                        k.op("dve" if g == 0 else "pool", lambda e, g=g: e.tensor_tensor(out=Gb[:T, g * 4:(g + 1) * 4, :T], in0=big2[:T, g * 4:(g + 1) * 4, :T],
                                                                                        in1=cbm[:T, g, :T].unsqueeze(1).broadcast_to([T, 4, T]), op=ALU.mult), R=[B_big2, B_cbm], W=[B_G])
                    yield
                    for h in range(8):
                        k.op("pe", lambda e, h=h: e.matmul(out=ps[3][:T, h * 64:(h + 1) * 64], lhsT=Gb[:T, h, :T], rhs=xdt[:T, h * 64:(h + 1) * 64], start=True, stop=True),
                             R=[B_G, B_xdt], W=[B_ps[3]])
                    if smp and 'S' not in DBG:
                        yield from sample_state_loop()
                    for g in range(0 if smp else 2):
                        k.op("pe", lambda e, g=g: e.matmul(out=ps[4][:T, g * 256:(g + 1) * 256], lhsT=bcT[:, 2 + g, :T], rhs=Hbf[:, g * 256:(g + 1) * 256], start=True, stop=True),
                             R=[B_bcT, B_Hb], W=[B_ps[4]])
                    y3 = ysb[:T, :].rearrange("p (h d) -> p h d", h=8)
                    k.op("dve", lambda e: e.tensor_tensor(out=y3, in0=ps[4][:T, :].rearrange("p (h d) -> p h d", h=8), in1=bc8(EACS[:T], T, 64), op=ALU.mult), R=[B_ps[4], B_sm8], W=[B_y])
                    k.op("dve", lambda e: e.tensor_tensor(out=ysb[:T, :], in0=ysb[:T, :], in1=ps[3][:T, :], op=ALU.add), R=[B_y, B_ps[3]], W=[B_y])
                    k.op("pool", lambda e: e.tensor_tensor(out=tmpf[:T, 0:512].rearrange("p (h d) -> p h d", h=8), in0=x3, in1=bc8(smallc[:T, 2, :], T, 64), op=ALU.mult), R=[B_xtok, B_small], W=[B_tmpf])
                    k.op("dve", lambda e: e.tensor_tensor(out=ysb[:T, :], in0=ysb[:T, :], in1=tmpf[:T, 0:512], op=ALU.add), R=[B_y, B_tmpf], W=[B_y])
                yield
                for g in range(0 if smp else 2):
                    k.op("pe", lambda e, g=g: e.matmul(out=ps[1][:, g * 256:(g + 1) * 256], lhsT=btok[:T, g * 128:(g + 1) * 128], rhs=xdec[:T, g * 256:(g + 1) * 256], start=True, stop=True),
                         R=[B_btok, B_xdec], W=[B_ps[1]])
                if not smp:
                    k.op("dve", lambda e: e.tensor_tensor(out=Hst[:, :].rearrange("p (h d) -> p h d", h=8), in0=Hst[:, :].rearrange("p (h d) -> p h d", h=8), in1=bc8(EATOT, 128, 64), op=ALU.mult),
                         R=[B_H, B_sm8], W=[B_H])
                    k.op("dve", lambda e: e.tensor_tensor(out=Hst[:, :], in0=Hst[:, :], in1=ps[1][:, :], op=ALU.add), R=[B_H, B_ps[1]], W=[B_H])
                    k.op("act", lambda e: e.copy(out=Hbf[:, :], in_=Hst[:, :]), R=[B_H], W=[B_Hb])
                if full:
                    yield
                    k.op("dve", lambda e: e.tensor_tensor(out=ysb[:T, :], in0=ysb[:T, :], in1=zs[:T, :], op=ALU.mult), R=[B_y, B_zs], W=[B_y])
                    RS = rs8[:, :]
                    for g in range(2):
                        k.op("act", lambda e, g=g: e.activation(out=tmpf[:T, g * 256:(g + 1) * 256], in_=ysb[:T, g * 256:(g + 1) * 256], func=AF.Square, accum_out=RS[:T, g:g + 1]), R=[B_y], W=[B_tmpf, B_rs8])
                    k.op("dve", lambda e: e.tensor_scalar(out=RS[:T, 2:4], in0=RS[:T, 0:2], scalar1=1.0 / 256.0, scalar2=RMS_EPS, op0=ALU.mult, op1=ALU.add), R=[B_rs8], W=[B_rs8])
                    k.op("act", lambda e: e.activation(out=RS[:T, 2:4], in_=RS[:T, 2:4], func=AF.Ln), R=[B_rs8], W=[B_rs8])
                    k.op("act", lambda e: e.activation(out=RS[:T, 4:6], in_=RS[:T, 2:4], func=AF.Exp, scale=-0.5), R=[B_rs8], W=[B_rs8])
                    for g in range(2):
                        k.op("dve", lambda e, g=g: e.scalar_tensor_tensor(out=ymix[:T, g * 256:(g + 1) * 256], in0=ysb[:T, g * 256:(g + 1) * 256], scalar=RS[:T, 4 + g:5 + g], in1=normw[:T, g * 256:(g + 1) * 256],
                                                                          op0=ALU.mult, op1=ALU.mult), R=[B_y, B_rs8], W=[B_ymix])

            def chainB():
                if kv:
                    cosb = rope[:T, 0:8]; sinb = rope[:T, 8:16]
                    k3 = kvs[:T, 0:128].rearrange("p (h d) -> p h d", h=2)
                    rotary(k3, B_kvs, 2, T, cosb, sinb)
                    k.op("act", lambda e: e.copy(out=qb[:T, 512:640], in_=kvs[:T, 0:128]), R=[B_kvs], W=[B_qb])
                    k.op("dve", lambda e: e.tensor_copy(out=vb[par][:T, :], in_=kvs[:T, 128:256]), R=[B_kvs], W=[B_vb[par]])
                    k.op("pe", lambda e: e.transpose(out=psb(0)[:, 512:512 + T], in_=qb[:T, 512:640], identity=identb[:T, :T]), R=[B_qb, B_cm], W=[B_ps[0]])
                    k.op("act", lambda e: e.copy(out=kTb[par][:, :T], in_=psb(0)[:, 512:512 + T]), R=[B_ps[0]], W=[B_kT[par]])
                    if smp:
                        k.dma("sp", [(k_s[:, 124 + t_, :], kvs[t_:64:4, 0:128]) for t_ in range(4)] + [(v_s[:, 124 + t_, :], kvs[t_:64:4, 128:256]) for t_ in range(4)]
    , R=[B_kvs], W=[B_dummy], dsem=k.site("kv_s"))
                if full:
                    yield
                    q3 = qs[:T, :].rearrange("p (h d) -> p h d", h=8)
                    rotary(q3, B_qs, 8, T, rope[:T, 0:8], rope[:T, 8:16])
                    k.op("act", lambda e: e.copy(out=qb[:T, 0:512], in_=qs[:T, :]), R=[B_qs], W=[B_qb])
                    for c in range(4):
                        k.op("pe", lambda e, c=c: e.transpose(out=psb(0)[:, c * 128:c * 128 + T], in_=qb[:T, c * 128:(c + 1) * 128], identity=identb[:T, :T]), R=[B_qb, B_cm], W=[B_ps[0]])
                    k.op("act", lambda e: e.copy(out=qT[:, :, :T], in_=psb(0)[:, 0:512].rearrange("p (a b) -> p a b", a=4)[:, :, :T]), R=[B_ps[0]], W=[B_qT])
                    if smp and 'A' not in DBG:
                        yield from sample_attention(par)
                    for jg in range(0 if smp else 2):
                        yield
                        rows = slice(jg * 64, (jg + 1) * 64)
                        for c in range(4):
                            pb = (5, 7)[c // 2]
                            o0 = (c % 2) * 256
                            k.op("pe", lambda e, c=c, pb=pb, o0=o0: e.matmul(out=ps[pb][:T, o0:o0 + 128], lhsT=qT[rows, c, :T], rhs=kTb[1 - par][rows, :], start=True, stop=True),
                                 R=[B_qT, B_kT[1 - par]], W=[B_ps[pb]])
                            k.op("pe", lambda e, c=c, pb=pb, o0=o0: e.matmul(out=ps[pb][:T, o0 + 128:o0 + 256], lhsT=qT[rows, c, :T], rhs=kTb[par][rows, :], start=True, stop=True),
                                 R=[B_qT, B_kT[par]], W=[B_ps[pb]])
                        smv = smvb[:T, :, :]
                        for hh in range(2):
                            k.op("dve", lambda e, hh=hh: e.tensor_tensor(out=smv[:, hh * 2:(hh + 1) * 2, :], in0=ps[(5, 7)[hh]][:T, :].rearrange("p (a b) -> p a b", a=2),
                                                                         in1=mask_ap[:T, :].unsqueeze(1).broadcast_to([T, 2, 256]), op=ALU.add), R=[B_ps[(5, 7)[hh]], B_am], W=[B_smvb])
                        RMX = att8[:, 0, :]; M_ = att8[:, 1, :]; NM = att8[:, 2, :]; RSM = att8[:, 3, :]; ES = att8[:, 4, :]; DEN = att8[:, 5, :]; RDEN = att8[:, 6, :]
                        sk = smallc[:T, 3, jg * 4:(jg + 1) * 4]
                        k.op("dve", lambda e: e.tensor_reduce(out=RMX[:T, 0:4], in_=smv, axis=AX.X, op=ALU.max), R=[B_smvb], W=[B_att8])
                        k.op("dve", lambda e: e.tensor_tensor(out=M_[:T, 0:4], in0=RMX[:T, 0:4], in1=sk, op=ALU.max), R=[B_att8, B_small], W=[B_att8])
                        k.op("dve", lambda e: e.tensor_scalar(out=NM[:T, 0:4], in0=M_[:T, 0:4], scalar1=-1.0, scalar2=None, op0=ALU.mult), R=[B_att8], W=[B_att8])
                        Pv = Pb[:T, :, :]
                        for c in range(4):
                            k.op("act", lambda e, c=c: e.activation(out=Pv[:, c, :], in_=smv[:, c, :], func=AF.Exp, bias=NM[:T, c:c + 1], accum_out=RSM[:T, c:c + 1]), R=[B_smvb, B_att8], W=[B_Pb, B_att8])
                        k.op("dve", lambda e: e.tensor_tensor(out=ES[:T, 0:4], in0=sk, in1=M_[:T, 0:4], op=ALU.subtract), R=[B_att8, B_small], W=[B_att8])
                        k.op("act", lambda e: e.activation(out=ES[:T, 0:4], in_=ES[:T, 0:4], func=AF.Exp), R=[B_att8], W=[B_att8])
                        k.op("dve", lambda e: e.tensor_tensor(out=DEN[:T, 0:4], in0=ES[:T, 0:4], in1=RSM[:T, 0:4], op=ALU.add), R=[B_att8], W=[B_att8])
                        k.op("dve", lambda e: e.reciprocal(out=RDEN[:T, jg * 4:(jg + 1) * 4], in_=DEN[:T, 0:4]), R=[B_att8], W=[B_att8])
                        for c in range(4):
                            for kb in range(2):
                                k.op("pe", lambda e, c=c, kb=kb: e.transpose(out=psb(0)[:, (c * 2 + kb) * 128:(c * 2 + kb) * 128 + T], in_=Pv[:, c, kb * 128:(kb + 1) * 128], identity=identb[:T, :T]),
                                     R=[B_Pb, B_cm], W=[B_ps[0]])
                        k.op("act", lambda e: e.copy(out=PT[:, :, :T], in_=psb(0)[:, 0:1024].rearrange("p (a b) -> p a b", a=8)[:, :, :T]), R=[B_ps[0]], W=[B_PT])
                        for c in range(4):
                            head = c + 4 * jg
                            k.op("pe", lambda e, c=c, head=head: e.matmul(out=ps[6][:T, head * 64:(head + 1) * 64], lhsT=PT[:, c * 2, :T], rhs=vb[1 - par][:, jg * 64:(jg + 1) * 64], start=True, stop=False),
                                 R=[B_PT, B_vb[1 - par]], W=[B_ps[6]])
                            k.op("pe", lambda e, c=c, head=head: e.matmul(out=ps[6][:T, head * 64:(head + 1) * 64], lhsT=PT[:, c * 2 + 1, :T], rhs=vb[par][:, jg * 64:(jg + 1) * 64], start=False, stop=True),
                                 R=[B_PT, B_vb[par]], W=[B_ps[6]])
                    if not smp:
                      k.op("dve", lambda e: e.tensor_tensor(out=ymix[:T, 512:1024].rearrange("p (h d) -> p h d", h=8), in0=ps[6][:T, :].rearrange("p (h d) -> p h d", h=8),
                                                          in1=bc8(att8[:T, 6, :], T, 64), op=ALU.mult), R=[B_ps[6], B_att8], W=[B_ymix])
                yield

            yield "AB"
            if full and 'I' not in DBG:
                yield from zip_gens(chainA(), chainB())
            else:
                yield from chainA()
                yield from chainB()
            if not full:
                return
            yield "S"
            pend_o = [oload(c_) for c_ in range(NOS - 1)]
            for kk in range(8):
                k.op("pe", lambda e, kk=kk: e.transpose(out=psb(0)[:, kk * 128:kk * 128 + T], in_=ymix[:T, kk * 128:(kk + 1) * 128], identity=identb[:T, :T]), R=[B_ymix, B_cm], W=[B_ps[0]])
            k.op("act", lambda e: e.copy(out=ymixT[:, :, :T], in_=psb(0)[:, 0:1024].rearrange("p (a b) -> p a b", a=8)[:, :, :T]), R=[B_ps[0]], W=[B_ymixT])
            yield
            for c_ in range(16):
                kk, n = c_ // 2, c_ % 2
                i = pend_o.pop(0)
                if c_ + NOS - 1 < 16:
                    pend_o.append(oload(c_ + NOS - 1))
                k.op("pe", lambda e, n=n, kk=kk, i=i: e.matmul(out=ps[1 + n][:T, :], lhsT=ymixT[:, kk, :T], rhs=oslot[i][:, :], start=(kk == 0), stop=(kk == 7)),
                     R=[B_ymixT, B_oslot[i]], W=[B_ps[1 + n]])
            for n in range(2):
                sl = slice(n * 512, (n + 1) * 512)
                k.op("dve", lambda e, n=n, sl=sl: e.tensor_tensor(out=tmpf[:T, sl], in0=ps[1 + n][:T, :], in1=ada[:T, 2, sl], op=ALU.mult), R=[B_ps[1 + n], B_ada], W=[B_tmpf])
            k.op("dve", lambda e: e.scalar_tensor_tensor(out=X1v[:T, sub, :], in0=xp[:T, :], scalar=ALPHA, in1=tmpf[:T, :], op0=ALU.mult, op1=ALU.add), R=[B_xp, B_tmpf], W=[B_X1v[sub]])
            yield
            layer_norm_rows(X1v[:, sub, :], B_X1v[sub], T, None, X1v[:, sub, :], B_X1v[sub])
            yield
            modulate_T(X1v[:, sub, :], B_X1v[sub], T, 4, h2Tv, B_h2Tv[sub], sub * 128)
            yield
            affine(X1v[:, sub, :], B_X1v[sub], T, 2, X1v[:, sub, :], B_X1v[sub], ("pool", "pool"))

        def sample_state_loop():
            T = 64
            A_ = sm8[:, 1, :]
            k.op("dve", lambda e: e.tensor_copy(out=tmpf[:T, 0:512].rearrange("p (h d) -> p h d", h=8), in_=bc8(A_[:T], T, 64)), R=[B_sm8], W=[B_tmpf])
            for j in range(4):
                k.op("pe", lambda e, j=j: e.matmul(out=ps[2][:, 256 + j * 16:256 + (j + 1) * 16], lhsT=tmpf[:T, j * 128:(j + 1) * 128], rhs=smk[:T, :], start=True, stop=True),
                     R=[B_tmpf, B_smk], W=[B_ps[2]])
            k.op("act", lambda e: e.activation(out=dcol[:, :, :], in_=ps[2][:, 256:320].rearrange("p (a b) -> p a b", a=4), func=AF.Exp), R=[B_ps[2]], W=[B_dcol])
            for sq in range(NSEQ):
                sl_ = sq % 2
                k.dma("sp", [(hn[sl_].rearrange("p (j n) -> p j n", j=4), sssm[sq].rearrange("(j p) n -> p j n", p=128))], R=[], W=[B_hn[sl_]], dsem=D_hn[sl_])
                k.op("act", lambda e, sl_=sl_: e.copy(out=hnb, in_=hn[sl_]), R=[B_hn[sl_]], W=[B_hnb])
                for j in range(4):
                    k.op("pe", lambda e, j=j: e.transpose(out=psb(0)[:, j * 128:(j + 1) * 128], in_=hnb[:, j * 128:(j + 1) * 128], identity=identb[:, :]), R=[B_hnb, B_cm], W=[B_ps[0]])
                k.op("dve", lambda e: e.tensor_copy(out=h0T, in_=psb(0)[:, 0:512]), R=[B_ps[0]], W=[B_h0T])
                for j in range(4):
                    k.op("pe", lambda e, j=j, sq=sq: e.matmul(out=ps[2][:, j * 64 + 4 * sq:j * 64 + 4 * sq + 4], lhsT=h0T[:, j * 128:(j + 1) * 128], rhs=bcT[:, 2 + j // 2, 4 * sq:4 * sq + 4], start=True, stop=True),
                         R=[B_h0T, B_bcT], W=[B_ps[2]])
                k.op("pool", lambda e, sq=sq: e.tensor_scalar(out=xdm[:T, :], in0=xdec[:T, :], scalar1=smk[:T, sq:sq + 1], scalar2=None, op0=ALU.mult), R=[B_xdec, B_smk], W=[B_xdm])
                pb = 1
                for j in range(4):
                    k.op("pe", lambda e, j=j, pb=pb: e.matmul(out=ps[pb][:, j * 128:(j + 1) * 128], lhsT=xdm[:T, j * 128:(j + 1) * 128], rhs=btok[:T, (j // 2) * 128:(j // 2 + 1) * 128], start=True, stop=True),
                         R=[B_xdm, B_btok], W=[B_ps[pb]])
                h3 = hn[sl_].rearrange("p (j n) -> p j n", j=4)
                k.op("dve", lambda e, h3=h3, sq=sq: e.tensor_tensor(out=h3, in0=h3, in1=dcol[:, :, sq:sq + 1].broadcast_to([128, 4, 128]), op=ALU.mult), R=[B_hn[sl_], B_dcol], W=[B_hn[sl_]])
                k.op("dve", lambda e, sl_=sl_, pb=pb: e.tensor_tensor(out=hn[sl_], in0=hn[sl_], in1=ps[pb][:, :], op=ALU.add), R=[B_hn[sl_], B_ps[pb]], W=[B_hn[sl_]])
                k.dma("sp", [(ssm_s[sq].rearrange("(j p) n -> p j n", p=128), h3)], R=[B_hn[sl_]], W=[B_dummy], dsem=D_hn[sl_])
                yield
            k.op("act", lambda e: e.copy(out=yoT, in_=ps[2][:, 0:256]), R=[B_ps[2]], W=[B_yoT])
            for j in range(4):
                k.op("pe", lambda e, j=j: e.transpose(out=ps[4][:T, j * 128:(j + 1) * 128], in_=yoT[:, j * 64:(j + 1) * 64], identity=identf), R=[B_yoT, B_cm], W=[B_ps[4]])

        def sample_attention(par):
            T = 64
            mc = dmk[:16, 0:128]
            for c in range(4):
                k.op("pe", lambda e, c=c: e.transpose(out=psb(0)[:64, 640 + c * 64:640 + (c + 1) * 64], in_=qb[:T, c * 128 + 64:(c + 1) * 128], identity=identb[:T, :T]), R=[B_qb, B_cm], W=[B_ps[0]])
            k.op("pe", lambda e: e.transpose(out=psb(0)[:64, 896:960], in_=qb[:T, 576:640], identity=identb[:T, :T]), R=[B_qb, B_cm], W=[B_ps[0]])
            k.op("dve", lambda e: e.tensor_copy(out=qTs[:, 0, :, :].rearrange("p s (c t) -> p s c t", c=4), in_=qT[0:64, :, 0:64].rearrange("p c (s t) -> p s c t", t=4)), R=[B_qT], W=[B_qTs])
            k.op("dve", lambda e: e.tensor_copy(out=qTs[:, 1, :, :].rearrange("p s (c t) -> p s c t", c=4), in_=psb(0)[:64, 640:896].rearrange("p (c s t) -> p s c t", c=4, t=4)), R=[B_ps[0]], W=[B_qTs])
            k.op("act", lambda e: e.copy(out=knT1[:, :], in_=psb(0)[:64, 896:960]), R=[B_ps[0]], W=[B_qTs])

            for sq in range(NSEQ):
                k.dma("sp", [(ckf, ck[sq]), (cvf, cv[sq])], R=[], W=[B_ckf, B_cvf], dsem=D_ckv)
                if '1' not in DBG:
                    k.dma("sp", [(k_s[sq, 0:124, :], ckf[4:128, :]), (v_s[sq, 0:124, :], cvf[4:128, :])], R=[B_ckf, B_cvf], W=[B_dummy], dsem=k.site("kvc_s"))
                k.op("act", lambda e: e.copy(out=ckb, in_=ckf), R=[B_ckf], W=[B_ckb])
                k.op("pool", lambda e: e.tensor_copy(out=cvb, in_=cvf), R=[B_cvf], W=[B_cvb])
                for j in range(2):
                    k.op("pe", lambda e, j=j: e.transpose(out=psb(0)[:64, j * 128:(j + 1) * 128], in_=ckb[:, j * 64:(j + 1) * 64], identity=identb[:, :]), R=[B_ckb, B_cm], W=[B_ps[0]])
                k.op("act", lambda e: e.copy(out=kcT2[:, :, :].rearrange("p j x -> p (j x)"), in_=psb(0)[:64, 0:256]), R=[B_ps[0]], W=[B_kcT])
                pb = 5 + sq % 2
                if '4' in DBG:
                    continue
                for j in range(2):
                    lq = qTs[:, j, sq, :]
                    knew = kTb[par][0:64, 0:64] if j == 0 else knT1[:, :]
                    k.op("pe", lambda e, j=j, pb=pb, lq=lq: e.matmul(out=ps[pb][:16, j * 192:j * 192 + 128], lhsT=lq, rhs=kcT2[:, j, :], start=True, stop=True), R=[B_qTs, B_kcT], W=[B_ps[pb]])
                    k.op("pe", lambda e, j=j, pb=pb, lq=lq, knew=knew: e.matmul(out=ps[pb][:16, j * 192 + 128:j * 192 + 192], lhsT=lq, rhs=knew, start=True, stop=True), R=[B_qTs, B_kT[par]], W=[B_ps[pb]])
                smv = smvb[:16, 0:2, 0:192]
                psv = ps[pb][:16, 0:384].rearrange("p (j x) -> p j x", j=2)
                k.op("dve", lambda e, psv=psv: e.tensor_tensor(out=smv[:, :, 0:128], in0=psv[:, :, 0:128], in1=mc.unsqueeze(1).broadcast_to([16, 2, 128]), op=ALU.add), R=[B_ps[pb], B_dmk], W=[B_smvb])
                mn = dmk[:16, 128 + sq * 64:128 + (sq + 1) * 64]
                k.op("dve", lambda e, psv=psv, mn=mn: e.tensor_tensor(out=smv[:, :, 128:192], in0=psv[:, :, 128:192], in1=mn.unsqueeze(1).broadcast_to([16, 2, 64]), op=ALU.add), R=[B_ps[pb], B_dmk], W=[B_smvb])
                RMX = att8[:16, 0, 0:2]; M_ = att8[:16, 1, 0:2]; RSM = att8[:16, 3, 0:2]; ES = att8[:16, 4, 0:2]; DEN = att8[:16, 5, 0:2]; RDEN = att8[:16, 6, 0:2]
                k.op("dve", lambda e: e.tensor_reduce(out=RMX, in_=smv, axis=AX.X, op=ALU.max), R=[B_smvb], W=[B_att8])
                k.op("dve", lambda e: e.tensor_tensor(out=M_, in0=RMX, in1=sink16[:, :], op=ALU.max), R=[B_att8, B_smk], W=[B_att8])
                k.op("dve", lambda e: e.tensor_tensor(out=smv, in0=smv, in1=M_.unsqueeze(2).broadcast_to([16, 2, 192]), op=ALU.subtract), R=[B_smvb, B_att8], W=[B_smvb])
                k.op("act", lambda e: e.activation(out=smv, in_=smv, func=AF.Exp), R=[B_smvb], W=[B_smvb])
                k.op("dve", lambda e: e.tensor_reduce(out=RSM, in_=smv, axis=AX.X, op=ALU.add), R=[B_smvb], W=[B_att8])
                k.op("dve", lambda e: e.tensor_tensor(out=ES, in0=sink16[:, :], in1=M_, op=ALU.subtract), R=[B_att8, B_smk], W=[B_att8])
                k.op("act", lambda e: e.activation(out=ES, in_=ES, func=AF.Exp), R=[B_att8], W=[B_att8])
                k.op("dve", lambda e: e.tensor_tensor(out=DEN, in0=ES, in1=RSM, op=ALU.add), R=[B_att8], W=[B_att8])
                k.op("dve", lambda e: e.reciprocal(out=RDEN, in_=DEN), R=[B_att8], W=[B_att8])
                Pv = Pb[:16, 0:2, 0:192]
                k.op("dve", lambda e: e.tensor_tensor(out=Pv, in0=smv, in1=RDEN.unsqueeze(2).broadcast_to([16, 2, 192]), op=ALU.mult), R=[B_smvb, B_att8], W=[B_Pb])
                if '3' in DBG:
                    continue
                for j in range(2):
                    k.op("pe", lambda e, j=j: e.transpose(out=psb(0)[:, (2 * j) * 16:(2 * j) * 16 + 16], in_=Pv[:, j, 0:128], identity=identb[:16, :16]), R=[B_Pb, B_cm], W=[B_ps[0]])
                    k.op("pe", lambda e, j=j: e.transpose(out=psb(0)[:64, (2 * j + 1) * 16:(2 * j + 1) * 16 + 16], in_=Pv[:, j, 128:192], identity=identb[:16, :16]), R=[B_Pb, B_cm], W=[B_ps[0]])
                k.op("act", lambda e: e.copy(out=PT[:, 0, 0:64], in_=psb(0)[:, 0:64]), R=[B_ps[0]], W=[B_PT])
                for j in range(2):
                    k.op("pe", lambda e, j=j: e.matmul(out=ps[7][:16, j * 64:(j + 1) * 64], lhsT=PT[:, 0, (2 * j) * 16:(2 * j) * 16 + 16], rhs=cvb[:, j * 64:(j + 1) * 64], start=True, stop=False), R=[B_PT, B_cvb], W=[B_ps[7]])
                    k.op("pe", lambda e, j=j: e.matmul(out=ps[7][:16, j * 64:(j + 1) * 64], lhsT=PT[:64, 0, (2 * j + 1) * 16:(2 * j + 1) * 16 + 16], rhs=vb[par][:64, j * 64:(j + 1) * 64], start=False, stop=True), R=[B_PT, B_vb[par]], W=[B_ps[7]])
                k.op("act", lambda e, sq=sq: e.copy(out=osb[:16, sq * 128:(sq + 1) * 128], in_=ps[7][:16, 0:128]), R=[B_ps[7]], W=[B_X1[NXB - 1][0]])
                yield
            if '2' in DBG:
                return
            B_osc = Buf("osc")
            k.dma("sp", [(osc[:, :], osb[:16, :])], R=[B_X1[NXB - 1][0]], W=[B_osc], dsem=k.site("osc_w"))
            srcv = osc.rearrange("(c t) (s j d) -> t j s c d", t=4, s=NSEQ, j=2)
            k.dma("sp", [(ymix[t_:64:4, 512 + j * 256:512 + (j + 1) * 256].rearrange("p (c d) -> p c d", c=4), srcv[t_, j]) for t_ in range(4) for j in range(2)],
                  R=[B_osc], W=[B_ymix], dsem=k.site("osc_r"))

        def rotary(v3, B_v, nh, T, cosb, sinb):
            cb = cosb.unsqueeze(1).broadcast_to([T, nh, 8]); sbb = sinb.unsqueeze(1).broadcast_to([T, nh, 8])
            t = [rt[:T, i, 0:nh * 8].rearrange("p (h d) -> p h d", h=nh) for i in range(4)]
            x1 = v3[:, :, 0:8]; x2 = v3[:, :, 8:16]
            k.op("dve", lambda e: e.tensor_tensor(out=t[0], in0=x1, in1=cb, op=ALU.mult), R=[B_v, B_rope], W=[B_rt])
            k.op("pool", lambda e: e.tensor_tensor(out=t[1], in0=x2, in1=sbb, op=ALU.mult), R=[B_v, B_rope], W=[B_rt])
            k.op("dve", lambda e: e.tensor_tensor(out=t[2], in0=x2, in1=cb, op=ALU.mult), R=[B_v, B_rope], W=[B_rt])
            k.op("pool", lambda e: e.tensor_tensor(out=t[3], in0=x1, in1=sbb, op=ALU.mult), R=[B_v, B_rope], W=[B_rt])
            k.op("dve", lambda e: e.tensor_tensor(out=x1, in0=t[0], in1=t[1], op=ALU.subtract), R=[B_rt], W=[B_v])
            k.op("dve", lambda e: e.tensor_tensor(out=x2, in0=t[2], in1=t[3], op=ALU.add), R=[B_rt], W=[B_v])

        wq = {"n": 0}

        def wload(src_ap):
            i = wq["n"] % NSLOT
            wq["n"] += 1
            k.dma("sp", [(wslot[i][:, :], src_ap)], R=[B_wup_s, B_wdn_s, B_wout_s], W=[B_wslot[i]], dsem=D_wslot[i])
            return i

        def mlp(T, nsub, y_rows_fn, xb=0, part="all"):
            xb = xb % NXB
            X1v = X1[:, xb]; B_X1v = B_X1[xb]; h2Tv = h2T[:, xb]; B_h2Tv = B_h2T[xb]
            TTl = (nsub - 1) * 128 + T
            if part == "tail":
                for s_ in range(nsub):
                    Ts = T if s_ == nsub - 1 else 128
                    layer_norm_rows(X1v[:, s_, :], B_X1v[s_], Ts, 4, X1v[:, s_, :], B_X1v[s_])
                    yield
                    k.dma("sp", [(y_rows_fn(s_, Ts), X1v[:Ts, s_, :])], R=[B_X1v[s_]], W=[B_dummy], dsem=k.site(f"yo{s_}"))
                    yield
                return

            def upsrc(g2):
                return wup_s[g2 // 4, :, (g2 % 4) * 1024:(g2 % 4 + 1) * 1024]
            WD = NSLOT - 1
            pend = [wload(upsrc(g_)) for g_ in range(WD)]
            for g in range(32):
                i = pend.pop(0)
                if g + WD < 32:
                    pend.append(wload(upsrc(g + WD)))
                wv = wslot[i][:, :].rearrange("p (fc kk f) -> p fc kk f", fc=1, kk=8)
                for fcl in range(1):
                    fc = g
                    ub = fc % 2
                    for kk in range(8):
                        k.op("pe", lambda e, fcl=fcl, kk=kk, ub=ub, wv=wv: e.matmul(out=ps[6 + ub][:, 0:TTl], lhsT=wv[:, fcl, kk, :], rhs=h2Tv[:, kk, 0:TTl], start=(kk == 0), stop=(kk == 7)),
                             R=[B_wslot[i]] + B_h2Tv[:nsub], W=[B_ps[6 + ub]])
                    k.op("act", lambda e, ub=ub: e.activation(out=urelu[ub][:, 0:TTl], in_=ps[6 + ub][:, 0:TTl], func=AF.Relu), R=[B_ps[6 + ub]], W=[B_urelu[ub]])
                    k.op("pool", lambda e, fc=fc, ub=ub: e.tensor_tensor(out=uT[:, fc, 0:TTl], in0=urelu[ub][:, 0:TTl], in1=urelu[ub][:, 0:TTl], op=ALU.mult), R=[B_urelu[ub]], W=[B_uT[fc], B_stgb[fc // 16]])
                    yield
            for n in range(2):
                def dnsrc(g2, n=n):
                    return wdn_s[n, g2 // 4, :, (g2 % 4) * 1024:(g2 % 4 + 1) * 1024]
                pend = [wload(dnsrc(g_)) for g_ in range(WD)]
                for g in range(16):
                    i = pend.pop(0)
                    if g + WD < 16:
                        pend.append(wload(dnsrc(g + WD)))
                    wv = wslot[i][:, :].rearrange("p (a n) -> p a n", a=2)
                    for s_ in range(nsub):
                        Ts = T if s_ == nsub - 1 else 128
                        for fl in range(2):
                            fk = g * 2 + fl
                            k.op("pe", lambda e, s_=s_, fl=fl, fk=fk, wv=wv, Ts=Ts: e.matmul(out=ps[4 + s_][:Ts, :], lhsT=uT[:, fk, s_ * 128:s_ * 128 + Ts], rhs=wv[:, fl, :], start=(fk == 0), stop=(fk == 31)),
                                 R=[B_wslot[i], B_uT[fk]], W=[B_ps[4 + s_]])
                    yield
                for s_ in range(nsub):
                    Ts = T if s_ == nsub - 1 else 128
                    sl = slice(n * 512, (n + 1) * 512)
                    k.op("dve", lambda e, s_=s_, sl=sl, Ts=Ts: e.tensor_tensor(out=tmpf[:Ts, sl], in0=ps[4 + s_][:Ts, :], in1=ada[:Ts, 5, sl], op=ALU.mult), R=[B_ps[4 + s_], B_ada], W=[B_tmpf])
                    k.op("dve", lambda e, s_=s_, sl=sl, Ts=Ts: e.scalar_tensor_tensor(out=X1v[:Ts, s_, sl], in0=X1v[:Ts, s_, sl], scalar=ALPHA, in1=tmpf[:Ts, sl], op0=ALU.mult, op1=ALU.add),
                         R=[B_X1v[s_], B_tmpf], W=[B_X1v[s_]])
            if part == "main":
                return
            for s_ in range(nsub):
                Ts = T if s_ == nsub - 1 else 128
                layer_norm_rows(X1v[:, s_, :], B_X1v[s_], Ts, 4, X1v[:, s_, :], B_X1v[s_])
                yield
                k.dma("sp", [(y_rows_fn(s_, Ts), X1v[:Ts, s_, :])], R=[B_X1v[s_]], W=[B_dummy], dsem=k.site(f"yo{s_}"))

        def run(g):
            for _ in g:
                pass

        def zip_gens(ga, gb):
            da = db = False
            while not (da and db):
                if not da:
                    try:
                        next(ga)
                    except StopIteration:
                        da = True
                if not db:
                    try:
                        next(gb)
                    except StopIteration:
                        db = True
                yield

        def interleave(ga, gb):
            da = db = False
            while not (da and db):
                if not da:
                    try:
                        next(ga)
                    except StopIteration:
                        da = True
                if not db:
                    try:
                        next(gb)
                    except StopIteration:
                        db = True

        def adv_until(g, mark):
            for v in g:
                if v == mark:
                    return True
            return False

        def zip_until(ga, gb, mark_b):
            da = db = False
            while not (da and db):
                if not da:
                    try:
                        next(ga)
                    except StopIteration:
                        da = True
                if not db:
                    try:
                        if next(gb) == mark_b:
                            db = True
                    except StopIteration:
                        db = True

        k.op("dve", lambda e: e.memset(Hst[:, :], 0.0), R=[], W=[B_H])
        k.op("dve", lambda e: e.memset(Hbf[:, :], 0.0), R=[], W=[B_Hb])
        k.op("pool", lambda e: e.memset(xbcT[:, :, :], 0.0), R=[], W=[B_xbcT])
        for i in range(2):
            k.op("pool", lambda e, i=i: e.memset(kTb[i][:, :], 0.0), R=[], W=[B_kT[i]])
            k.op("pool", lambda e, i=i: e.memset(vb[i][:, :], 0.0), R=[], W=[B_vb[i]])

        if with_sample:
            compute_ada(c_s, NTS)
            late_weight_prep()
            fence()
            k.dma("sp", [(cms[:, :, :], cmask_s), (dmk[:16, :], dmask)], R=[], W=[B_cms, B_dmk], dsem=k.site("cms"))
            k.dma("sp", [(smk[:, :], smk_d), (sink16[:, :], sink16_d)], R=[], W=[B_smk], dsem=k.site("smk"))
            run(mixer_tile(x_s, rope_s, NTS, 0, "full", None, 0, smp=True, xb=1))
            fence()
            run(mlp(NTS, 1, lambda s_, Ts: y_s[0:Ts, :], xb=1))
        if not with_sample:
            late_weight_prep()
        compute_ada(c_p.broadcast_to([128, D]), 128)

        def pre_gen(t):
            return mixer_tile(x_prev[t * 128:(t + 1) * 128, :], rope_prev[:, :], 128, 1, "statekv" if t == NT - 1 else "state", None, 0)

        if 'Q' in DBG:
            for t in range(NT):
                run(pre_gen(t))
        else:
            curp = pre_gen(0)
            adv_until(curp, "P2")
            for t in range(NT):
                nxtp = pre_gen(t + 1) if t + 1 < NT else None
                if nxtp is not None:
                    zip_until(curp, nxtp, "P2")
                else:
                    run(curp)
                curp = nxtp
        k.op("dve", lambda e: e.tensor_scalar(out=Hst[:, :], in0=Hst[:, :], scalar1=flg[:, 0:1], scalar2=None, op0=ALU.mult), R=[B_H, B_am], W=[B_H])
        k.op("act", lambda e: e.copy(out=Hbf[:, :], in_=Hst[:, :]), R=[B_H], W=[B_Hb])
        k.op("pool", lambda e: e.tensor_scalar(out=xbcT[:, :, 0:3], in0=xbcT[:, :, 0:3], scalar1=flg[:, 0:1], scalar2=None, op0=ALU.mult), R=[B_xbcT, B_am], W=[B_xbcT])

        def final_outputs():
            k.dma("sp", [(k_p[:, :], kvs[:, 0:128]), (v_p[:, :], kvs[:, 128:256])], R=[B_kvs], W=[B_dummy], dsem=k.site("kv_p"))
            k.dma("sp", [(conv_p[:, c * 128:(c + 1) * 128].rearrange("j p -> p j"), xbcT[:, c, 0:3]) for c in range(8)], R=[B_xbcT], W=[B_dummy], dsem=k.site("conv_p"))
            for c in range(4):
                k.op("pe", lambda e, c=c: e.transpose(out=ps[4][:, c * 128:(c + 1) * 128], in_=Hst[:, c * 128:(c + 1) * 128], identity=identf), R=[B_H, B_cm], W=[B_ps[4]])
            k.op("act", lambda e: e.copy(out=tmpf[:, 0:512], in_=ps[4][:, :]), R=[B_ps[4]], W=[B_tmpf])
            k.dma("sp", [(ssm_p.rearrange("(c p) n -> p c n", p=128), tmpf[:, 0:512].rearrange("p (c n) -> p c n", c=4))], R=[B_tmpf], W=[B_dummy], dsem=k.site("ssm_p"))

        def tile_gen(t):
            return mixer_tile(x_main[t * 128:(t + 1) * 128, :], rope_main[t * 128:(t + 1) * 128, :], 128, t % 2, "full",
                              am[:, 0, :] if t == 0 else am[:, 1, :], t % NSUB, xb=0)

        def mlp_st(st, part):
            return mlp(128, NSUB, lambda s_, Ts, st=st: y_main[(st * NSUB + s_) * 128:(st * NSUB + s_) * 128 + Ts, :], xb=0, part=part)

        def empty():
            return
            yield

        cur = tile_gen(0)
        adv_until(cur, "S")
        for t in range(NT):
            nxt = tile_gen(t + 1) if t + 1 < NT else None
            if nxt is not None and 'P' not in DBG:
                zip_until(cur, nxt, "AB")
            else:
                run(cur)
                if nxt is not None:
                    adv_until(nxt, "AB")
            if t == NT - 1:
                final_outputs()
            tail = empty()
            if (t + 1) % NSUB == 0:
                st = t // NSUB
                run(mlp_st(st, "main"))
                tail = mlp_st(st, "tail")
            if nxt is not None:
                zip_until(tail, nxt, "S")
            else:
                run(tail)
            cur = nxt

        k.finish("sp")
        build.ninstr = k.ninstr
    return nc


def _rope_table(pos):
    half = 8
    inv = (np.float32(500000.0) ** (-np.arange(half, dtype=np.float32) / np.float32(half))).astype(np.float32)
    ang = (pos.astype(np.float32)[:, None] * inv[None, :]).astype(np.float32)
    return np.concatenate([np.cos(ang), np.sin(ang)], 1).astype(np.float32)


def _band_mask(first_masked):
    i = np.arange(128)[:, None]
    j = np.arange(256)[None, :]
    ok = (j > i) & (j <= i + 128)
    if first_masked:
        ok = ok & (j >= 128)
    return np.where(ok, 0.0, NEG).astype(np.float32)


def _prep_inputs(inp, LH):
    f = np.ascontiguousarray
    xprompt = inp["x_prompt"]
    B = xprompt.shape[0]
    w_in = inp["w_in"][0]
    Z_END = 512; XBC_END = 1536; DT_END = 1544; Q_END = 2056; K_END = 2184
    qcols = []
    for c in range(4):
        for j in range(2):
            h = c + 4 * j
            qcols += list(range(DT_END + h * 64, DT_END + (h + 1) * 64))
    cols = list(range(0, XBC_END)) + qcols + list(range(Q_END, K_END)) + list(range(K_END, K_END + 128)) + list(range(XBC_END, DT_END))
    w_in_r = f(w_in[:, cols])
    sinks = inp["sinks"][0]
    sinks_p = f(np.array([sinks[c + 4 * j] for j in range(2) for c in range(4)], np.float32)[None, :])
    U, SL, BD = _consts(128, 1)
    ident = np.eye(128, dtype=np.float32)
    cau = U.copy()
    cmask = f(np.stack([U, SL, np.ones((128, 128), np.float32), ident, cau], 1))
    common = {
        "ln_in_g": f(inp["ln_in_g"][None, :]), "ln_in_b": f(inp["ln_in_b"][None, :]),
        "w_ada": f(inp["w_ada"][0]), "b_ada": f(inp["b_ada"]), "w_in": w_in_r,
        "conv_w": f(inp["conv_w"][0]), "conv_b": f(inp["conv_b"]), "dt_bias": f(inp["dt_bias"]),
        "A_log": f(inp["A_log"]), "D_skip": f(inp["D_skip"]), "ssm_norm_w": f(inp["ssm_norm_w"]),
        "sinks_p": sinks_p, "w_out": f(inp["w_out"][0]), "ln1_g": f(inp["ln1_g"]), "ln1_b": f(inp["ln1_b"]),
        "w_up": f(inp["w_up"][0]), "w_down": f(inp["w_down"][0]), "ln2_g": f(inp["ln2_g"]), "ln2_b": f(inp["ln2_b"]),
        "cmask": cmask, "m2": _band_mask(False),
    }
    Us, SLs, BDs = _consts(64, 16)
    cms_ = np.zeros((128, 3, 128), np.float32)
    for i_, a_ in enumerate((Us, SLs, BDs)):
        cms_[:64, i_, :64] = a_
    smk_ = np.zeros((128, 16), np.float32)
    smk_[np.arange(64), np.arange(64) // 4] = 1.0
    sink16 = np.zeros((16, 2), np.float32)
    dm = np.full((16, 128 + 16 * 64), NEG, np.float32)
    for c_ in range(4):
        for t_ in range(4):
            r_ = c_ * 4 + t_
            for j_ in range(2):
                sink16[r_, j_] = sinks[c_ + 4 * j_]
            dm[r_, t_ + 1:128] = 0.0
            for s_ in range(16):
                dm[r_, 128 + s_ * 64 + s_ * 4:128 + s_ * 64 + s_ * 4 + t_ + 1] = 0.0
    common.update({"cmask_s": f(cms_), "smk": smk_, "sink16": sink16, "dmask": dm,
                   "rope_s": f(np.tile(_rope_table(PAST + np.arange(4)), (16, 1)))})
    maps = []
    for c in range(8):
        b, half = c // 2, c % 2
        m = dict(common)
        sq = slice(16 * c, 16 * (c + 1))
        m["x_s"] = f(inp["x_sample"][sq].reshape(64, D))
        m["c_s"] = f(np.repeat(inp["c_sample"][sq], 4, axis=0))
        m["sconv"] = f(inp["state_conv"][0, sq].reshape(48, D))
        m["sssm"] = f(inp["state_ssm"][0, sq].reshape(16, 512, 128))
        m["ck"] = f(inp["cache_k"][0, sq].reshape(16, 128, 128))
        m["cv"] = f(inp["cache_v"][0, sq].reshape(16, 128, 128))
        m["x_prev"] = f(xprompt[b, 0:LH])
        m["x_main"] = f(xprompt[b, half * LH:(half + 1) * LH])
        m["c_p"] = f(inp["c_prompt"][b][None, :])
        fl = np.zeros((128, 2), np.float32); fl[:, 0] = float(half)
        m["flag"] = fl
        m["m0"] = _band_mask(half == 0)
        m["rope_main"] = _rope_table(np.arange(half * LH, (half + 1) * LH))
        m["rope_prev"] = _rope_table(np.arange(LH - 128, LH))
        maps.append(m)
    return maps


_CACHE = {}


def kernel(**inputs):
    inp = {k_: np.asarray(v) for k_, v in inputs.items()}
    B, S, _ = inp["x_prompt"].shape
    LH = S // 2
    key = LH
    if key not in _CACHE:
        _CACHE[key] = build(LH)
    nc = _CACHE[key]
    maps = _prep_inputs(inp, LH)
    res = run_bass_kernel_spmd(nc, maps, core_ids=list(range(8)))
    R = res.results
    y_prompt = np.stack([np.concatenate([R[2 * b]["y_main"], R[2 * b + 1]["y_main"]], 0) for b in range(B)], 0)
    conv_p = np.stack([R[2 * b + 1]["conv_p"] for b in range(B)], 0)[None]
    ssm_p = np.stack([R[2 * b + 1]["ssm_p"].reshape(8, 64, 128) for b in range(B)], 0)[None]
    k_p = np.stack([R[2 * b + 1]["k_p"].reshape(128, 2, 64) for b in range(B)], 0)[None]
    v_p = np.stack([R[2 * b + 1]["v_p"].reshape(128, 2, 64) for b in range(B)], 0)[None]
    y_sample = np.concatenate([R[c]["y_s"].reshape(16, 4, D) for c in range(8)], 0)
    conv_s = np.concatenate([R[c]["conv_s"] for c in range(8)], 0)[None]
    ssm_s = np.concatenate([R[c]["ssm_s"].reshape(16, 8, 64, 128) for c in range(8)], 0)[None]
    k_s = np.concatenate([R[c]["k_s"].reshape(16, 128, 2, 64) for c in range(8)], 0)[None]
    v_s = np.concatenate([R[c]["v_s"].reshape(16, 128, 2, 64) for c in range(8)], 0)[None]
    return (y_prompt, y_sample, conv_p, ssm_p, k_p, v_p, conv_s, ssm_s, k_s, v_s)
```

```python
import numpy as np
from contextlib import ExitStack
import concourse.bass as bass
import concourse.mybir as mybir
from concourse.bass_utils import run_bass_kernel_spmd

F32 = mybir.dt.float32
BF16 = mybir.dt.bfloat16
AF = mybir.ActivationFunctionType
ALU = mybir.AluOpType
AX = mybir.AxisListType

D = 1024
NQ = 8
NKV = 2
HD = 64
NH = 8
HP = 64
NST = 128
DFF = 4096
PAST = 16384
ALPHA = 2.0 ** 0.25
LN_EPS = 1e-5
RMS_EPS = 1e-5
NEG = -30000.0
C_Z, C_XBC, C_Q, C_K, C_V, C_DT, C_END = 0, 512, 1536, 2048, 2176, 2304, 2312


class Buf:
    __slots__ = ("name", "w", "r", "multi")

    def __init__(self, name, multi=False):
        self.name = name
        self.w = {}
        self.r = {}
        self.multi = multi


class K:
    def __init__(self, nc, es):
        self.nc = nc
        self.es = es
        self.eng = {"pe": nc.tensor, "dve": nc.vector, "act": nc.scalar, "pool": nc.gpsimd, "sp": nc.sync}
        self.sem = {}
        self.cnt = {}
        self.waited = {k: {} for k in self.eng}
        self.semobj = {}
        for k in ("pe", "dve", "act", "pool"):
            s = es.enter_context(nc.semaphore("sem_" + k))
            self.sem[k] = s
            self.semobj[id(s)] = s
            self.cnt[k] = 0
        self.dsems = []
        self._sites = {}
        self.ninstr = 0

    def new_dsem(self, name):
        s = self.es.enter_context(self.nc.semaphore("d_" + name))
        self.semobj[id(s)] = s
        h = [s, 0]
        self.dsems.append(h)
        return h

    def _need(self, ek, R, W):
        need = {}
        for b in R:
            for s, v in b.w.items():
                if need.get(s, 0) < v:
                    need[s] = v
        for b in W:
            if b.multi:
                continue
            for dd in (b.w, b.r):
                for s, v in dd.items():
                    if need.get(s, 0) < v:
                        need[s] = v
        return need

    def _waits(self, ek, need):
        e = self.eng[ek]
        own = id(self.sem[ek]) if ek in self.sem else None
        wd = self.waited[ek]
        for s, v in need.items():
            if s == own:
                if ek == "pe":
                    continue
                if v < self.cnt[ek] - 8:
                    continue
            if wd.get(s, 0) >= v:
                continue
            e.wait_ge(self.semobj[s], v)
            self.ninstr += 1
            wd[s] = v

    def _record(self, R, W, s, v):
        for b in W:
            if b.multi:
                if b.w.get(s, 0) < v:
                    b.w[s] = v
            else:
                b.w = {s: v}
                b.r = {}
        for b in R:
            if b in W:
                continue
            if b.r.get(s, 0) < v:
                b.r[s] = v

    def op(self, ek, fn, R=(), W=()):
        need = self._need(ek, R, W)
        self._waits(ek, need)
        ins = fn(self.eng[ek])
        self.cnt[ek] += 1
        ins.then_inc(self.sem[ek], 1)
        self.ninstr += 1
        self._record(R, W, id(self.sem[ek]), self.cnt[ek])

    def dma(self, qk, outs_ins, R, W, dsem):
        need = self._need(qk, R, W)
        self._waits(qk, need)
        for (o, i) in outs_ins:
            self.eng[qk].dma_start(out=o, in_=i, allow_slow_non_contiguous=True).then_inc(dsem[0], 16)
            dsem[1] += 16
            self.ninstr += 1
        self._record(R, W, id(dsem[0]), dsem[1])

    def site(self, name):
        if name not in self._sites:
            self._sites[name] = self.new_dsem(name)
        return self._sites[name]

    def finish(self, ek="sp"):
        for h in self.dsems:
            if h[1] > 0:
                self.eng[ek].wait_ge(h[0], h[1])
        for k2 in ("pe", "dve", "act", "pool"):
            if self.cnt[k2] > 0:
                self.eng[ek].wait_ge(self.sem[k2], self.cnt[k2])


def _consts(T, nseq):
    tl = T // nseq
    idx = np.arange(T)
    seq = idx // tl
    same = seq[:, None] == seq[None, :]
    U = (same & (idx[:, None] <= idx[None, :])).astype(np.float32)
    SL = (same & (idx[:, None] > idx[None, :])).astype(np.float32)
    BD = same.astype(np.float32)
    return U, SL, BD


def build(LH, NTS=64, NSEQ=16, with_sample=True, NSUB=2):
    import os
    with_sample = with_sample and os.environ.get('NO_SAMPLE') is None
    DBG = os.environ.get('DBG', '')
    nc = bass.Bass("TRN2", target_bir_lowering=False)
    NT = LH // 128
    assert NT % NSUB == 0

    def din(name, shape, dt=F32):
        return nc.dram_tensor(name, list(shape), dt, kind="ExternalInput").ap()

    def dout(name, shape, dt=F32):
        return nc.dram_tensor(name, list(shape), dt, kind="ExternalOutput").ap()

    def dscr(name, shape, dt):
        return nc.dram_tensor(name, list(shape), dt, kind="Internal").ap()
    assert NTS == 64 and NSEQ == 16

    x_prev = din("x_prev", [LH, D])
    x_main = din("x_main", [LH, D])
    c_p = din("c_p", [1, D])
    flag = din("flag", [128, 2])
    m0 = din("m0", [128, 256])
    m2 = din("m2", [128, 256])
    rope_main = din("rope_main", [LH, 16])
    rope_prev = din("rope_prev", [128, 16])
    cmask = din("cmask", [128, 5, 128])
    ln_in_g = din("ln_in_g", [1, D]); ln_in_b = din("ln_in_b", [1, D])
    w_ada = din("w_ada", [D, 6 * D]); b_ada = din("b_ada", [1, 6 * D])
    w_in = din("w_in", [D, C_END])
    conv_w = din("conv_w", [4, D]); conv_b = din("conv_b", [1, D])
    dt_bias = din("dt_bias", [1, NH]); A_log = din("A_log", [1, NH]); D_skip = din("D_skip", [1, NH])
    ssm_norm_w = din("ssm_norm_w", [1, 512])
    sinks_p = din("sinks_p", [1, NQ])
    w_out = din("w_out", [D, D])
    ln1_g = din("ln1_g", [1, D]); ln1_b = din("ln1_b", [1, D])
    w_up = din("w_up", [D, DFF]); w_down = din("w_down", [DFF, D])
    ln2_g = din("ln2_g", [1, D]); ln2_b = din("ln2_b", [1, D])

    x_s = din("x_s", [NTS, D]); c_s = din("c_s", [NTS, D]); rope_s = din("rope_s", [NTS, 16])
    sconv = din("sconv", [NSEQ * 3, D]); sssm = din("sssm", [NSEQ, 512, 128])
    ck = din("ck", [NSEQ, 128, 128]); cv = din("cv", [NSEQ, 128, 128])
    cmask_s = din("cmask_s", [128, 3, 128]); smk_d = din("smk", [128, NSEQ]); sink16_d = din("sink16", [16, 2])
    dmask = din("dmask", [16, 128 + NSEQ * 64])
    y_s = dout("y_s", [NTS, D]); conv_s = dout("conv_s", [NSEQ, 3, D]); ssm_s = dout("ssm_s", [NSEQ, 512, 128])
    k_s = dout("k_s", [NSEQ, 128, 128]); v_s = dout("v_s", [NSEQ, 128, 128])
    osc = dscr("osc", [16, NSEQ * 128], BF16)
    y_main = dout("y_main", [LH, D])
    conv_p = dout("conv_p", [3, D])
    ssm_p = dout("ssm_p", [512, 128])
    k_p = dout("k_p", [128, 128])
    v_p = dout("v_p", [128, 128])

    wup_s = dscr("wup_s", [8, 128, 4096], BF16)
    wdn_s = dscr("wdn_s", [2, 4, 128, 4096], BF16)
    wout_s = dscr("wout_s", [4, 128, 2048], BF16)

    es = ExitStack()
    with es:
        k = K(nc, es)

        def sb(name, shape, dt=F32):
            return es.enter_context(nc.sbuf_tensor(name, list(shape), dt))

        w_in_b = sb("w_in_b", [128, 8, C_END], BF16); B_win = Buf("w_in_b", multi=True)
        B_wout_s = Buf("wout_s", multi=True)
        NSLOT = 6
        wslot = [sb(f"wslot{i}", [128, 1024], BF16) for i in range(NSLOT)]
        B_wslot = [Buf(f"wslot{i}") for i in range(NSLOT)]
        D_wslot = [k.new_dsem(f"wslot{i}") for i in range(NSLOT)]
        NOS = 4
        oslot = [sb(f"oslot{i}", [128, 512], BF16) for i in range(NOS)]
        B_oslot = [Buf(f"oslot{i}") for i in range(NOS)]
        D_oslot = [k.new_dsem(f"oslot{i}") for i in range(NOS)]
        oq = {"n": 0}

        def oload(c_):
            kk, n = c_ // 2, c_ % 2
            i = oq["n"] % NOS
            oq["n"] += 1
            k.dma("sp", [(oslot[i][:, :], wout_s[kk // 2, :, (kk % 2) * 1024 + n * 512:(kk % 2) * 1024 + (n + 1) * 512])], R=[B_wout_s], W=[B_oslot[i]], dsem=D_oslot[i])
            return i
        D_stgf = [k.new_dsem(f"stgf{i}") for i in range(2)]
        B_stgb = [Buf(f"stgb{i}") for i in range(2)]
        D_stgb = [k.new_dsem(f"stgb{i}") for i in range(2)]
        B_wup_s = Buf("wup_s", multi=True); B_wdn_s = Buf("wdn_s", multi=True)

        ada = sb("ada", [128, 6, D], F32); B_ada = Buf("ada")
        gb = sb("gb", [128, 6, D], F32); B_gb = Buf("gb")
        normw = sb("normw", [128, 512], F32)
        smallc = sb("smallc", [128, 5, 8], F32)
        B_small = Buf("smallc")
        cw = sb("cw", [128, 8, 5], F32); B_cw = Buf("cw")
        cm = sb("cm", [128, 5, 128], F32); B_cm = Buf("cm")
        identb = sb("identb", [128, 128], BF16)
        am = sb("am", [128, 2, 256], F32); B_am = Buf("am")
        flg = sb("flg", [128, 2], F32)
        D_c = k.new_dsem("consts")

        x_in = sb("x_in", [128, D], F32); B_xin = Buf("x_in"); D_xin = k.new_dsem("x_in")
        rope = sb("rope", [128, 16], F32); B_rope = Buf("rope"); D_rope = k.new_dsem("rope")
        stats0 = sb("stats", [128, 2, 6], F32); B_stats0 = Buf("stats")
        mv0 = sb("mv", [128, 8], F32); B_mv0 = Buf("mv")
        xpb = sb("xpb", [128, 2, D], F32); B_xpb = [Buf("xp0"), Buf("xp1")]
        xp = xpb[:, 0, :]; B_xp = B_xpb[0]
        hbP = sb("hbP", [128, D], BF16); B_hbP = Buf("hbP")
        hTP = sb("hTP", [128, 8, 128], BF16); B_hTP = Buf("hTP")
        hT = sb("hT", [128, 8, 128], BF16); B_hT = Buf("hT")
        cT = hT; B_cT = B_hT
        xbcT = sb("xbcT", [128, 8, 131], F32); B_xbcT = Buf("xbcT")
        xacc = sb("xacc", [128, 8, 128], F32); B_xacc = Buf("xacc")
        xcT = xacc[:, 0:4, :]; B_xcT = B_xacc
        B_xc = [Buf(f"xc{i}") for i in range(9)]
        bcT = sb("bcT", [128, 4, 128], BF16); B_bcT = Buf("bcT")
        xtok = sb("xtok", [128, 512], F32); B_xtok = Buf("xtok")
        btok = sb("btok", [128, 256], BF16); B_btok = Buf("btok")
        sm8 = sb("sm8", [128, 8, 8], F32); B_sm8 = Buf("sm8")
        xdt = sb("xdt", [128, 512], BF16); B_xdt = Buf("xdt")
        xdec = sb("xdec", [128, 512], BF16); B_xdec = Buf("xdec")
        Hst = sb("Hst", [128, 512], F32); B_H = Buf("H")
        Hbf = sb("Hbf", [128, 512], BF16); B_Hb = Buf("Hb")
        big1 = sb("big1", [128, 8, 128], F32); B_big1 = Buf("big1")
        big2 = big1; B_big2 = B_big1
        Gb = sb("Gb", [128, 8, 128], BF16); B_G = Buf("G")
        cbm = sb("cbm", [128, 2, 128], F32); B_cbm = Buf("cbm")
        ysb = sb("ysb", [128, 512], F32); B_y = Buf("y")
        qs = sb("qs", [128, 512], F32); B_qs = Buf("qs")
        kvs = sb("kvs", [128, 256], F32); B_kvs = Buf("kvs")
        rt = sb("rt", [128, 4, 64], F32); B_rt = Buf("rt")
        qb = sb("qb", [128, 640], BF16); B_qb = Buf("qb")
        qT = sb("qT", [128, 4, 128], BF16); B_qT = Buf("qT")
        kTb = [sb(f"kTb{i}", [128, 128], BF16) for i in range(2)]; B_kT = [Buf(f"kT{i}") for i in range(2)]
        vb = [sb(f"vb{i}", [128, 128], BF16) for i in range(2)]; B_vb = [Buf(f"vb{i}") for i in range(2)]
        PT = sb("PT", [128, 8, 128], BF16); B_PT = Buf("PT")
        att8 = sb("att8", [128, 8, 8], F32); B_att8 = Buf("att8")
        rs8 = sb("rs8", [128, 8], F32); B_rs8 = Buf("rs8")
        ymix = sb("ymix", [128, D], BF16); B_ymix = Buf("ymix")
        hb = ymix; B_hb = B_ymix
        ymixT = hT; B_ymixT = B_hT
        tmpf = sb("tmpf", [128, D], F32); B_tmpf = Buf("tmpf")
        zs = sb("zs", [128, 512], F32); B_zs = Buf("zs")
        TT = NSUB * 128
        NXB = 1
        X1 = sb("X1", [128, NXB, NSUB, D], F32); B_X1 = [[Buf(f"X1_{b}_{i}") for i in range(NSUB)] for b in range(NXB)]
        h2T = sb("h2T", [128, NXB, 8, TT], BF16); B_h2T = [[Buf(f"h2T_{b}_{i}") for i in range(NSUB)] for b in range(NXB)]
        smvb = sb("smvb", [128, 4, 256], F32); B_smvb = Buf("smvb")
        Pb = sb("Pb", [128, 4, 256], BF16); B_Pb = Buf("Pb")
        uT = sb("uT", [128, 32, TT], BF16); B_uT = [Buf(f"uT_{i}") for i in range(32)]
        urelu = [sb(f"urelu{i}", [128, TT], BF16) for i in range(2)]; B_urelu = [Buf(f"urelu{i}") for i in range(2)]
        D_yo = k.new_dsem("yo")
        stg_f = [tmpf, X1[:, NXB - 1, NSUB - 1, :]]
        B_stgf = [B_tmpf, B_X1[NXB - 1][NSUB - 1]]
        stg_b = [uT[:, 0:16, :].rearrange("p a b -> p (a b)"), uT[:, 16:32, :].rearrange("p a b -> p (a b)")]
        assert TT == 256

        def W_stgb(j):
            return [B_stgb[j]] + B_uT[16 * j:16 * j + 16]
        uflat = uT[:, :, :].rearrange("p a b -> p (a b)")

        def scr(lo, hi, dt=None):
            v = uflat[:, lo:hi]
            return v.bitcast(dt) if dt is not None else v
        hn = [scr(0, 1024, F32), scr(1024, 2048, F32)]; B_hn = [Buf("hn0"), Buf("hn1")]; D_hn = [k.new_dsem("hn0"), k.new_dsem("hn1")]
        hnb = scr(2048, 2560); B_hnb = Buf("hnb")
        h0T = scr(2560, 3072); B_h0T = Buf("h0T")
        xdm = scr(3072, 3584); B_xdm = Buf("xdm")
        ckf = scr(3584, 3840, F32); cvf = scr(3840, 4096, F32); B_ckf = Buf("ckf"); B_cvf = Buf("cvf"); D_ckv = k.new_dsem("ckv")
        ckb = scr(4096, 4224); cvb = scr(4224, 4352); kcT = scr(4352, 4480); B_ckb = Buf("ckb"); B_cvb = Buf("cvb"); B_kcT = Buf("kcT")
        dmk = scr(4480, 6784, F32); B_dmk = Buf("dmk")
        cms = scr(6784, 7552, F32).rearrange("p (a b) -> p a b", a=3); B_cms = Buf("cms")
        yoT = scr(7552, 8064, F32); B_yoT = Buf("yoT")
        dcol = scr(8064, 8192, F32).rearrange("p (a b) -> p a b", a=4); B_dcol = Buf("dcol")
        smk = sb("smk_sb", [128, NSEQ], F32); sink16 = sb("sink16_sb", [16, 2], F32); B_smk = Buf("smk")
        fdummy = sb("fdummy", [128, 2], F32)
        qTs = sb("qTs", [64, 2, NSEQ, 16], BF16); B_qTs = Buf("qTs")
        knT1 = sb("knT1", [64, 64], BF16)
        kcT2 = sb("kcT2", [64, 2, 128], BF16)
        SCR_BUFS = B_hn + [B_hnb, B_h0T, B_xdm, B_ckf, B_cvf, B_ckb, B_cvb, B_kcT, B_dmk, B_cms, B_yoT, B_dcol]
        osb = X1[:, NXB - 1, 0, :].bitcast(BF16)

        def fence():
            allb = SCR_BUFS + B_stgb + B_uT
            k.op("dve", lambda e: e.memset(fdummy[:, 0:1], 0.0), R=[], W=allb)
        D_misc = k.new_dsem("misc")
        B_dummy = Buf("dram_out", multi=True)

        ps = [es.enter_context(nc.psum_tensor(f"ps{i}", [128, 512], F32)) for i in range(8)]
        B_ps = [Buf(f"ps{i}") for i in range(8)]
        B_ps7h = [Buf("ps7a"), Buf("ps7b")]

        def psb(i, dt=BF16):
            return ps[i][:, :].bitcast(dt)

        def bc_row(ap_row, n):
            return ap_row.broadcast_to([128, n])

        k.dma("sp", [(gb[:, 0, :], bc_row(ln_in_g, D)), (gb[:, 1, :], bc_row(ln_in_b, D)),
                     (gb[:, 2, :], bc_row(ln1_g, D)), (gb[:, 3, :], bc_row(ln1_b, D)),
                     (gb[:, 4, :], bc_row(ln2_g, D)), (gb[:, 5, :], bc_row(ln2_b, D)),
                     (normw[:, :], bc_row(ssm_norm_w, 512)),
                     (smallc[:, 0, :], bc_row(dt_bias, NH)), (smallc[:, 1, :], bc_row(A_log, NH)),
                     (smallc[:, 2, :], bc_row(D_skip, NH)), (smallc[:, 3, :], bc_row(sinks_p, NQ)),
                     (cm[:, :, :], cmask), (am[:, 0, :], m0), (am[:, 1, :], m2), (flg[:, :], flag)],
              R=[], W=[B_gb, B_small, B_cm, B_am], dsem=D_c)
        k.dma("sp", [(cw[:, :, j], conv_w[j, :].rearrange("(c p) -> p c", p=128)) for j in range(4)]
              + [(cw[:, :, 4], conv_b[0, :].rearrange("(c p) -> p c", p=128))],
              R=[], W=[B_cw], dsem=k.site("cw"))
        k.op("act", lambda e: e.activation(out=smallc[:, 1, :], in_=smallc[:, 1, :], func=AF.Exp), R=[B_small], W=[B_small])
        k.op("dve", lambda e: e.tensor_scalar(out=smallc[:, 1, :], in0=smallc[:, 1, :], scalar1=-1.0, scalar2=None, op0=ALU.mult), R=[B_small], W=[B_small])
        k.op("dve", lambda e: e.tensor_copy(out=identb[:, :], in_=cm[:, 3, :]), R=[B_cm], W=[B_cm])
        identf = cm[:, 3, :]

        cast_engs = ["dve", "pool", "act"]
        state = {"stg": 0, "ce": 0}

        def cast(out_ap, in_ap, R, W):
            ek = cast_engs[state["ce"] % 3]
            state["ce"] += 1
            if ek == "act":
                k.op("act", lambda e: e.copy(out=out_ap, in_=in_ap), R=R, W=W)
            else:
                k.op(ek, lambda e: e.tensor_copy(out=out_ap, in_=in_ap), R=R, W=W)

        def load_stage(dram_ap_3d, shape3):
            i = state["stg"] % 2
            state["stg"] += 1
            view = stg_f[i][:, 0:shape3[0] * shape3[1]].rearrange("p (a b) -> p a b", a=shape3[0])
            k.dma("sp", [(view, dram_ap_3d)], R=[], W=[B_stgf[i]], dsem=D_stgf[i])
            return i, view

        D_wprep = k.new_dsem("wprep")
        k.dma("pool", [(w_in_b[:, kk, :], w_in[kk * 128:(kk + 1) * 128, :]) for kk in range(8)], R=[], W=[B_win], dsem=k.new_dsem("wp_in"))
        def late_weight_prep():
            k.dma("pool", [(wout_s[q_, :, :].rearrange("p (a n) -> p a n", a=2), w_out[q_ * 256:(q_ + 1) * 256, :].rearrange("(a p) n -> p a n", p=128)) for q_ in range(4)],
                  R=[], W=[B_wout_s], dsem=k.new_dsem("wp_out"))
            k.dma("pool", [(wup_s[g, :, fc * 1024:(fc + 1) * 1024].rearrange("p (kk f) -> p kk f", kk=8), w_up[:, (g * 4 + fc) * 128:(g * 4 + fc + 1) * 128].rearrange("(kk p) f -> p kk f", p=128))
                           for g in range(8) for fc in range(4)], R=[], W=[B_wup_s], dsem=k.new_dsem("wp_up"))
            k.dma("pool", [(wdn_s[n, g, :, :].rearrange("p (fk d) -> p fk d", fk=8), w_down[g * 1024:(g + 1) * 1024, n * 512:(n + 1) * 512].rearrange("(fk p) d -> p fk d", p=128))
                           for n in range(2) for g in range(4)], R=[], W=[B_wdn_s], dsem=k.new_dsem("wp_dn"))

        sbi = 0

        def compute_ada(c_rows_ap, T):
            k.dma("sp", [(xp[:T, :], c_rows_ap)], R=[], W=[B_xp], dsem=k.site("ada_c"))
            k.op("act", lambda e: e.activation(out=hb[:T, :], in_=xp[:T, :], func=AF.Silu), R=[B_xp], W=[B_hb])
            for kk in range(8):
                k.op("pe", lambda e, kk=kk: e.transpose(out=psb(0)[:, kk * 128:kk * 128 + T], in_=hb[:T, kk * 128:(kk + 1) * 128], identity=identb[:T, :T]),
                     R=[B_hb, B_cm], W=[B_ps[0]])
            k.op("act", lambda e: e.copy(out=cT[:, :, :T], in_=psb(0)[:, 0:1024].rearrange("p (a b) -> p a b", a=8)[:, :, :T]), R=[B_ps[0]], W=[B_cT])
            k.dma("sp", [(ada[:T, :, :].rearrange("p a b -> p (a b)"), b_ada.broadcast_to([T, 6 * D]))], R=[], W=[B_ada], dsem=k.site("ada_b"))
            for nn in range(12):
                pb = 1 + (nn % 2)
                j = sbi_box[0] % 2
                sbi_box[0] += 1
                wv = stg_b[j].rearrange("p (a n) -> p a n", a=8)
                k.dma("pool", [(wv, w_ada[:, nn * 512:(nn + 1) * 512].rearrange("(a p) n -> p a n", p=128))], R=[], W=W_stgb(j), dsem=D_stgb[j])
                for kk in range(8):
                    k.op("pe", lambda e, kk=kk, wv=wv, pb=pb: e.matmul(out=ps[pb][:T, :], lhsT=cT[:, kk, :T], rhs=wv[:, kk, :], start=(kk == 0), stop=(kk == 7)),
                         R=[B_cT, B_stgb[j]], W=[B_ps[pb]])
                av = ada[:T, :, :].rearrange("p a b -> p (a b)")[:, nn * 512:(nn + 1) * 512]
                k.op("dve", lambda e, av=av, pb=pb: e.tensor_tensor(out=av, in0=ps[pb][:T, :], in1=av, op=ALU.add), R=[B_ps[pb], B_ada], W=[B_ada])
            for s_, gi_ in ((1, 0), (4, 2)):
                k.op("pool", lambda e, s_=s_: e.tensor_scalar(out=ada[:T, s_, :], in0=ada[:T, s_, :], scalar1=1.0, scalar2=None, op0=ALU.add), R=[B_ada], W=[B_ada])
                k.op("dve", lambda e, s_=s_, gi_=gi_: e.tensor_tensor(out=tmpf[:T, :], in0=ada[:T, s_, :], in1=gb[:T, gi_ + 1, :], op=ALU.mult), R=[B_ada, B_gb], W=[B_tmpf])
                k.op("dve", lambda e, s_=s_: e.tensor_tensor(out=ada[:T, s_ - 1, :], in0=ada[:T, s_ - 1, :], in1=tmpf[:T, :], op=ALU.add), R=[B_ada, B_tmpf], W=[B_ada])
                k.op("pool", lambda e, s_=s_, gi_=gi_: e.tensor_tensor(out=ada[:T, s_, :], in0=ada[:T, s_, :], in1=gb[:T, gi_, :], op=ALU.mult), R=[B_ada, B_gb], W=[B_ada])

        sbi_box = [sbi]

        statsP = sb("statsP", [128, 2, 6], F32); B_statsP = Buf("statsP")
        mvP = sb("mvP", [128, 8], F32); B_mvP = Buf("mvP")

        def layer_norm_rows(src, B_src, T, gi, dst, B_dst, scr=None):
            if scr is None:
                stats, B_stats, mv, B_mv = stats0, B_stats0, mv0, B_mv0
            else:
                stats, B_stats, mv, B_mv = scr
            for c in range(2):
                k.op("dve", lambda e, c=c: e.bn_stats(out=stats[:T, c, :], in_=src[:T, c * 512:(c + 1) * 512]), R=[B_src], W=[B_stats])
            k.op("dve", lambda e: e.bn_aggr(out=mv[:T, 0:2], in_=stats[:T, :, :].rearrange('p a b -> p (a b)')), R=[B_stats], W=[B_mv])
            k.op("dve", lambda e: e.tensor_scalar(out=mv[:T, 2:3], in0=mv[:T, 1:2], scalar1=LN_EPS, scalar2=None, op0=ALU.add), R=[B_mv], W=[B_mv])
            k.op("act", lambda e: e.activation(out=mv[:T, 2:3], in_=mv[:T, 2:3], func=AF.Ln), R=[B_mv], W=[B_mv])
            k.op("act", lambda e: e.activation(out=mv[:T, 2:3], in_=mv[:T, 2:3], func=AF.Exp, scale=-0.5), R=[B_mv], W=[B_mv])
            k.op("dve", lambda e: e.tensor_scalar(out=mv[:T, 3:4], in0=mv[:T, 0:1], scalar1=mv[:T, 2:3], scalar2=-1.0, op0=ALU.mult, op1=ALU.mult), R=[B_mv], W=[B_mv])
            k.op("act", lambda e: e.activation(out=dst[:T, :], in_=src[:T, :], func=AF.Identity, bias=mv[:T, 3:4], scale=mv[:T, 2:3]), R=[B_src, B_mv], W=[B_dst])
            if gi is not None:
                affine(dst, B_dst, T, gi, dst, B_dst, ("dve", "pool"))

        def affine(src, B_src, T, gi, dst, B_dst, engs):
            k.op(engs[0], lambda e: e.tensor_tensor(out=dst[:T, :], in0=src[:T, :], in1=gb[:T, gi, :], op=ALU.mult), R=[B_src, B_gb], W=[B_dst])
            k.op(engs[1], lambda e: e.tensor_tensor(out=dst[:T, :], in0=dst[:T, :], in1=gb[:T, gi + 1, :], op=ALU.add), R=[B_dst, B_gb], W=[B_dst])

        def modulate_T(src, B_src, T, si, dstT, B_dstT, col0, hb=None, B_hb=None, tmp=None, B_tmp=None, pbk=0):
            if hb is None:
                hb, B_hb = ymix, B_ymix
            if tmp is None:
                tmp, B_tmp = tmpf, B_tmpf
            k.op("dve", lambda e: e.tensor_tensor(out=tmp[:T, :], in0=src[:T, :], in1=ada[:T, si, :], op=ALU.mult), R=[B_src, B_ada], W=[B_tmp])
            k.op("dve", lambda e: e.tensor_tensor(out=hb[:T, :], in0=tmp[:T, :], in1=ada[:T, si - 1, :], op=ALU.add), R=[B_tmp, B_ada], W=[B_hb])
            for kk in range(8):
                k.op("pe", lambda e, kk=kk: e.transpose(out=psb(pbk)[:, kk * 128:kk * 128 + T], in_=hb[:T, kk * 128:(kk + 1) * 128], identity=identb[:T, :T]),
                     R=[B_hb, B_cm], W=[B_ps[pbk]])
            k.op("act", lambda e: e.copy(out=dstT[:, :, col0:col0 + T], in_=psb(pbk)[:, 0:1024].rearrange("p (a b) -> p a b", a=8)[:, :, :T]), R=[B_ps[pbk]], W=[B_dstT])

        def bc8(ap_t8, T, n):
            return ap_t8.unsqueeze(2).broadcast_to([T, 8, n])

        def mixer_tile(x_rows, rope_rows, T, par, mode, mask_ap, sub, last_conv_out=False, smp=False, xb=0):
            full = mode == "full"
            xb = xb % NXB
            X1v = X1[:, xb]; B_X1v = B_X1[xb]; h2Tv = h2T[:, xb]; B_h2Tv = B_h2T[xb]
            if smp:
                U_ap = cms[:, 0, :]; SL_ap = cms[:, 1, :]; BD_ap = cms[:, 2, :]; CAU_ap = cms[:, 0, :]; B_cmx = B_cms
            else:
                U_ap = cm[:, 0, :]; SL_ap = cm[:, 1, :]; BD_ap = cm[:, 2, :]; CAU_ap = cm[:, 4, :]; B_cmx = B_cm
            xbcS = xbcT[:, :, :].rearrange("p a b -> p (a b)")[:, 0:896].rearrange("p (c s u) -> p c s u", c=8, s=16)
            kv = mode in ("full", "statekv")
            k.dma("sp", [(x_in[:T, :], x_rows)], R=[], W=[B_xin], dsem=D_xin)
            if kv:
                k.dma("sp", [(rope[:T, :], rope_rows)], R=[], W=[B_rope], dsem=D_rope)
            xp = xpb[:, par % 2, :]; B_xp = B_xpb[par % 2]
            hT = hTP; B_hT = B_hTP
            layer_norm_rows(x_in, B_xin, T, None, x_in, B_xin, scr=(statsP, B_statsP, mvP, B_mvP))
            modulate_T(x_in, B_xin, T, 1, hT, B_hT, 0, hbP, B_hbP, xp, B_xp, 5)
            if full:
                affine(x_in, B_xin, T, 0, xp, B_xp, ("pool", "pool"))
            yield
            nch = 6 if mode == 'state' else 8
            for c in range(nch):
                pb = 6 + c // 4
                for kk in range(8):
                    k.op("pe", lambda e, c=c, kk=kk, pb=pb: e.matmul(out=ps[pb][:, (c % 4) * 128:(c % 4) * 128 + T], lhsT=w_in_b[:, kk, C_XBC + c * 128:C_XBC + (c + 1) * 128],
                                                                   rhs=hT[:, kk, :T], start=(kk == 0), stop=(kk == 7)), R=[B_win, B_hT], W=[B_ps[pb]])
            n2 = nch - 4
            if not smp:
                k.op("act", lambda e: e.copy(out=xbcT[:, 0:4, 3:3 + T], in_=ps[6][:, :].rearrange("p (a b) -> p a b", a=4)[:, :, :T]), R=[B_ps[6]], W=[B_xbcT])
                k.op("dve", lambda e: e.tensor_copy(out=xbcT[:, 4:4 + n2, 3:3 + T], in_=ps[7][:, :].rearrange("p (a b) -> p a b", a=4)[:, 0:n2, :T]), R=[B_ps[7]], W=[B_xbcT])
            else:
                for hh in range(2):
                    k.op("act" if hh == 0 else "dve", lambda e, hh=hh: (e.copy if hh == 0 else e.tensor_copy)(out=xbcS[:, hh * 4:(hh + 1) * 4, :, 3:7],
                         in_=ps[6 + hh][:, :].rearrange("p (a b) -> p a b", a=4)[:, :, :64].rearrange("p a (s t) -> p a s t", t=4)), R=[B_ps[6 + hh]], W=[B_xbcT])
                k.dma("sp", [(x_in[:48, :], sconv)], R=[], W=[B_xin], dsem=D_xin)
                for c in range(8):
                    k.op("pe", lambda e, c=c: e.transpose(out=ps[4][:, c * 48:(c + 1) * 48], in_=x_in[:48, c * 128:(c + 1) * 128], identity=identf[:48, :48]), R=[B_xin, B_cm], W=[B_ps[4]])
                k.op("act", lambda e: e.copy(out=xbcS[:, :, :, 0:3], in_=ps[4][:, 0:384].rearrange("p (c s u) -> p c s u", c=8, s=16)), R=[B_ps[4]], W=[B_xbcT])
                for n in range(2):
                    for kk in range(8):
                        k.op("pe", lambda e, n=n, kk=kk: e.matmul(out=ps[1 + n][:T, :], lhsT=hT[:, kk, :T], rhs=w_in_b[:, kk, C_XBC + n * 512:C_XBC + (n + 1) * 512], start=(kk == 0), stop=(kk == 7)),
                             R=[B_win, B_hT], W=[B_ps[1 + n]])
                k.op("act", lambda e: e.copy(out=tmpf[:T, 0:512], in_=ps[1][:T, :]), R=[B_ps[1]], W=[B_tmpf])
                k.op("dve", lambda e: e.tensor_copy(out=tmpf[:T, 512:1024], in_=ps[2][:T, :]), R=[B_ps[2]], W=[B_tmpf])
                k.dma("sp", [(conv_s[:, t_ - 1, :], tmpf[t_:64:4, :]) for t_ in range(1, 4)], R=[B_tmpf], W=[B_dummy], dsem=k.site("conv_s"))
            c0 = C_K if kv else C_DT
            ncol = C_END - c0
            for kk in range(8):
                k.op("pe", lambda e, kk=kk: e.matmul(out=ps[3][:T, 0:ncol], lhsT=hT[:, kk, :T], rhs=w_in_b[:, kk, c0:C_END], start=(kk == 0), stop=(kk == 7)),
                     R=[B_win, B_hT], W=[B_ps[3]])
            dtcol = C_DT - c0
            if full:
                for kk in range(8):
                    k.op("pe", lambda e, kk=kk: e.matmul(out=ps[4][:T, :], lhsT=hT[:, kk, :T], rhs=w_in_b[:, kk, C_Z:C_Z + 512], start=(kk == 0), stop=(kk == 7)),
                         R=[B_win, B_hT], W=[B_ps[4]])
                for kk in range(8):
                    k.op("pe", lambda e, kk=kk: e.matmul(out=ps[7][:T, :], lhsT=hT[:, kk, :T], rhs=w_in_b[:, kk, C_Q:C_Q + 512], start=(kk == 0), stop=(kk == 7)),
                         R=[B_win, B_hT], W=[B_ps[7]])
                k.op("act", lambda e: e.activation(out=zs[:T, :], in_=ps[4][:T, :], func=AF.Silu), R=[B_ps[4]], W=[B_zs])
                k.op("act", lambda e: e.activation(out=qs[:T, :], in_=ps[7][:T, :], func=AF.Copy, scale=0.125), R=[B_ps[7]], W=[B_qs])
            yield "P2"
            DT, A_, ACS, EACS, DEC, EATOT, DTDEC, TMP = [sm8[:, i, :] for i in range(8)]
            k.op("dve", lambda e: e.tensor_tensor(out=DT[:T], in0=ps[3][:T, dtcol:dtcol + 8], in1=smallc[:T, 0, :], op=ALU.add), R=[B_ps[3], B_small], W=[B_sm8])
            k.op("act", lambda e: e.activation(out=DT[:T], in_=DT[:T], func=AF.Exp), R=[B_sm8], W=[B_sm8])
            k.op("act", lambda e: e.activation(out=DT[:T], in_=DT[:T], func=AF.Ln, bias=1.0), R=[B_sm8], W=[B_sm8])
            k.op("dve", lambda e: e.tensor_tensor(out=A_[:T], in0=DT[:T], in1=smallc[:T, 1, :], op=ALU.mult), R=[B_sm8, B_small], W=[B_sm8])
            if kv:
                k.op("act", lambda e: e.copy(out=kvs[:T, :], in_=ps[3][:T, 0:256]), R=[B_ps[3]], W=[B_kvs])
            k.op("pe", lambda e: e.matmul(out=ps[4][:T, 0:8], lhsT=U_ap[:T, :T], rhs=A_[:T], start=True, stop=True), R=[B_cmx, B_sm8], W=[B_ps[4]])
            NB = T if smp else 128
            k.op("pe", lambda e: e.matmul(out=ps[4][:NB, 8:16], lhsT=BD_ap[:T, :NB], rhs=A_[:T], start=True, stop=True), R=[B_cmx, B_sm8], W=[B_ps[4]])
            k.op("dve", lambda e: e.tensor_copy(out=ACS[:T], in_=ps[4][:T, 0:8]), R=[B_ps[4]], W=[B_sm8])
            k.op("act", lambda e: e.activation(out=EACS[:T], in_=ps[4][:T, 0:8], func=AF.Exp), R=[B_ps[4]], W=[B_sm8])
            k.op("dve", lambda e: e.tensor_tensor(out=TMP[:T], in0=ps[4][:T, 8:16], in1=ACS[:T], op=ALU.subtract), R=[B_ps[4], B_sm8], W=[B_sm8])
            k.op("act", lambda e: e.activation(out=DEC[:T], in_=TMP[:T], func=AF.Exp), R=[B_sm8], W=[B_sm8])
            if not smp:
                k.op("act", lambda e: e.activation(out=EATOT, in_=ps[4][:, 8:16], func=AF.Exp), R=[B_ps[4]], W=[B_sm8])
            k.op("dve", lambda e: e.tensor_tensor(out=DTDEC[:T], in0=DT[:T], in1=DEC[:T], op=ALU.mult), R=[B_sm8], W=[B_sm8])
            yield
            if smp:
                def cwb(j):
                    return cw[:, 0:8, j:j + 1].unsqueeze(3).broadcast_to([128, 8, 16, 4])

                def v4(ap3):
                    return ap3.rearrange("p c (s t) -> p c s t", t=4)
                k.op("pool", lambda e: e.tensor_tensor(out=v4(xacc[:, 0:8, :64]), in0=xbcS[:, :, :, 0:4], in1=cwb(0), op=ALU.mult), R=[B_xbcT, B_cw], W=[B_xacc])
                for j in range(1, 4):
                    k.op("pool", lambda e, j=j: e.tensor_tensor(out=v4(big1[:, 0:8, :64]), in0=xbcS[:, :, :, j:j + 4], in1=cwb(j), op=ALU.mult), R=[B_xbcT, B_cw], W=[B_big1])
                    k.op("pool", lambda e: e.tensor_tensor(out=xacc[:, 0:8, :64], in0=xacc[:, 0:8, :64], in1=big1[:, 0:8, :64], op=ALU.add), R=[B_xacc, B_big1], W=[B_xacc])
                k.op("pool", lambda e: e.tensor_tensor(out=v4(xacc[:, 0:8, :64]), in0=v4(xacc[:, 0:8, :64]), in1=cwb(4), op=ALU.add), R=[B_xacc, B_cw], W=[B_xacc])

            if not smp:
                nD = nch - 2
                for b_ in B_xc:
                    b_.w = {}
                    b_.r = {}
                    for dd in (B_xacc.w, B_xacc.r):
                        for s__, v__ in dd.items():
                            if b_.w.get(s__, 0) < v__:
                                b_.w[s__] = v__
                for j in range(4):
                    for c in range(nD):
                        if j == 0:
                            k.op("dve", lambda e, c=c: e.tensor_scalar(out=xacc[:, c, :T], in0=xbcT[:, c, 0:T], scalar1=cw[:, c, 0:1], scalar2=cw[:, c, 4:5], op0=ALU.mult, op1=ALU.add),
                                 R=[B_xbcT, B_cw], W=[B_xc[c]])
                        else:
                            k.op("dve", lambda e, c=c, j=j: e.scalar_tensor_tensor(out=xacc[:, c, :T], in0=xbcT[:, c, j:j + T], scalar=cw[:, c, j:j + 1], in1=xacc[:, c, :T], op0=ALU.mult, op1=ALU.add),
                                 R=[B_xbcT, B_cw], W=[B_xc[c]])

                def cwb(j):
                    return cw[:, nD:nch, j:j + 1].broadcast_to([128, nch - nD, T])
                k.op("pool", lambda e: e.tensor_tensor(out=xacc[:, nD:nch, :T], in0=xbcT[:, nD:nch, 0:T], in1=cwb(0), op=ALU.mult), R=[B_xbcT, B_cw], W=[B_xc[8]])
                for j in range(1, 4):
                    k.op("pool", lambda e, j=j: e.tensor_tensor(out=big1[:, nD:nch, :T], in0=xbcT[:, nD:nch, j:j + T], in1=cwb(j), op=ALU.mult), R=[B_xbcT, B_cw], W=[B_big1])
                    k.op("pool", lambda e: e.tensor_tensor(out=xacc[:, nD:nch, :T], in0=xacc[:, nD:nch, :T], in1=big1[:, nD:nch, :T], op=ALU.add), R=[B_big1], W=[B_xc[8]])
                k.op("pool", lambda e: e.tensor_tensor(out=xacc[:, nD:nch, :T], in0=xacc[:, nD:nch, :T], in1=cwb(4), op=ALU.add), R=[B_cw], W=[B_xc[8]])
            if last_conv_out:
                pass
            if not smp:
                k.op("pool", lambda e: e.tensor_copy(out=xbcT[:, :, 0:3], in_=xbcT[:, :, T:T + 3]), R=[B_xbcT], W=[B_xbcT])
            k.op("act", lambda e: e.activation(out=xcT[:, :, :T], in_=xacc[:, 0:4, :T], func=AF.Silu), R=[B_xacc] + B_xc, W=[B_xcT])
            k.op("act", lambda e: e.activation(out=bcT[:, 0:n2, :T], in_=xacc[:, 4:4 + n2, :T], func=AF.Silu), R=[B_xacc] + B_xc, W=[B_bcT])
            yield
            for c in range(4):
                k.op("pe", lambda e, c=c: e.transpose(out=ps[4][:T, c * 128:(c + 1) * 128], in_=xcT[:, c, :T], identity=identf), R=[B_xcT, B_cm], W=[B_ps[4]])
            k.op("act", lambda e: e.copy(out=xtok[:T, :], in_=ps[4][:T, :]), R=[B_ps[4]], W=[B_xtok])
            for g in range(2):
                k.op("pe", lambda e, g=g: e.transpose(out=psb(5)[:T, g * 128:(g + 1) * 128], in_=bcT[:, g, :T], identity=identb[:, :]), R=[B_bcT, B_cm], W=[B_ps[5]])
            k.op("dve", lambda e: e.tensor_copy(out=btok[:T, :], in_=psb(5)[:T, 0:256]), R=[B_ps[5]], W=[B_btok])
            x3 = xtok[:T, :].rearrange("p (h d) -> p h d", h=8)
            k.op("dve", lambda e: e.tensor_tensor(out=xdec[:T, :].rearrange("p (h d) -> p h d", h=8), in0=x3, in1=bc8(DTDEC[:T], T, 64), op=ALU.mult), R=[B_xtok, B_sm8], W=[B_xdec])
            def chainA():
                if full:
                    k.op("pool", lambda e: e.tensor_tensor(out=xdt[:T, :].rearrange("p (h d) -> p h d", h=8), in0=x3, in1=bc8(DT[:T], T, 64), op=ALU.mult), R=[B_xtok, B_sm8], W=[B_xdt])
                    k.op("dve", lambda e: e.tensor_tensor(out=big1[:T, :, :T], in0=A_[:T].unsqueeze(2).broadcast_to([T, 8, T]),
                                                          in1=SL_ap[:T, :T].unsqueeze(1).broadcast_to([T, 8, T]), op=ALU.mult), R=[B_sm8, B_cmx], W=[B_big1])
                    for h in range(8):
                        pb = 1 + h // 4
                        k.op("pe", lambda e, h=h, pb=pb: e.matmul(out=ps[pb][:T, (h % 4) * 128:(h % 4) * 128 + T], lhsT=big1[:T, h, :T], rhs=U_ap[:T, :T], start=True, stop=True),
                             R=[B_big1, B_cmx], W=[B_ps[pb]])
                    for hh in range(2):
                        k.op("act", lambda e, hh=hh: e.activation(out=big2[:T, hh * 4:(hh + 1) * 4, :T], in_=ps[1 + hh][:T, :].rearrange("p (a b) -> p a b", a=4)[:, :, :T], func=AF.Exp),
                             R=[B_ps[1 + hh]], W=[B_big2])
                    yield
                    for g in range(2):
                        k.op("pe", lambda e, g=g: e.matmul(out=ps[4][:T, g * 128:g * 128 + T], lhsT=bcT[:, g, :T], rhs=bcT[:, 2 + g, :T], start=True, stop=True), R=[B_bcT], W=[B_ps[4]])
                    k.op("dve", lambda e: e.tensor_tensor(out=cbm[:T, :, :T], in0=ps[4][:T, 0:256].rearrange("p (a b) -> p a b", a=2)[:, :, :T],
                                                          in1=CAU_ap[:T, :T].unsqueeze(1).broadcast_to([T, 2, T]), op=ALU.mult), R=[B_ps[4], B_cmx], W=[B_cbm])
                    for g in range(2):
                        k.op("dve" if g == 0 else "pool", lambda e, g=g: e.tensor_tensor(out=Gb[:T, g * 4:(g + 1) * 4, :T], in0=big2[:T, g * 4:(g + 1) * 4, :T],
                                                                                        in1=cbm[:T, g, :T].unsqueeze(1).broadcast_to([T, 4, T]), op=ALU.mult), R=[B_big2, B_cbm], W=[B_G])
                    yield
                    for h in range(8):
                        k.op("pe", lambda e, h=h: e.matmul(out=ps[3][:T, h * 64:(h + 1) * 64], lhsT=Gb[:T, h, :T], rhs=xdt[:T, h * 64:(h + 1) * 64], start=True, stop=True),
                             R=[B_G, B_xdt], W=[B_ps[3]])
                    if smp and 'S' not in DBG:
                        yield from sample_state_loop()
                    for g in range(0 if smp else 2):
                        k.op("pe", lambda e, g=g: e.matmul(out=ps[4][:T, g * 256:(g + 1) * 256], lhsT=bcT[:, 2 + g, :T], rhs=Hbf[:, g * 256:(g + 1) * 256], start=True, stop=True),
                             R=[B_bcT, B_Hb], W=[B_ps[4]])
                    y3 = ysb[:T, :].rearrange("p (h d) -> p h d", h=8)
                    k.op("dve", lambda e: e.tensor_tensor(out=y3, in0=ps[4][:T, :].rearrange("p (h d) -> p h d", h=8), in1=bc8(EACS[:T], T, 64), op=ALU.mult), R=[B_ps[4], B_sm8], W=[B_y])
                    k.op("dve", lambda e: e.tensor_tensor(out=ysb[:T, :], in0=ysb[:T, :], in1=ps[3][:T, :], op=ALU.add), R=[B_y, B_ps[3]], W=[B_y])
                    k.op("pool", lambda e: e.tensor_tensor(out=tmpf[:T, 0:512].rearrange("p (h d) -> p h d", h=8), in0=x3, in1=bc8(smallc[:T, 2, :], T, 64), op=ALU.mult), R=[B_xtok, B_small], W=[B_tmpf])
                    k.op("dve", lambda e: e.tensor_tensor(out=ysb[:T, :], in0=ysb[:T, :], in1=tmpf[:T, 0:512], op=ALU.add), R=[B_y, B_tmpf], W=[B_y])
                yield
                for g in range(0 if smp else 2):
                    k.op("pe", lambda e, g=g: e.matmul(out=ps[1][:, g * 256:(g + 1) * 256], lhsT=btok[:T, g * 128:(g + 1) * 128], rhs=xdec[:T, g * 256:(g + 1) * 256], start=True, stop=True),
                         R=[B_btok, B_xdec], W=[B_ps[1]])
                if not smp:
                    k.op("dve", lambda e: e.tensor_tensor(out=Hst[:, :].rearrange("p (h d) -> p h d", h=8), in0=Hst[:, :].rearrange("p (h d) -> p h d", h=8), in1=bc8(EATOT, 128, 64), op=ALU.mult),
                         R=[B_H, B_sm8], W=[B_H])
                    k.op("dve", lambda e: e.tensor_tensor(out=Hst[:, :], in0=Hst[:, :], in1=ps[1][:, :], op=ALU.add), R=[B_H, B_ps[1]], W=[B_H])
                    k.op("act", lambda e: e.copy(out=Hbf[:, :], in_=Hst[:, :]), R=[B_H], W=[B_Hb])
                if full:
                    yield
                    k.op("dve", lambda e: e.tensor_tensor(out=ysb[:T, :], in0=ysb[:T, :], in1=zs[:T, :], op=ALU.mult), R=[B_y, B_zs], W=[B_y])
                    RS = rs8[:, :]
                    for g in range(2):
                        k.op("act", lambda e, g=g: e.activation(out=tmpf[:T, g * 256:(g + 1) * 256], in_=ysb[:T, g * 256:(g + 1) * 256], func=AF.Square, accum_out=RS[:T, g:g + 1]), R=[B_y], W=[B_tmpf, B_rs8])
                    k.op("dve", lambda e: e.tensor_scalar(out=RS[:T, 2:4], in0=RS[:T, 0:2], scalar1=1.0 / 256.0, scalar2=RMS_EPS, op0=ALU.mult, op1=ALU.add), R=[B_rs8], W=[B_rs8])
                    k.op("act", lambda e: e.activation(out=RS[:T, 2:4], in_=RS[:T, 2:4], func=AF.Ln), R=[B_rs8], W=[B_rs8])
                    k.op("act", lambda e: e.activation(out=RS[:T, 4:6], in_=RS[:T, 2:4], func=AF.Exp, scale=-0.5), R=[B_rs8], W=[B_rs8])
                    for g in range(2):
                        k.op("dve", lambda e, g=g: e.scalar_tensor_tensor(out=ymix[:T, g * 256:(g + 1) * 256], in0=ysb[:T, g * 256:(g + 1) * 256], scalar=RS[:T, 4 + g:5 + g], in1=normw[:T, g * 256:(g + 1) * 256],
                                                                          op0=ALU.mult, op1=ALU.mult), R=[B_y, B_rs8], W=[B_ymix])

            def chainB():
                if kv:
                    cosb = rope[:T, 0:8]; sinb = rope[:T, 8:16]
                    k3 = kvs[:T, 0:128].rearrange("p (h d) -> p h d", h=2)
                    rotary(k3, B_kvs, 2, T, cosb, sinb)
                    k.op("act", lambda e: e.copy(out=qb[:T, 512:640], in_=kvs[:T, 0:128]), R=[B_kvs], W=[B_qb])
                    k.op("dve", lambda e: e.tensor_copy(out=vb[par][:T, :], in_=kvs[:T, 128:256]), R=[B_kvs], W=[B_vb[par]])
                    k.op("pe", lambda e: e.transpose(out=psb(0)[:, 512:512 + T], in_=qb[:T, 512:640], identity=identb[:T, :T]), R=[B_qb, B_cm], W=[B_ps[0]])
                    k.op("act", lambda e: e.copy(out=kTb[par][:, :T], in_=psb(0)[:, 512:512 + T]), R=[B_ps[0]], W=[B_kT[par]])
                    if smp:
                        k.dma("sp", [(k_s[:, 124 + t_, :], kvs[t_:64:4, 0:128]) for t_ in range(4)] + [(v_s[:, 124 + t_, :], kvs[t_:64:4, 128:256]) for t_ in range(4)]
    , R=[B_kvs], W=[B_dummy], dsem=k.site("kv_s"))
                if full:
                    yield
                    q3 = qs[:T, :].rearrange("p (h d) -> p h d", h=8)
                    rotary(q3, B_qs, 8, T, rope[:T, 0:8], rope[:T, 8:16])
                    k.op("act", lambda e: e.copy(out=qb[:T, 0:512], in_=qs[:T, :]), R=[B_qs], W=[B_qb])
                    for c in range(4):
                        k.op("pe", lambda e, c=c: e.transpose(out=psb(0)[:, c * 128:c * 128 + T], in_=qb[:T, c * 128:(c + 1) * 128], identity=identb[:T, :T]), R=[B_qb, B_cm], W=[B_ps[0]])
                    k.op("act", lambda e: e.copy(out=qT[:, :, :T], in_=psb(0)[:, 0:512].rearrange("p (a b) -> p a b", a=4)[:, :, :T]), R=[B_ps[0]], W=[B_qT])
                    if smp and 'A' not in DBG:
                        yield from sample_attention(par)
                    for jg in range(0 if smp else 2):
                        yield
                        rows = slice(jg * 64, (jg + 1) * 64)
                        for c in range(4):
                            pb = (5, 7)[c // 2]
                            o0 = (c % 2) * 256
                            k.op("pe", lambda e, c=c, pb=pb, o0=o0: e.matmul(out=ps[pb][:T, o0:o0 + 128], lhsT=qT[rows, c, :T], rhs=kTb[1 - par][rows, :], start=True, stop=True),
                                 R=[B_qT, B_kT[1 - par]], W=[B_ps[pb]])
                            k.op("pe", lambda e, c=c, pb=pb, o0=o0: e.matmul(out=ps[pb][:T, o0 + 128:o0 + 256], lhsT=qT[rows, c, :T], rhs=kTb[par][rows, :], start=True, stop=True),
                                 R=[B_qT, B_kT[par]], W=[B_ps[pb]])
                        smv = smvb[:T, :, :]
                        for hh in range(2):
                            k.op("dve", lambda e, hh=hh: e.tensor_tensor(out=smv[:, hh * 2:(hh + 1) * 2, :], in0=ps[(5, 7)[hh]][:T, :].rearrange("p (a b) -> p a b", a=2),
                                                                         in1=mask_ap[:T, :].unsqueeze(1).broadcast_to([T, 2, 256]), op=ALU.add), R=[B_ps[(5, 7)[hh]], B_am], W=[B_smvb])
                        RMX = att8[:, 0, :]; M_ = att8[:, 1, :]; NM = att8[:, 2, :]; RSM = att8[:, 3, :]; ES = att8[:, 4, :]; DEN = att8[:, 5, :]; RDEN = att8[:, 6, :]
                        sk = smallc[:T, 3, jg * 4:(jg + 1) * 4]
                        k.op("dve", lambda e: e.tensor_reduce(out=RMX[:T, 0:4], in_=smv, axis=AX.X, op=ALU.max), R=[B_smvb], W=[B_att8])
                        k.op("dve", lambda e: e.tensor_tensor(out=M_[:T, 0:4], in0=RMX[:T, 0:4], in1=sk, op=ALU.max), R=[B_att8, B_small], W=[B_att8])
                        k.op("dve", lambda e: e.tensor_scalar(out=NM[:T, 0:4], in0=M_[:T, 0:4], scalar1=-1.0, scalar2=None, op0=ALU.mult), R=[B_att8], W=[B_att8])
                        Pv = Pb[:T, :, :]
                        for c in range(4):
                            k.op("act", lambda e, c=c: e.activation(out=Pv[:, c, :], in_=smv[:, c, :], func=AF.Exp, bias=NM[:T, c:c + 1], accum_out=RSM[:T, c:c + 1]), R=[B_smvb, B_att8], W=[B_Pb, B_att8])
                        k.op("dve", lambda e: e.tensor_tensor(out=ES[:T, 0:4], in0=sk, in1=M_[:T, 0:4], op=ALU.subtract), R=[B_att8, B_small], W=[B_att8])
                        k.op("act", lambda e: e.activation(out=ES[:T, 0:4], in_=ES[:T, 0:4], func=AF.Exp), R=[B_att8], W=[B_att8])
                        k.op("dve", lambda e: e.tensor_tensor(out=DEN[:T, 0:4], in0=ES[:T, 0:4], in1=RSM[:T, 0:4], op=ALU.add), R=[B_att8], W=[B_att8])
                        k.op("dve", lambda e: e.reciprocal(out=RDEN[:T, jg * 4:(jg + 1) * 4], in_=DEN[:T, 0:4]), R=[B_att8], W=[B_att8])
                        for c in range(4):
                            for kb in range(2):
                                k.op("pe", lambda e, c=c, kb=kb: e.transpose(out=psb(0)[:, (c * 2 + kb) * 128:(c * 2 + kb) * 128 + T], in_=Pv[:, c, kb * 128:(kb + 1) * 128], identity=identb[:T, :T]),
                                     R=[B_Pb, B_cm], W=[B_ps[0]])
                        k.op("act", lambda e: e.copy(out=PT[:, :, :T], in_=psb(0)[:, 0:1024].rearrange("p (a b) -> p a b", a=8)[:, :, :T]), R=[B_ps[0]], W=[B_PT])
                        for c in range(4):
                            head = c + 4 * jg
                            k.op("pe", lambda e, c=c, head=head: e.matmul(out=ps[6][:T, head * 64:(head + 1) * 64], lhsT=PT[:, c * 2, :T], rhs=vb[1 - par][:, jg * 64:(jg + 1) * 64], start=True, stop=False),
                                 R=[B_PT, B_vb[1 - par]], W=[B_ps[6]])
                            k.op("pe", lambda e, c=c, head=head: e.matmul(out=ps[6][:T, head * 64:(head + 1) * 64], lhsT=PT[:, c * 2 + 1, :T], rhs=vb[par][:, jg * 64:(jg + 1) * 64], start=False, stop=True),
                                 R=[B_PT, B_vb[par]], W=[B_ps[6]])
                    if not smp:
                      k.op("dve", lambda e: e.tensor_tensor(out=ymix[:T, 512:1024].rearrange("p (h d) -> p h d", h=8), in0=ps[6][:T, :].rearrange("p (h d) -> p h d", h=8),
                                                          in1=bc8(att8[:T, 6, :], T, 64), op=ALU.mult), R=[B_ps[6], B_att8], W=[B_ymix])
                yield

            yield "AB"
            if full:
                pend_o = [oload(c_) for c_ in range(NOS - 1)]
            if full and 'I' not in DBG:
                yield from zip_gens(chainA(), chainB())
            else:
                yield from chainA()
                yield from chainB()
            if not full:
                return
            yield "S"
            for kk in range(8):
                k.op("pe", lambda e, kk=kk: e.transpose(out=psb(0)[:, kk * 128:kk * 128 + T], in_=ymix[:T, kk * 128:(kk + 1) * 128], identity=identb[:T, :T]), R=[B_ymix, B_cm], W=[B_ps[0]])
            k.op("act", lambda e: e.copy(out=ymixT[:, :, :T], in_=psb(0)[:, 0:1024].rearrange("p (a b) -> p a b", a=8)[:, :, :T]), R=[B_ps[0]], W=[B_ymixT])
            yield
            for c_ in range(16):
                kk, n = c_ // 2, c_ % 2
                i = pend_o.pop(0)
                if c_ + NOS - 1 < 16:
                    pend_o.append(oload(c_ + NOS - 1))
                k.op("pe", lambda e, n=n, kk=kk, i=i: e.matmul(out=ps[1 + n][:T, :], lhsT=ymixT[:, kk, :T], rhs=oslot[i][:, :], start=(kk == 0), stop=(kk == 7)),
                     R=[B_ymixT, B_oslot[i]], W=[B_ps[1 + n]])
            for n in range(2):
                sl = slice(n * 512, (n + 1) * 512)
                k.op("dve", lambda e, n=n, sl=sl: e.tensor_tensor(out=tmpf[:T, sl], in0=ps[1 + n][:T, :], in1=ada[:T, 2, sl], op=ALU.mult), R=[B_ps[1 + n], B_ada], W=[B_tmpf])
            k.op("dve", lambda e: e.scalar_tensor_tensor(out=X1v[:T, sub, :], in0=xp[:T, :], scalar=ALPHA, in1=tmpf[:T, :], op0=ALU.mult, op1=ALU.add), R=[B_xp, B_tmpf], W=[B_X1v[sub]])
            yield
            layer_norm_rows(X1v[:, sub, :], B_X1v[sub], T, None, X1v[:, sub, :], B_X1v[sub])
            yield
            modulate_T(X1v[:, sub, :], B_X1v[sub], T, 4, h2Tv, B_h2Tv[sub], sub * 128)
            yield
            affine(X1v[:, sub, :], B_X1v[sub], T, 2, X1v[:, sub, :], B_X1v[sub], ("pool", "pool"))

        def sample_state_loop():
            T = 64
            A_ = sm8[:, 1, :]
            k.op("dve", lambda e: e.tensor_copy(out=tmpf[:T, 0:512].rearrange("p (h d) -> p h d", h=8), in_=bc8(A_[:T], T, 64)), R=[B_sm8], W=[B_tmpf])
            for j in range(4):
                k.op("pe", lambda e, j=j: e.matmul(out=ps[2][:, 256 + j * 16:256 + (j + 1) * 16], lhsT=tmpf[:T, j * 128:(j + 1) * 128], rhs=smk[:T, :], start=True, stop=True),
                     R=[B_tmpf, B_smk], W=[B_ps[2]])
            k.op("act", lambda e: e.activation(out=dcol[:, :, :], in_=ps[2][:, 256:320].rearrange("p (a b) -> p a b", a=4), func=AF.Exp), R=[B_ps[2]], W=[B_dcol])
            for sq in range(NSEQ):
                sl_ = sq % 2
                k.dma("sp", [(hn[sl_].rearrange("p (j n) -> p j n", j=4), sssm[sq].rearrange("(j p) n -> p j n", p=128))], R=[], W=[B_hn[sl_]], dsem=D_hn[sl_])
                k.op("act", lambda e, sl_=sl_: e.copy(out=hnb, in_=hn[sl_]), R=[B_hn[sl_]], W=[B_hnb])
                for j in range(4):
                    k.op("pe", lambda e, j=j: e.transpose(out=psb(0)[:, j * 128:(j + 1) * 128], in_=hnb[:, j * 128:(j + 1) * 128], identity=identb[:, :]), R=[B_hnb, B_cm], W=[B_ps[0]])
                k.op("dve", lambda e: e.tensor_copy(out=h0T, in_=psb(0)[:, 0:512]), R=[B_ps[0]], W=[B_h0T])
                for j in range(4):
                    k.op("pe", lambda e, j=j, sq=sq: e.matmul(out=ps[2][:, j * 64 + 4 * sq:j * 64 + 4 * sq + 4], lhsT=h0T[:, j * 128:(j + 1) * 128], rhs=bcT[:, 2 + j // 2, 4 * sq:4 * sq + 4], start=True, stop=True),
                         R=[B_h0T, B_bcT], W=[B_ps[2]])
                k.op("pool", lambda e, sq=sq: e.tensor_scalar(out=xdm[:T, :], in0=xdec[:T, :], scalar1=smk[:T, sq:sq + 1], scalar2=None, op0=ALU.mult), R=[B_xdec, B_smk], W=[B_xdm])
                pb = 1
                for j in range(4):
                    k.op("pe", lambda e, j=j, pb=pb: e.matmul(out=ps[pb][:, j * 128:(j + 1) * 128], lhsT=xdm[:T, j * 128:(j + 1) * 128], rhs=btok[:T, (j // 2) * 128:(j // 2 + 1) * 128], start=True, stop=True),
                         R=[B_xdm, B_btok], W=[B_ps[pb]])
                h3 = hn[sl_].rearrange("p (j n) -> p j n", j=4)
                k.op("dve", lambda e, h3=h3, sq=sq: e.tensor_tensor(out=h3, in0=h3, in1=dcol[:, :, sq:sq + 1].broadcast_to([128, 4, 128]), op=ALU.mult), R=[B_hn[sl_], B_dcol], W=[B_hn[sl_]])
                k.op("dve", lambda e, sl_=sl_, pb=pb: e.tensor_tensor(out=hn[sl_], in0=hn[sl_], in1=ps[pb][:, :], op=ALU.add), R=[B_hn[sl_], B_ps[pb]], W=[B_hn[sl_]])
                k.dma("sp", [(ssm_s[sq].rearrange("(j p) n -> p j n", p=128), h3)], R=[B_hn[sl_]], W=[B_dummy], dsem=D_hn[sl_])
                yield
            k.op("act", lambda e: e.copy(out=yoT, in_=ps[2][:, 0:256]), R=[B_ps[2]], W=[B_yoT])
            for j in range(4):
                k.op("pe", lambda e, j=j: e.transpose(out=ps[4][:T, j * 128:(j + 1) * 128], in_=yoT[:, j * 64:(j + 1) * 64], identity=identf), R=[B_yoT, B_cm], W=[B_ps[4]])

        def sample_attention(par):
            T = 64
            mc = dmk[:16, 0:128]
            for c in range(4):
                k.op("pe", lambda e, c=c: e.transpose(out=psb(0)[:64, 640 + c * 64:640 + (c + 1) * 64], in_=qb[:T, c * 128 + 64:(c + 1) * 128], identity=identb[:T, :T]), R=[B_qb, B_cm], W=[B_ps[0]])
            k.op("pe", lambda e: e.transpose(out=psb(0)[:64, 896:960], in_=qb[:T, 576:640], identity=identb[:T, :T]), R=[B_qb, B_cm], W=[B_ps[0]])
            k.op("dve", lambda e: e.tensor_copy(out=qTs[:, 0, :, :].rearrange("p s (c t) -> p s c t", c=4), in_=qT[0:64, :, 0:64].rearrange("p c (s t) -> p s c t", t=4)), R=[B_qT], W=[B_qTs])
            k.op("dve", lambda e: e.tensor_copy(out=qTs[:, 1, :, :].rearrange("p s (c t) -> p s c t", c=4), in_=psb(0)[:64, 640:896].rearrange("p (c s t) -> p s c t", c=4, t=4)), R=[B_ps[0]], W=[B_qTs])
            k.op("act", lambda e: e.copy(out=knT1[:, :], in_=psb(0)[:64, 896:960]), R=[B_ps[0]], W=[B_qTs])

            for sq in range(NSEQ):
                k.dma("sp", [(ckf, ck[sq]), (cvf, cv[sq])], R=[], W=[B_ckf, B_cvf], dsem=D_ckv)
                if '1' not in DBG:
                    k.dma("sp", [(k_s[sq, 0:124, :], ckf[4:128, :]), (v_s[sq, 0:124, :], cvf[4:128, :])], R=[B_ckf, B_cvf], W=[B_dummy], dsem=k.site("kvc_s"))
                k.op("act", lambda e: e.copy(out=ckb, in_=ckf), R=[B_ckf], W=[B_ckb])
                k.op("pool", lambda e: e.tensor_copy(out=cvb, in_=cvf), R=[B_cvf], W=[B_cvb])
                for j in range(2):
                    k.op("pe", lambda e, j=j: e.transpose(out=psb(0)[:64, j * 128:(j + 1) * 128], in_=ckb[:, j * 64:(j + 1) * 64], identity=identb[:, :]), R=[B_ckb, B_cm], W=[B_ps[0]])
                k.op("act", lambda e: e.copy(out=kcT2[:, :, :].rearrange("p j x -> p (j x)"), in_=psb(0)[:64, 0:256]), R=[B_ps[0]], W=[B_kcT])
                pb = 5 + sq % 2
                if '4' in DBG:
                    continue
                for j in range(2):
                    lq = qTs[:, j, sq, :]
                    knew = kTb[par][0:64, 0:64] if j == 0 else knT1[:, :]
                    k.op("pe", lambda e, j=j, pb=pb, lq=lq: e.matmul(out=ps[pb][:16, j * 192:j * 192 + 128], lhsT=lq, rhs=kcT2[:, j, :], start=True, stop=True), R=[B_qTs, B_kcT], W=[B_ps[pb]])
                    k.op("pe", lambda e, j=j, pb=pb, lq=lq, knew=knew: e.matmul(out=ps[pb][:16, j * 192 + 128:j * 192 + 192], lhsT=lq, rhs=knew, start=True, stop=True), R=[B_qTs, B_kT[par]], W=[B_ps[pb]])
                smv = smvb[:16, 0:2, 0:192]
                psv = ps[pb][:16, 0:384].rearrange("p (j x) -> p j x", j=2)
                k.op("dve", lambda e, psv=psv: e.tensor_tensor(out=smv[:, :, 0:128], in0=psv[:, :, 0:128], in1=mc.unsqueeze(1).broadcast_to([16, 2, 128]), op=ALU.add), R=[B_ps[pb], B_dmk], W=[B_smvb])
                mn = dmk[:16, 128 + sq * 64:128 + (sq + 1) * 64]
                k.op("dve", lambda e, psv=psv, mn=mn: e.tensor_tensor(out=smv[:, :, 128:192], in0=psv[:, :, 128:192], in1=mn.unsqueeze(1).broadcast_to([16, 2, 64]), op=ALU.add), R=[B_ps[pb], B_dmk], W=[B_smvb])
                RMX = att8[:16, 0, 0:2]; M_ = att8[:16, 1, 0:2]; RSM = att8[:16, 3, 0:2]; ES = att8[:16, 4, 0:2]; DEN = att8[:16, 5, 0:2]; RDEN = att8[:16, 6, 0:2]
                k.op("dve", lambda e: e.tensor_reduce(out=RMX, in_=smv, axis=AX.X, op=ALU.max), R=[B_smvb], W=[B_att8])
                k.op("dve", lambda e: e.tensor_tensor(out=M_, in0=RMX, in1=sink16[:, :], op=ALU.max), R=[B_att8, B_smk], W=[B_att8])
                k.op("dve", lambda e: e.tensor_tensor(out=smv, in0=smv, in1=M_.unsqueeze(2).broadcast_to([16, 2, 192]), op=ALU.subtract), R=[B_smvb, B_att8], W=[B_smvb])
                k.op("act", lambda e: e.activation(out=smv, in_=smv, func=AF.Exp), R=[B_smvb], W=[B_smvb])
                k.op("dve", lambda e: e.tensor_reduce(out=RSM, in_=smv, axis=AX.X, op=ALU.add), R=[B_smvb], W=[B_att8])
                k.op("dve", lambda e: e.tensor_tensor(out=ES, in0=sink16[:, :], in1=M_, op=ALU.subtract), R=[B_att8, B_smk], W=[B_att8])
                k.op("act", lambda e: e.activation(out=ES, in_=ES, func=AF.Exp), R=[B_att8], W=[B_att8])
                k.op("dve", lambda e: e.tensor_tensor(out=DEN, in0=ES, in1=RSM, op=ALU.add), R=[B_att8], W=[B_att8])
                k.op("dve", lambda e: e.reciprocal(out=RDEN, in_=DEN), R=[B_att8], W=[B_att8])
                Pv = Pb[:16, 0:2, 0:192]
                k.op("dve", lambda e: e.tensor_tensor(out=Pv, in0=smv, in1=RDEN.unsqueeze(2).broadcast_to([16, 2, 192]), op=ALU.mult), R=[B_smvb, B_att8], W=[B_Pb])
                if '3' in DBG:
                    continue
                for j in range(2):
                    k.op("pe", lambda e, j=j: e.transpose(out=psb(0)[:, (2 * j) * 16:(2 * j) * 16 + 16], in_=Pv[:, j, 0:128], identity=identb[:16, :16]), R=[B_Pb, B_cm], W=[B_ps[0]])
                    k.op("pe", lambda e, j=j: e.transpose(out=psb(0)[:64, (2 * j + 1) * 16:(2 * j + 1) * 16 + 16], in_=Pv[:, j, 128:192], identity=identb[:16, :16]), R=[B_Pb, B_cm], W=[B_ps[0]])
                k.op("act", lambda e: e.copy(out=PT[:, 0, 0:64], in_=psb(0)[:, 0:64]), R=[B_ps[0]], W=[B_PT])
                for j in range(2):
                    k.op("pe", lambda e, j=j: e.matmul(out=ps[7][:16, j * 64:(j + 1) * 64], lhsT=PT[:, 0, (2 * j) * 16:(2 * j) * 16 + 16], rhs=cvb[:, j * 64:(j + 1) * 64], start=True, stop=False), R=[B_PT, B_cvb], W=[B_ps[7]])
                    k.op("pe", lambda e, j=j: e.matmul(out=ps[7][:16, j * 64:(j + 1) * 64], lhsT=PT[:64, 0, (2 * j + 1) * 16:(2 * j + 1) * 16 + 16], rhs=vb[par][:64, j * 64:(j + 1) * 64], start=False, stop=True), R=[B_PT, B_vb[par]], W=[B_ps[7]])
                k.op("act", lambda e, sq=sq: e.copy(out=osb[:16, sq * 128:(sq + 1) * 128], in_=ps[7][:16, 0:128]), R=[B_ps[7]], W=[B_X1[NXB - 1][0]])
                yield
            if '2' in DBG:
                return
            B_osc = Buf("osc")
            k.dma("sp", [(osc[:, :], osb[:16, :])], R=[B_X1[NXB - 1][0]], W=[B_osc], dsem=k.site("osc_w"))
            srcv = osc.rearrange("(c t) (s j d) -> t j s c d", t=4, s=NSEQ, j=2)
            k.dma("sp", [(ymix[t_:64:4, 512 + j * 256:512 + (j + 1) * 256].rearrange("p (c d) -> p c d", c=4), srcv[t_, j]) for t_ in range(4) for j in range(2)],
                  R=[B_osc], W=[B_ymix], dsem=k.site("osc_r"))

        def rotary(v3, B_v, nh, T, cosb, sinb):
            cb = cosb.unsqueeze(1).broadcast_to([T, nh, 8]); sbb = sinb.unsqueeze(1).broadcast_to([T, nh, 8])
            t = [rt[:T, i, 0:nh * 8].rearrange("p (h d) -> p h d", h=nh) for i in range(4)]
            x1 = v3[:, :, 0:8]; x2 = v3[:, :, 8:16]
            k.op("dve", lambda e: e.tensor_tensor(out=t[0], in0=x1, in1=cb, op=ALU.mult), R=[B_v, B_rope], W=[B_rt])
            k.op("pool", lambda e: e.tensor_tensor(out=t[1], in0=x2, in1=sbb, op=ALU.mult), R=[B_v, B_rope], W=[B_rt])
            k.op("dve", lambda e: e.tensor_tensor(out=t[2], in0=x2, in1=cb, op=ALU.mult), R=[B_v, B_rope], W=[B_rt])
            k.op("pool", lambda e: e.tensor_tensor(out=t[3], in0=x1, in1=sbb, op=ALU.mult), R=[B_v, B_rope], W=[B_rt])
            k.op("dve", lambda e: e.tensor_tensor(out=x1, in0=t[0], in1=t[1], op=ALU.subtract), R=[B_rt], W=[B_v])
            k.op("dve", lambda e: e.tensor_tensor(out=x2, in0=t[2], in1=t[3], op=ALU.add), R=[B_rt], W=[B_v])

        wq = {"n": 0}

        def wload(src_ap):
            i = wq["n"] % NSLOT
            wq["n"] += 1
            k.dma("sp", [(wslot[i][:, :], src_ap)], R=[B_wup_s, B_wdn_s, B_wout_s], W=[B_wslot[i]], dsem=D_wslot[i])
            return i

        def mlp(T, nsub, y_rows_fn, xb=0, part="all"):
            xb = xb % NXB
            X1v = X1[:, xb]; B_X1v = B_X1[xb]; h2Tv = h2T[:, xb]; B_h2Tv = B_h2T[xb]
            TTl = (nsub - 1) * 128 + T
            if part == "tail":
                for s_ in range(nsub):
                    Ts = T if s_ == nsub - 1 else 128
                    layer_norm_rows(X1v[:, s_, :], B_X1v[s_], Ts, 4, X1v[:, s_, :], B_X1v[s_])
                    yield
                    k.dma("sp", [(y_rows_fn(s_, Ts), X1v[:Ts, s_, :])], R=[B_X1v[s_]], W=[B_dummy], dsem=k.site(f"yo{s_}"))
                    yield
                return

            def upsrc(g2):
                return wup_s[g2 // 4, :, (g2 % 4) * 1024:(g2 % 4 + 1) * 1024]
            WD = NSLOT - 1
            pend = [wload(upsrc(g_)) for g_ in range(WD)]
            for g in range(32):
                i = pend.pop(0)
                if g + WD < 32:
                    pend.append(wload(upsrc(g + WD)))
                wv = wslot[i][:, :].rearrange("p (fc kk f) -> p fc kk f", fc=1, kk=8)
                for fcl in range(1):
                    fc = g
                    ub = fc % 2
                    for kk in range(8):
                        k.op("pe", lambda e, fcl=fcl, kk=kk, ub=ub, wv=wv: e.matmul(out=ps[6 + ub][:, 0:TTl], lhsT=wv[:, fcl, kk, :], rhs=h2Tv[:, kk, 0:TTl], start=(kk == 0), stop=(kk == 7)),
                             R=[B_wslot[i]] + B_h2Tv[:nsub], W=[B_ps[6 + ub]])
                    k.op("act", lambda e, ub=ub: e.activation(out=urelu[ub][:, 0:TTl], in_=ps[6 + ub][:, 0:TTl], func=AF.Relu), R=[B_ps[6 + ub]], W=[B_urelu[ub]])
                    k.op("pool", lambda e, fc=fc, ub=ub: e.tensor_tensor(out=uT[:, fc, 0:TTl], in0=urelu[ub][:, 0:TTl], in1=urelu[ub][:, 0:TTl], op=ALU.mult), R=[B_urelu[ub]], W=[B_uT[fc], B_stgb[fc // 16]])
                    yield
            for n in range(2):
                def dnsrc(g2, n=n):
                    return wdn_s[n, g2 // 4, :, (g2 % 4) * 1024:(g2 % 4 + 1) * 1024]
                pend = [wload(dnsrc(g_)) for g_ in range(WD)]
                for g in range(16):
                    i = pend.pop(0)
                    if g + WD < 16:
                        pend.append(wload(dnsrc(g + WD)))
                    wv = wslot[i][:, :].rearrange("p (a n) -> p a n", a=2)
                    for s_ in range(nsub):
                        Ts = T if s_ == nsub - 1 else 128
                        for fl in range(2):
                            fk = g * 2 + fl
                            k.op("pe", lambda e, s_=s_, fl=fl, fk=fk, wv=wv, Ts=Ts: e.matmul(out=ps[4 + s_][:Ts, :], lhsT=uT[:, fk, s_ * 128:s_ * 128 + Ts], rhs=wv[:, fl, :], start=(fk == 0), stop=(fk == 31)),
                                 R=[B_wslot[i], B_uT[fk]], W=[B_ps[4 + s_]])
                    yield
                for s_ in range(nsub):
                    Ts = T if s_ == nsub - 1 else 128
                    sl = slice(n * 512, (n + 1) * 512)
                    k.op("dve", lambda e, s_=s_, sl=sl, Ts=Ts: e.tensor_tensor(out=tmpf[:Ts, sl], in0=ps[4 + s_][:Ts, :], in1=ada[:Ts, 5, sl], op=ALU.mult), R=[B_ps[4 + s_], B_ada], W=[B_tmpf])
                    k.op("dve", lambda e, s_=s_, sl=sl, Ts=Ts: e.scalar_tensor_tensor(out=X1v[:Ts, s_, sl], in0=X1v[:Ts, s_, sl], scalar=ALPHA, in1=tmpf[:Ts, sl], op0=ALU.mult, op1=ALU.add),
                         R=[B_X1v[s_], B_tmpf], W=[B_X1v[s_]])
            if part == "main":
                return
            for s_ in range(nsub):
                Ts = T if s_ == nsub - 1 else 128
                layer_norm_rows(X1v[:, s_, :], B_X1v[s_], Ts, 4, X1v[:, s_, :], B_X1v[s_])
                yield
                k.dma("sp", [(y_rows_fn(s_, Ts), X1v[:Ts, s_, :])], R=[B_X1v[s_]], W=[B_dummy], dsem=k.site(f"yo{s_}"))

        def run(g):
            for _ in g:
                pass

        def zip_gens(ga, gb):
            da = db = False
            while not (da and db):
                if not da:
                    try:
                        next(ga)
                    except StopIteration:
                        da = True
                if not db:
                    try:
                        next(gb)
                    except StopIteration:
                        db = True
                yield

        def interleave(ga, gb):
            da = db = False
            while not (da and db):
                if not da:
                    try:
                        next(ga)
                    except StopIteration:
                        da = True
                if not db:
                    try:
                        next(gb)
                    except StopIteration:
                        db = True

        def adv_until(g, mark):
            for v in g:
                if v == mark:
                    return True
            return False

        def zip_until(ga, gb, mark_b):
            da = db = False
            while not (da and db):
                if not da:
                    try:
                        next(ga)
                    except StopIteration:
                        da = True
                if not db:
                    try:
                        if next(gb) == mark_b:
                            db = True
                    except StopIteration:
                        db = True

        k.op("dve", lambda e: e.memset(Hst[:, :], 0.0), R=[], W=[B_H])
        k.op("dve", lambda e: e.memset(Hbf[:, :], 0.0), R=[], W=[B_Hb])
        k.op("pool", lambda e: e.memset(xbcT[:, :, :], 0.0), R=[], W=[B_xbcT])
        for i in range(2):
            k.op("pool", lambda e, i=i: e.memset(kTb[i][:, :], 0.0), R=[], W=[B_kT[i]])
            k.op("pool", lambda e, i=i: e.memset(vb[i][:, :], 0.0), R=[], W=[B_vb[i]])

        if with_sample:
            compute_ada(c_s, NTS)
            late_weight_prep()
            fence()
            k.dma("sp", [(cms[:, :, :], cmask_s), (dmk[:16, :], dmask)], R=[], W=[B_cms, B_dmk], dsem=k.site("cms"))
            k.dma("sp", [(smk[:, :], smk_d), (sink16[:, :], sink16_d)], R=[], W=[B_smk], dsem=k.site("smk"))
            run(mixer_tile(x_s, rope_s, NTS, 0, "full", None, 0, smp=True, xb=1))
            fence()
            run(mlp(NTS, 1, lambda s_, Ts: y_s[0:Ts, :], xb=1))
        if not with_sample:
            late_weight_prep()
        compute_ada(c_p.broadcast_to([128, D]), 128)

        def pre_gen(t):
            return mixer_tile(x_prev[t * 128:(t + 1) * 128, :], rope_prev[:, :], 128, 1, "statekv" if t == NT - 1 else "state", None, 0)

        if 'Q' in DBG:
            for t in range(NT):
                run(pre_gen(t))
        else:
            curp = pre_gen(0)
            adv_until(curp, "P2")
            for t in range(NT):
                nxtp = pre_gen(t + 1) if t + 1 < NT else None
                if nxtp is not None:
                    zip_until(curp, nxtp, "P2")
                else:
                    run(curp)
                curp = nxtp
        k.op("dve", lambda e: e.tensor_scalar(out=Hst[:, :], in0=Hst[:, :], scalar1=flg[:, 0:1], scalar2=None, op0=ALU.mult), R=[B_H, B_am], W=[B_H])
        k.op("act", lambda e: e.copy(out=Hbf[:, :], in_=Hst[:, :]), R=[B_H], W=[B_Hb])
        k.op("pool", lambda e: e.tensor_scalar(out=xbcT[:, :, 0:3], in0=xbcT[:, :, 0:3], scalar1=flg[:, 0:1], scalar2=None, op0=ALU.mult), R=[B_xbcT, B_am], W=[B_xbcT])

        def final_outputs():
            k.dma("sp", [(k_p[:, :], kvs[:, 0:128]), (v_p[:, :], kvs[:, 128:256])], R=[B_kvs], W=[B_dummy], dsem=k.site("kv_p"))
            k.dma("sp", [(conv_p[:, c * 128:(c + 1) * 128].rearrange("j p -> p j"), xbcT[:, c, 0:3]) for c in range(8)], R=[B_xbcT], W=[B_dummy], dsem=k.site("conv_p"))
            for c in range(4):
                k.op("pe", lambda e, c=c: e.transpose(out=ps[4][:, c * 128:(c + 1) * 128], in_=Hst[:, c * 128:(c + 1) * 128], identity=identf), R=[B_H, B_cm], W=[B_ps[4]])
            k.op("act", lambda e: e.copy(out=tmpf[:, 0:512], in_=ps[4][:, :]), R=[B_ps[4]], W=[B_tmpf])
            k.dma("sp", [(ssm_p.rearrange("(c p) n -> p c n", p=128), tmpf[:, 0:512].rearrange("p (c n) -> p c n", c=4))], R=[B_tmpf], W=[B_dummy], dsem=k.site("ssm_p"))

        def tile_gen(t):
            return mixer_tile(x_main[t * 128:(t + 1) * 128, :], rope_main[t * 128:(t + 1) * 128, :], 128, t % 2, "full",
                              am[:, 0, :] if t == 0 else am[:, 1, :], t % NSUB, xb=0)

        def mlp_st(st, part):
            return mlp(128, NSUB, lambda s_, Ts, st=st: y_main[(st * NSUB + s_) * 128:(st * NSUB + s_) * 128 + Ts, :], xb=0, part=part)

        def empty():
            return
            yield

        cur = tile_gen(0)
        adv_until(cur, "S")
        for t in range(NT):
            nxt = tile_gen(t + 1) if t + 1 < NT else None
            if nxt is not None and 'P' not in DBG:
                zip_until(cur, nxt, "AB")
            else:
                run(cur)
                if nxt is not None:
                    adv_until(nxt, "AB")
            if t == NT - 1:
                final_outputs()
            tail = empty()
            if (t + 1) % NSUB == 0:
                st = t // NSUB
                run(mlp_st(st, "main"))
                tail = mlp_st(st, "tail")
            if nxt is not None:
                zip_until(tail, nxt, "S")
            else:
                run(tail)
            cur = nxt

        k.finish("sp")
        build.ninstr = k.ninstr
    return nc


def _rope_table(pos):
    half = 8
    inv = (np.float32(500000.0) ** (-np.arange(half, dtype=np.float32) / np.float32(half))).astype(np.float32)
    ang = (pos.astype(np.float32)[:, None] * inv[None, :]).astype(np.float32)
    return np.concatenate([np.cos(ang), np.sin(ang)], 1).astype(np.float32)


def _band_mask(first_masked):
    i = np.arange(128)[:, None]
    j = np.arange(256)[None, :]
    ok = (j > i) & (j <= i + 128)
    if first_masked:
        ok = ok & (j >= 128)
    return np.where(ok, 0.0, NEG).astype(np.float32)


def _prep_inputs(inp, LH):
    f = np.ascontiguousarray
    xprompt = inp["x_prompt"]
    B = xprompt.shape[0]
    w_in = inp["w_in"][0]
    Z_END = 512; XBC_END = 1536; DT_END = 1544; Q_END = 2056; K_END = 2184
    qcols = []
    for c in range(4):
        for j in range(2):
            h = c + 4 * j
            qcols += list(range(DT_END + h * 64, DT_END + (h + 1) * 64))
    cols = list(range(0, XBC_END)) + qcols + list(range(Q_END, K_END)) + list(range(K_END, K_END + 128)) + list(range(XBC_END, DT_END))
    w_in_r = f(w_in[:, cols])
    sinks = inp["sinks"][0]
    sinks_p = f(np.array([sinks[c + 4 * j] for j in range(2) for c in range(4)], np.float32)[None, :])
    U, SL, BD = _consts(128, 1)
    ident = np.eye(128, dtype=np.float32)
    cau = U.copy()
    cmask = f(np.stack([U, SL, np.ones((128, 128), np.float32), ident, cau], 1))
    common = {
        "ln_in_g": f(inp["ln_in_g"][None, :]), "ln_in_b": f(inp["ln_in_b"][None, :]),
        "w_ada": f(inp["w_ada"][0]), "b_ada": f(inp["b_ada"]), "w_in": w_in_r,
        "conv_w": f(inp["conv_w"][0]), "conv_b": f(inp["conv_b"]), "dt_bias": f(inp["dt_bias"]),
        "A_log": f(inp["A_log"]), "D_skip": f(inp["D_skip"]), "ssm_norm_w": f(inp["ssm_norm_w"]),
        "sinks_p": sinks_p, "w_out": f(inp["w_out"][0]), "ln1_g": f(inp["ln1_g"]), "ln1_b": f(inp["ln1_b"]),
        "w_up": f(inp["w_up"][0]), "w_down": f(inp["w_down"][0]), "ln2_g": f(inp["ln2_g"]), "ln2_b": f(inp["ln2_b"]),
        "cmask": cmask, "m2": _band_mask(False),
    }
    Us, SLs, BDs = _consts(64, 16)
    cms_ = np.zeros((128, 3, 128), np.float32)
    for i_, a_ in enumerate((Us, SLs, BDs)):
        cms_[:64, i_, :64] = a_
    smk_ = np.zeros((128, 16), np.float32)
    smk_[np.arange(64), np.arange(64) // 4] = 1.0
    sink16 = np.zeros((16, 2), np.float32)
    dm = np.full((16, 128 + 16 * 64), NEG, np.float32)
    for c_ in range(4):
        for t_ in range(4):
            r_ = c_ * 4 + t_
            for j_ in range(2):
                sink16[r_, j_] = sinks[c_ + 4 * j_]
            dm[r_, t_ + 1:128] = 0.0
            for s_ in range(16):
                dm[r_, 128 + s_ * 64 + s_ * 4:128 + s_ * 64 + s_ * 4 + t_ + 1] = 0.0
    common.update({"cmask_s": f(cms_), "smk": smk_, "sink16": sink16, "dmask": dm,
                   "rope_s": f(np.tile(_rope_table(PAST + np.arange(4)), (16, 1)))})
    maps = []
    for c in range(8):
        b, half = c // 2, c % 2
        m = dict(common)
        sq = slice(16 * c, 16 * (c + 1))
        m["x_s"] = f(inp["x_sample"][sq].reshape(64, D))
        m["c_s"] = f(np.repeat(inp["c_sample"][sq], 4, axis=0))
        m["sconv"] = f(inp["state_conv"][0, sq].reshape(48, D))
        m["sssm"] = f(inp["state_ssm"][0, sq].reshape(16, 512, 128))
        m["ck"] = f(inp["cache_k"][0, sq].reshape(16, 128, 128))
        m["cv"] = f(inp["cache_v"][0, sq].reshape(16, 128, 128))
        m["x_prev"] = f(xprompt[b, 0:LH])
        m["x_main"] = f(xprompt[b, half * LH:(half + 1) * LH])
        m["c_p"] = f(inp["c_prompt"][b][None, :])
        fl = np.zeros((128, 2), np.float32); fl[:, 0] = float(half)
        m["flag"] = fl
        m["m0"] = _band_mask(half == 0)
        m["rope_main"] = _rope_table(np.arange(half * LH, (half + 1) * LH))
        m["rope_prev"] = _rope_table(np.arange(LH - 128, LH))
        maps.append(m)
    return maps


_CACHE = {}


def kernel(**inputs):
    inp = {k_: np.asarray(v) for k_, v in inputs.items()}
    B, S, _ = inp["x_prompt"].shape
    LH = S // 2
    key = LH
    if key not in _CACHE:
        _CACHE[key] = build(LH)
    nc = _CACHE[key]
    maps = _prep_inputs(inp, LH)
    res = run_bass_kernel_spmd(nc, maps, core_ids=list(range(8)))
    R = res.results
    y_prompt = np.stack([np.concatenate([R[2 * b]["y_main"], R[2 * b + 1]["y_main"]], 0) for b in range(B)], 0)
    conv_p = np.stack([R[2 * b + 1]["conv_p"] for b in range(B)], 0)[None]
    ssm_p = np.stack([R[2 * b + 1]["ssm_p"].reshape(8, 64, 128) for b in range(B)], 0)[None]
    k_p = np.stack([R[2 * b + 1]["k_p"].reshape(128, 2, 64) for b in range(B)], 0)[None]
    v_p = np.stack([R[2 * b + 1]["v_p"].reshape(128, 2, 64) for b in range(B)], 0)[None]
    y_sample = np.concatenate([R[c]["y_s"].reshape(16, 4, D) for c in range(8)], 0)
    conv_s = np.concatenate([R[c]["conv_s"] for c in range(8)], 0)[None]
    ssm_s = np.concatenate([R[c]["ssm_s"].reshape(16, 8, 64, 128) for c in range(8)], 0)[None]
    k_s = np.concatenate([R[c]["k_s"].reshape(16, 128, 2, 64) for c in range(8)], 0)[None]
    v_s = np.concatenate([R[c]["v_s"].reshape(16, 128, 2, 64) for c in range(8)], 0)[None]
    return (y_prompt, y_sample, conv_p, ssm_p, k_p, v_p, conv_s, ssm_s, k_s, v_s)
```

```python
import numpy as np
from contextlib import ExitStack
import concourse.bass as bass
import concourse.mybir as mybir
from concourse.bass_utils import run_bass_kernel_spmd

F32 = mybir.dt.float32
BF16 = mybir.dt.bfloat16
AF = mybir.ActivationFunctionType
ALU = mybir.AluOpType
AX = mybir.AxisListType

D = 1024
NQ = 8
NKV = 2
HD = 64
NH = 8
HP = 64
NST = 128
DFF = 4096
PAST = 16384
ALPHA = 2.0 ** 0.25
LN_EPS = 1e-5
RMS_EPS = 1e-5
NEG = -30000.0
C_Z, C_XBC, C_Q, C_K, C_V, C_DT, C_END = 0, 512, 1536, 2048, 2176, 2304, 2312


class Buf:
    __slots__ = ("name", "w", "r", "multi")

    def __init__(self, name, multi=False):
        self.name = name
        self.w = {}
        self.r = {}
        self.multi = multi


class K:
    def __init__(self, nc, es):
        self.nc = nc
        self.es = es
        self.eng = {"pe": nc.tensor, "dve": nc.vector, "act": nc.scalar, "pool": nc.gpsimd, "sp": nc.sync}
        self.sem = {}
        self.cnt = {}
        self.waited = {k: {} for k in self.eng}
        self.semobj = {}
        for k in ("pe", "dve", "act", "pool"):
            s = es.enter_context(nc.semaphore("sem_" + k))
            self.sem[k] = s
            self.semobj[id(s)] = s
            self.cnt[k] = 0
        self.dsems = []
        self._sites = {}
        self.ninstr = 0

    def new_dsem(self, name):
        s = self.es.enter_context(self.nc.semaphore("d_" + name))
        self.semobj[id(s)] = s
        h = [s, 0]
        self.dsems.append(h)
        return h

    def _need(self, ek, R, W):
        need = {}
        for b in R:
            for s, v in b.w.items():
                if need.get(s, 0) < v:
                    need[s] = v
        for b in W:
            if b.multi:
                continue
            for dd in (b.w, b.r):
                for s, v in dd.items():
                    if need.get(s, 0) < v:
                        need[s] = v
        return need

    def _waits(self, ek, need):
        e = self.eng[ek]
        own = id(self.sem[ek]) if ek in self.sem else None
        wd = self.waited[ek]
        for s, v in need.items():
            if s == own:
                if ek == "pe":
                    continue
                if v < self.cnt[ek] - 8:
                    continue
            if wd.get(s, 0) >= v:
                continue
            e.wait_ge(self.semobj[s], v)
            self.ninstr += 1
            wd[s] = v

    def _record(self, R, W, s, v):
        for b in W:
            if b.multi:
                if b.w.get(s, 0) < v:
                    b.w[s] = v
            else:
                b.w = {s: v}
                b.r = {}
        for b in R:
            if b in W:
                continue
            if b.r.get(s, 0) < v:
                b.r[s] = v

    def op(self, ek, fn, R=(), W=()):
        need = self._need(ek, R, W)
        self._waits(ek, need)
        ins = fn(self.eng[ek])
        self.cnt[ek] += 1
        ins.then_inc(self.sem[ek], 1)
        self.ninstr += 1
        self._record(R, W, id(self.sem[ek]), self.cnt[ek])

    def dma(self, qk, outs_ins, R, W, dsem):
        need = self._need(qk, R, W)
        self._waits(qk, need)
        for (o, i) in outs_ins:
            self.eng[qk].dma_start(out=o, in_=i, allow_slow_non_contiguous=True).then_inc(dsem[0], 16)
            dsem[1] += 16
            self.ninstr += 1
        self._record(R, W, id(dsem[0]), dsem[1])

    def site(self, name):
        if name not in self._sites:
            self._sites[name] = self.new_dsem(name)
        return self._sites[name]

    def finish(self, ek="sp"):
        for h in self.dsems:
            if h[1] > 0:
                self.eng[ek].wait_ge(h[0], h[1])
        for k2 in ("pe", "dve", "act", "pool"):
            if self.cnt[k2] > 0:
                self.eng[ek].wait_ge(self.sem[k2], self.cnt[k2])


def _consts(T, nseq):
    tl = T // nseq
    idx = np.arange(T)
    seq = idx // tl
    same = seq[:, None] == seq[None, :]
    U = (same & (idx[:, None] <= idx[None, :])).astype(np.float32)
    SL = (same & (idx[:, None] > idx[None, :])).astype(np.float32)
    BD = same.astype(np.float32)
    return U, SL, BD


def build(LH, NTS=64, NSEQ=16, with_sample=True, NSUB=2):
    import os
    with_sample = with_sample and os.environ.get('NO_SAMPLE') is None
    DBG = os.environ.get('DBG', '')
    nc = bass.Bass("TRN2", target_bir_lowering=False)
    NT = LH // 128
    assert NT % NSUB == 0

    def din(name, shape, dt=F32):
        return nc.dram_tensor(name, list(shape), dt, kind="ExternalInput").ap()

    def dout(name, shape, dt=F32):
        return nc.dram_tensor(name, list(shape), dt, kind="ExternalOutput").ap()

    def dscr(name, shape, dt):
        return nc.dram_tensor(name, list(shape), dt, kind="Internal").ap()
    assert NTS == 64 and NSEQ == 16

    x_prev = din("x_prev", [LH, D])
    x_main = din("x_main", [LH, D])
    c_p = din("c_p", [1, D])
    flag = din("flag", [128, 2])
    m0 = din("m0", [128, 256])
    m2 = din("m2", [128, 256])
    rope_main = din("rope_main", [LH, 16])
    rope_prev = din("rope_prev", [128, 16])
    cmask = din("cmask", [128, 5, 128])
    ln_in_g = din("ln_in_g", [1, D]); ln_in_b = din("ln_in_b", [1, D])
    w_ada = din("w_ada", [D, 6 * D]); b_ada = din("b_ada", [1, 6 * D])
    w_in = din("w_in", [D, C_END])
    conv_w = din("conv_w", [4, D]); conv_b = din("conv_b", [1, D])
    dt_bias = din("dt_bias", [1, NH]); A_log = din("A_log", [1, NH]); D_skip = din("D_skip", [1, NH])
    ssm_norm_w = din("ssm_norm_w", [1, 512])
    sinks_p = din("sinks_p", [1, NQ])
    w_out = din("w_out", [D, D])
    ln1_g = din("ln1_g", [1, D]); ln1_b = din("ln1_b", [1, D])
    w_up = din("w_up", [D, DFF]); w_down = din("w_down", [DFF, D])
    ln2_g = din("ln2_g", [1, D]); ln2_b = din("ln2_b", [1, D])

    x_s = din("x_s", [NTS, D]); c_s = din("c_s", [NTS, D]); rope_s = din("rope_s", [NTS, 16])
    sconv = din("sconv", [NSEQ * 3, D]); sssm = din("sssm", [NSEQ, 512, 128])
    ck = din("ck", [NSEQ, 128, 128]); cv = din("cv", [NSEQ, 128, 128])
    cmask_s = din("cmask_s", [128, 3, 128]); smk_d = din("smk", [128, NSEQ]); sink16_d = din("sink16", [16, 2])
    dmask = din("dmask", [16, 128 + NSEQ * 64])
    y_s = dout("y_s", [NTS, D]); conv_s = dout("conv_s", [NSEQ, 3, D]); ssm_s = dout("ssm_s", [NSEQ, 512, 128])
    k_s = dout("k_s", [NSEQ, 128, 128]); v_s = dout("v_s", [NSEQ, 128, 128])
    osc = dscr("osc", [16, NSEQ * 128], BF16)
    y_main = dout("y_main", [LH, D])
    conv_p = dout("conv_p", [3, D])
    ssm_p = dout("ssm_p", [512, 128])
    k_p = dout("k_p", [128, 128])
    v_p = dout("v_p", [128, 128])

    wup_s = dscr("wup_s", [8, 128, 4096], BF16)
    wdn_s = dscr("wdn_s", [2, 4, 128, 4096], BF16)
    wout_s = dscr("wout_s", [4, 128, 2048], BF16)

    es = ExitStack()
    with es:
        k = K(nc, es)

        def sb(name, shape, dt=F32):
            return es.enter_context(nc.sbuf_tensor(name, list(shape), dt))

        w_in_b = sb("w_in_b", [128, 8, C_END], BF16); B_win = Buf("w_in_b", multi=True)
        B_wout_s = Buf("wout_s", multi=True)
        NSLOT = 6
        wslot = [sb(f"wslot{i}", [128, 1024], BF16) for i in range(NSLOT)]
        B_wslot = [Buf(f"wslot{i}") for i in range(NSLOT)]
        D_wslot = [k.new_dsem(f"wslot{i}") for i in range(NSLOT)]
        NOS = 4
        oslot = [sb(f"oslot{i}", [128, 512], BF16) for i in range(NOS)]
        B_oslot = [Buf(f"oslot{i}") for i in range(NOS)]
        D_oslot = [k.new_dsem(f"oslot{i}") for i in range(NOS)]
        oq = {"n": 0}

        def oload(c_):
            kk, n = c_ // 2, c_ % 2
            i = oq["n"] % NOS
            oq["n"] += 1
            k.dma("sp", [(oslot[i][:, :], wout_s[kk // 2, :, (kk % 2) * 1024 + n * 512:(kk % 2) * 1024 + (n + 1) * 512])], R=[B_wout_s], W=[B_oslot[i]], dsem=D_oslot[i])
            return i
        D_stgf = [k.new_dsem(f"stgf{i}") for i in range(2)]
        B_stgb = [Buf(f"stgb{i}") for i in range(2)]
        D_stgb = [k.new_dsem(f"stgb{i}") for i in range(2)]
        B_wup_s = Buf("wup_s", multi=True); B_wdn_s = Buf("wdn_s", multi=True)

        ada = sb("ada", [128, 6, D], F32); B_ada = Buf("ada")
        gb = sb("gb", [128, 6, D], F32); B_gb = Buf("gb")
        normw = sb("normw", [128, 512], F32)
        smallc = sb("smallc", [128, 5, 8], F32)
        B_small = Buf("smallc")
        cw = sb("cw", [128, 8, 5], F32); B_cw = Buf("cw")
        cm = sb("cm", [128, 5, 128], F32); B_cm = Buf("cm")
        identb = sb("identb", [128, 128], BF16)
        am = sb("am", [128, 2, 256], F32); B_am = Buf("am")
        flg = sb("flg", [128, 2], F32)
        D_c = k.new_dsem("consts")

        x_in = sb("x_in", [128, D], F32); B_xin = Buf("x_in"); D_xin = k.new_dsem("x_in")
        rope = sb("rope", [128, 16], F32); B_rope = Buf("rope"); D_rope = k.new_dsem("rope")
        stats0 = sb("stats", [128, 2, 6], F32); B_stats0 = Buf("stats")
        mv0 = sb("mv", [128, 8], F32); B_mv0 = Buf("mv")
        xpb = sb("xpb", [128, 2, D], F32); B_xpb = [Buf("xp0"), Buf("xp1")]
        xp = xpb[:, 0, :]; B_xp = B_xpb[0]
        hbP = sb("hbP", [128, D], BF16); B_hbP = Buf("hbP")
        hTP = sb("hTP", [128, 8, 128], BF16); B_hTP = Buf("hTP")
        hT = sb("hT", [128, 8, 128], BF16); B_hT = Buf("hT")
        cT = hT; B_cT = B_hT
        xbcT = sb("xbcT", [128, 8, 131], F32); B_xbcT = Buf("xbcT")
        xacc = sb("xacc", [128, 8, 128], F32); B_xacc = Buf("xacc")
        xcT = xacc[:, 0:4, :]; B_xcT = B_xacc
        B_xc = [Buf(f"xc{i}") for i in range(9)]
        bcT = sb("bcT", [128, 4, 128], BF16); B_bcT = Buf("bcT")
        xtok = sb("xtok", [128, 512], F32); B_xtok = Buf("xtok")
        btok = sb("btok", [128, 256], BF16); B_btok = Buf("btok")
        sm8 = sb("sm8", [128, 8, 8], F32); B_sm8 = Buf("sm8")
        xdt = sb("xdt", [128, 512], BF16); B_xdt = Buf("xdt")
        xdec = sb("xdec", [128, 512], BF16); B_xdec = Buf("xdec")
        Hst = sb("Hst", [128, 512], F32); B_H = Buf("H")
        Hbf = sb("Hbf", [128, 512], BF16); B_Hb = Buf("Hb")
        big1 = sb("big1", [128, 8, 128], F32); B_big1 = Buf("big1")
        big2 = big1; B_big2 = B_big1
        Gb = sb("Gb", [128, 8, 128], BF16); B_G = Buf("G")
        cbm = sb("cbm", [128, 2, 128], F32); B_cbm = Buf("cbm")
        ysb = sb("ysb", [128, 512], F32); B_y = Buf("y")
        qs = sb("qs", [128, 512], F32); B_qs = Buf("qs")
        kvs = sb("kvs", [128, 256], F32); B_kvs = Buf("kvs")
        rt = sb("rt", [128, 4, 64], F32); B_rt = Buf("rt")
        qb = sb("qb", [128, 640], BF16); B_qb = Buf("qb")
        qT = sb("qT", [128, 4, 128], BF16); B_qT = Buf("qT")
        kTb = [sb(f"kTb{i}", [128, 128], BF16) for i in range(2)]; B_kT = [Buf(f"kT{i}") for i in range(2)]
        vb = [sb(f"vb{i}", [128, 128], BF16) for i in range(2)]; B_vb = [Buf(f"vb{i}") for i in range(2)]
        PT = sb("PT", [128, 8, 128], BF16); B_PT = Buf("PT")
        att8 = sb("att8", [128, 8, 8], F32); B_att8 = Buf("att8")
        rs8 = sb("rs8", [128, 8], F32); B_rs8 = Buf("rs8")
        ymix = sb("ymix", [128, D], BF16); B_ymix = Buf("ymix")
        hb = ymix; B_hb = B_ymix
        ymixT = hT; B_ymixT = B_hT
        tmpf = sb("tmpf", [128, D], F32); B_tmpf = Buf("tmpf")
        zs = sb("zs", [128, 512], F32); B_zs = Buf("zs")
        TT = NSUB * 128
        NXB = 1
        X1 = sb("X1", [128, NXB, NSUB, D], F32); B_X1 = [[Buf(f"X1_{b}_{i}") for i in range(NSUB)] for b in range(NXB)]
        h2T = sb("h2T", [128, NXB, 8, TT], BF16); B_h2T = [[Buf(f"h2T_{b}_{i}") for i in range(NSUB)] for b in range(NXB)]
        smvb = sb("smvb", [128, 4, 256], F32); B_smvb = Buf("smvb")
        Pb = sb("Pb", [128, 4, 256], BF16); B_Pb = Buf("Pb")
        uT = sb("uT", [128, 32, TT], BF16); B_uT = [Buf(f"uT_{i}") for i in range(32)]
        urelu = [sb(f"urelu{i}", [128, TT], BF16) for i in range(2)]; B_urelu = [Buf(f"urelu{i}") for i in range(2)]
        D_yo = k.new_dsem("yo")
        stg_f = [tmpf, X1[:, NXB - 1, NSUB - 1, :]]
        B_stgf = [B_tmpf, B_X1[NXB - 1][NSUB - 1]]
        stg_b = [uT[:, 0:16, :].rearrange("p a b -> p (a b)"), uT[:, 16:32, :].rearrange("p a b -> p (a b)")]
        assert TT == 256

        def W_stgb(j):
            return [B_stgb[j]] + B_uT[16 * j:16 * j + 16]
        uflat = uT[:, :, :].rearrange("p a b -> p (a b)")

        def scr(lo, hi, dt=None):
            v = uflat[:, lo:hi]
            return v.bitcast(dt) if dt is not None else v
        hn = [scr(0, 1024, F32), scr(1024, 2048, F32)]; B_hn = [Buf("hn0"), Buf("hn1")]; D_hn = [k.new_dsem("hn0"), k.new_dsem("hn1")]
        hnb = scr(2048, 2560); B_hnb = Buf("hnb")
        h0T = scr(2560, 3072); B_h0T = Buf("h0T")
        xdm = scr(3072, 3584); B_xdm = Buf("xdm")
        ckf = scr(3584, 3840, F32); cvf = scr(3840, 4096, F32); B_ckf = Buf("ckf"); B_cvf = Buf("cvf"); D_ckv = k.new_dsem("ckv")
        ckb = scr(4096, 4224); cvb = scr(4224, 4352); kcT = scr(4352, 4480); B_ckb = Buf("ckb"); B_cvb = Buf("cvb"); B_kcT = Buf("kcT")
        dmk = scr(4480, 6784, F32); B_dmk = Buf("dmk")
        cms = scr(6784, 7552, F32).rearrange("p (a b) -> p a b", a=3); B_cms = Buf("cms")
        yoT = scr(7552, 8064, F32); B_yoT = Buf("yoT")
        dcol = scr(8064, 8192, F32).rearrange("p (a b) -> p a b", a=4); B_dcol = Buf("dcol")
        smk = sb("smk_sb", [128, NSEQ], F32); sink16 = sb("sink16_sb", [16, 2], F32); B_smk = Buf("smk")
        fdummy = sb("fdummy", [128, 2], F32)
        qTs = sb("qTs", [64, 2, NSEQ, 16], BF16); B_qTs = Buf("qTs")
        knT1 = sb("knT1", [64, 64], BF16)
        kcT2 = sb("kcT2", [64, 2, 128], BF16)
        SCR_BUFS = B_hn + [B_hnb, B_h0T, B_xdm, B_ckf, B_cvf, B_ckb, B_cvb, B_kcT, B_dmk, B_cms, B_yoT, B_dcol]
        osb = X1[:, NXB - 1, 0, :].bitcast(BF16)

        def fence():
            allb = SCR_BUFS + B_stgb + B_uT
            k.op("dve", lambda e: e.memset(fdummy[:, 0:1], 0.0), R=[], W=allb)
        D_misc = k.new_dsem("misc")
        B_dummy = Buf("dram_out", multi=True)

        ps = [es.enter_context(nc.psum_tensor(f"ps{i}", [128, 512], F32)) for i in range(8)]
        B_ps = [Buf(f"ps{i}") for i in range(8)]
        B_ps7h = [Buf("ps7a"), Buf("ps7b")]

        def psb(i, dt=BF16):
            return ps[i][:, :].bitcast(dt)

        def bc_row(ap_row, n):
            return ap_row.broadcast_to([128, n])

        k.dma("sp", [(gb[:, 0, :], bc_row(ln_in_g, D)), (gb[:, 1, :], bc_row(ln_in_b, D)),
                     (gb[:, 2, :], bc_row(ln1_g, D)), (gb[:, 3, :], bc_row(ln1_b, D)),
                     (gb[:, 4, :], bc_row(ln2_g, D)), (gb[:, 5, :], bc_row(ln2_b, D)),
                     (normw[:, :], bc_row(ssm_norm_w, 512)),
                     (smallc[:, 0, :], bc_row(dt_bias, NH)), (smallc[:, 1, :], bc_row(A_log, NH)),
                     (smallc[:, 2, :], bc_row(D_skip, NH)), (smallc[:, 3, :], bc_row(sinks_p, NQ)),
                     (cm[:, :, :], cmask), (am[:, 0, :], m0), (am[:, 1, :], m2), (flg[:, :], flag)],
              R=[], W=[B_gb, B_small, B_cm, B_am], dsem=D_c)
        k.dma("sp", [(cw[:, :, j], conv_w[j, :].rearrange("(c p) -> p c", p=128)) for j in range(4)]
              + [(cw[:, :, 4], conv_b[0, :].rearrange("(c p) -> p c", p=128))],
              R=[], W=[B_cw], dsem=k.site("cw"))
        k.op("act", lambda e: e.activation(out=smallc[:, 1, :], in_=smallc[:, 1, :], func=AF.Exp), R=[B_small], W=[B_small])
        k.op("dve", lambda e: e.tensor_scalar(out=smallc[:, 1, :], in0=smallc[:, 1, :], scalar1=-1.0, scalar2=None, op0=ALU.mult), R=[B_small], W=[B_small])
        k.op("dve", lambda e: e.tensor_copy(out=identb[:, :], in_=cm[:, 3, :]), R=[B_cm], W=[B_cm])
        identf = cm[:, 3, :]

        cast_engs = ["dve", "pool", "act"]
        state = {"stg": 0, "ce": 0}

        def cast(out_ap, in_ap, R, W):
            ek = cast_engs[state["ce"] % 3]
            state["ce"] += 1
            if ek == "act":
                k.op("act", lambda e: e.copy(out=out_ap, in_=in_ap), R=R, W=W)
            else:
                k.op(ek, lambda e: e.tensor_copy(out=out_ap, in_=in_ap), R=R, W=W)

        def load_stage(dram_ap_3d, shape3):
            i = state["stg"] % 2
            state["stg"] += 1
            view = stg_f[i][:, 0:shape3[0] * shape3[1]].rearrange("p (a b) -> p a b", a=shape3[0])
            k.dma("sp", [(view, dram_ap_3d)], R=[], W=[B_stgf[i]], dsem=D_stgf[i])
            return i, view

        D_wprep = k.new_dsem("wprep")
        k.dma("pool", [(w_in_b[:, kk, :], w_in[kk * 128:(kk + 1) * 128, :]) for kk in range(8)], R=[], W=[B_win], dsem=k.new_dsem("wp_in"))
        def late_weight_prep():
            k.dma("pool", [(wout_s[q_, :, :].rearrange("p (a n) -> p a n", a=2), w_out[q_ * 256:(q_ + 1) * 256, :].rearrange("(a p) n -> p a n", p=128)) for q_ in range(4)],
                  R=[], W=[B_wout_s], dsem=k.new_dsem("wp_out"))
            k.dma("pool", [(wup_s[g, :, fc * 1024:(fc + 1) * 1024].rearrange("p (kk f) -> p kk f", kk=8), w_up[:, (g * 4 + fc) * 128:(g * 4 + fc + 1) * 128].rearrange("(kk p) f -> p kk f", p=128))
                           for g in range(8) for fc in range(4)], R=[], W=[B_wup_s], dsem=k.new_dsem("wp_up"))
            k.dma("pool", [(wdn_s[n, g, :, :].rearrange("p (fk d) -> p fk d", fk=8), w_down[g * 1024:(g + 1) * 1024, n * 512:(n + 1) * 512].rearrange("(fk p) d -> p fk d", p=128))
                           for n in range(2) for g in range(4)], R=[], W=[B_wdn_s], dsem=k.new_dsem("wp_dn"))

        sbi = 0

        def compute_ada(c_rows_ap, T):
            k.dma("sp", [(xp[:T, :], c_rows_ap)], R=[], W=[B_xp], dsem=k.site("ada_c"))
            k.op("act", lambda e: e.activation(out=hb[:T, :], in_=xp[:T, :], func=AF.Silu), R=[B_xp], W=[B_hb])
            for kk in range(8):
                k.op("pe", lambda e, kk=kk: e.transpose(out=psb(0)[:, kk * 128:kk * 128 + T], in_=hb[:T, kk * 128:(kk + 1) * 128], identity=identb[:T, :T]),
                     R=[B_hb, B_cm], W=[B_ps[0]])
            k.op("act", lambda e: e.copy(out=cT[:, :, :T], in_=psb(0)[:, 0:1024].rearrange("p (a b) -> p a b", a=8)[:, :, :T]), R=[B_ps[0]], W=[B_cT])
            k.dma("sp", [(ada[:T, :, :].rearrange("p a b -> p (a b)"), b_ada.broadcast_to([T, 6 * D]))], R=[], W=[B_ada], dsem=k.site("ada_b"))
            for nn in range(12):
                pb = 1 + (nn % 2)
                j = sbi_box[0] % 2
                sbi_box[0] += 1
                wv = stg_b[j].rearrange("p (a n) -> p a n", a=8)
                k.dma("pool", [(wv, w_ada[:, nn * 512:(nn + 1) * 512].rearrange("(a p) n -> p a n", p=128))], R=[], W=W_stgb(j), dsem=D_stgb[j])
                for kk in range(8):
                    k.op("pe", lambda e, kk=kk, wv=wv, pb=pb: e.matmul(out=ps[pb][:T, :], lhsT=cT[:, kk, :T], rhs=wv[:, kk, :], start=(kk == 0), stop=(kk == 7)),
                         R=[B_cT, B_stgb[j]], W=[B_ps[pb]])
                av = ada[:T, :, :].rearrange("p a b -> p (a b)")[:, nn * 512:(nn + 1) * 512]
                k.op("dve", lambda e, av=av, pb=pb: e.tensor_tensor(out=av, in0=ps[pb][:T, :], in1=av, op=ALU.add), R=[B_ps[pb], B_ada], W=[B_ada])
            for s_, gi_ in ((1, 0), (4, 2)):
                k.op("pool", lambda e, s_=s_: e.tensor_scalar(out=ada[:T, s_, :], in0=ada[:T, s_, :], scalar1=1.0, scalar2=None, op0=ALU.add), R=[B_ada], W=[B_ada])
                k.op("dve", lambda e, s_=s_, gi_=gi_: e.tensor_tensor(out=tmpf[:T, :], in0=ada[:T, s_, :], in1=gb[:T, gi_ + 1, :], op=ALU.mult), R=[B_ada, B_gb], W=[B_tmpf])
                k.op("dve", lambda e, s_=s_: e.tensor_tensor(out=ada[:T, s_ - 1, :], in0=ada[:T, s_ - 1, :], in1=tmpf[:T, :], op=ALU.add), R=[B_ada, B_tmpf], W=[B_ada])
                k.op("pool", lambda e, s_=s_, gi_=gi_: e.tensor_tensor(out=ada[:T, s_, :], in0=ada[:T, s_, :], in1=gb[:T, gi_, :], op=ALU.mult), R=[B_ada, B_gb], W=[B_ada])

        sbi_box = [sbi]

        statsP = sb("statsP", [128, 2, 6], F32); B_statsP = Buf("statsP")
        mvP = sb("mvP", [128, 8], F32); B_mvP = Buf("mvP")

        def layer_norm_rows(src, B_src, T, gi, dst, B_dst, scr=None):
            if scr is None:
                stats, B_stats, mv, B_mv = stats0, B_stats0, mv0, B_mv0
            else:
                stats, B_stats, mv, B_mv = scr
            for c in range(2):
                k.op("dve", lambda e, c=c: e.bn_stats(out=stats[:T, c, :], in_=src[:T, c * 512:(c + 1) * 512]), R=[B_src], W=[B_stats])
            k.op("dve", lambda e: e.bn_aggr(out=mv[:T, 0:2], in_=stats[:T, :, :].rearrange('p a b -> p (a b)')), R=[B_stats], W=[B_mv])
            k.op("dve", lambda e: e.tensor_scalar(out=mv[:T, 2:3], in0=mv[:T, 1:2], scalar1=LN_EPS, scalar2=None, op0=ALU.add), R=[B_mv], W=[B_mv])
            k.op("act", lambda e: e.activation(out=mv[:T, 2:3], in_=mv[:T, 2:3], func=AF.Ln), R=[B_mv], W=[B_mv])
            k.op("act", lambda e: e.activation(out=mv[:T, 2:3], in_=mv[:T, 2:3], func=AF.Exp, scale=-0.5), R=[B_mv], W=[B_mv])
            k.op("dve", lambda e: e.tensor_scalar(out=mv[:T, 3:4], in0=mv[:T, 0:1], scalar1=mv[:T, 2:3], scalar2=-1.0, op0=ALU.mult, op1=ALU.mult), R=[B_mv], W=[B_mv])
            k.op("act", lambda e: e.activation(out=dst[:T, :], in_=src[:T, :], func=AF.Identity, bias=mv[:T, 3:4], scale=mv[:T, 2:3]), R=[B_src, B_mv], W=[B_dst])
            if gi is not None:
                affine(dst, B_dst, T, gi, dst, B_dst, ("dve", "pool"))

        def affine(src, B_src, T, gi, dst, B_dst, engs):
            k.op(engs[0], lambda e: e.tensor_tensor(out=dst[:T, :], in0=src[:T, :], in1=gb[:T, gi, :], op=ALU.mult), R=[B_src, B_gb], W=[B_dst])
            k.op(engs[1], lambda e: e.tensor_tensor(out=dst[:T, :], in0=dst[:T, :], in1=gb[:T, gi + 1, :], op=ALU.add), R=[B_dst, B_gb], W=[B_dst])

        def modulate_T(src, B_src, T, si, dstT, B_dstT, col0, hb=None, B_hb=None, tmp=None, B_tmp=None, pbk=0):
            if hb is None:
                hb, B_hb = ymix, B_ymix
            if tmp is None:
                tmp, B_tmp = tmpf, B_tmpf
            k.op("dve", lambda e: e.tensor_tensor(out=tmp[:T, :], in0=src[:T, :], in1=ada[:T, si, :], op=ALU.mult), R=[B_src, B_ada], W=[B_tmp])
            k.op("dve", lambda e: e.tensor_tensor(out=hb[:T, :], in0=tmp[:T, :], in1=ada[:T, si - 1, :], op=ALU.add), R=[B_tmp, B_ada], W=[B_hb])
            for kk in range(8):
                k.op("pe", lambda e, kk=kk: e.transpose(out=psb(pbk)[:, kk * 128:kk * 128 + T], in_=hb[:T, kk * 128:(kk + 1) * 128], identity=identb[:T, :T]),
                     R=[B_hb, B_cm], W=[B_ps[pbk]])
            k.op("act", lambda e: e.copy(out=dstT[:, :, col0:col0 + T], in_=psb(pbk)[:, 0:1024].rearrange("p (a b) -> p a b", a=8)[:, :, :T]), R=[B_ps[pbk]], W=[B_dstT])

        def bc8(ap_t8, T, n):
            return ap_t8.unsqueeze(2).broadcast_to([T, 8, n])

        def mixer_tile(x_rows, rope_rows, T, par, mode, mask_ap, sub, last_conv_out=False, smp=False, xb=0):
            full = mode == "full"
            xb = xb % NXB
            X1v = X1[:, xb]; B_X1v = B_X1[xb]; h2Tv = h2T[:, xb]; B_h2Tv = B_h2T[xb]
            if smp:
                U_ap = cms[:, 0, :]; SL_ap = cms[:, 1, :]; BD_ap = cms[:, 2, :]; CAU_ap = cms[:, 0, :]; B_cmx = B_cms
            else:
                U_ap = cm[:, 0, :]; SL_ap = cm[:, 1, :]; BD_ap = cm[:, 2, :]; CAU_ap = cm[:, 4, :]; B_cmx = B_cm
            xbcS = xbcT[:, :, :].rearrange("p a b -> p (a b)")[:, 0:896].rearrange("p (c s u) -> p c s u", c=8, s=16)
            kv = mode in ("full", "statekv")
            k.dma("sp", [(x_in[:T, :], x_rows)], R=[], W=[B_xin], dsem=D_xin)
            if kv:
                k.dma("sp", [(rope[:T, :], rope_rows)], R=[], W=[B_rope], dsem=D_rope)
            xp = xpb[:, par % 2, :]; B_xp = B_xpb[par % 2]
            hT = hTP; B_hT = B_hTP
            layer_norm_rows(x_in, B_xin, T, None, x_in, B_xin, scr=(statsP, B_statsP, mvP, B_mvP))
            modulate_T(x_in, B_xin, T, 1, hT, B_hT, 0, hbP, B_hbP, xp, B_xp, 5)
            if full:
                affine(x_in, B_xin, T, 0, xp, B_xp, ("pool", "pool"))
            yield
            nch = 6 if mode == 'state' else 8
            for c in range(nch):
                pb = 6 + c // 4
                for kk in range(8):
                    k.op("pe", lambda e, c=c, kk=kk, pb=pb: e.matmul(out=ps[pb][:, (c % 4) * 128:(c % 4) * 128 + T], lhsT=w_in_b[:, kk, C_XBC + c * 128:C_XBC + (c + 1) * 128],
                                                                   rhs=hT[:, kk, :T], start=(kk == 0), stop=(kk == 7)), R=[B_win, B_hT], W=[B_ps[pb]])
            n2 = nch - 4
            if not smp:
                k.op("act", lambda e: e.copy(out=xbcT[:, 0:4, 3:3 + T], in_=ps[6][:, :].rearrange("p (a b) -> p a b", a=4)[:, :, :T]), R=[B_ps[6]], W=[B_xbcT])
                k.op("dve", lambda e: e.tensor_copy(out=xbcT[:, 4:4 + n2, 3:3 + T], in_=ps[7][:, :].rearrange("p (a b) -> p a b", a=4)[:, 0:n2, :T]), R=[B_ps[7]], W=[B_xbcT])
            else:
                for hh in range(2):
                    k.op("act" if hh == 0 else "dve", lambda e, hh=hh: (e.copy if hh == 0 else e.tensor_copy)(out=xbcS[:, hh * 4:(hh + 1) * 4, :, 3:7],
                         in_=ps[6 + hh][:, :].rearrange("p (a b) -> p a b", a=4)[:, :, :64].rearrange("p a (s t) -> p a s t", t=4)), R=[B_ps[6 + hh]], W=[B_xbcT])
                k.dma("sp", [(x_in[:48, :], sconv)], R=[], W=[B_xin], dsem=D_xin)
                for c in range(8):
                    k.op("pe", lambda e, c=c: e.transpose(out=ps[4][:, c * 48:(c + 1) * 48], in_=x_in[:48, c * 128:(c + 1) * 128], identity=identf[:48, :48]), R=[B_xin, B_cm], W=[B_ps[4]])
                k.op("act", lambda e: e.copy(out=xbcS[:, :, :, 0:3], in_=ps[4][:, 0:384].rearrange("p (c s u) -> p c s u", c=8, s=16)), R=[B_ps[4]], W=[B_xbcT])
                for n in range(2):
                    for kk in range(8):
                        k.op("pe", lambda e, n=n, kk=kk: e.matmul(out=ps[1 + n][:T, :], lhsT=hT[:, kk, :T], rhs=w_in_b[:, kk, C_XBC + n * 512:C_XBC + (n + 1) * 512], start=(kk == 0), stop=(kk == 7)),
                             R=[B_win, B_hT], W=[B_ps[1 + n]])
                k.op("act", lambda e: e.copy(out=tmpf[:T, 0:512], in_=ps[1][:T, :]), R=[B_ps[1]], W=[B_tmpf])
                k.op("dve", lambda e: e.tensor_copy(out=tmpf[:T, 512:1024], in_=ps[2][:T, :]), R=[B_ps[2]], W=[B_tmpf])
                k.dma("sp", [(conv_s[:, t_ - 1, :], tmpf[t_:64:4, :]) for t_ in range(1, 4)], R=[B_tmpf], W=[B_dummy], dsem=k.site("conv_s"))
            c0 = C_K if kv else C_DT
            ncol = C_END - c0
            for kk in range(8):
                k.op("pe", lambda e, kk=kk: e.matmul(out=ps[3][:T, 0:ncol], lhsT=hT[:, kk, :T], rhs=w_in_b[:, kk, c0:C_END], start=(kk == 0), stop=(kk == 7)),
                     R=[B_win, B_hT], W=[B_ps[3]])
            dtcol = C_DT - c0
            if full:
                for kk in range(8):
                    k.op("pe", lambda e, kk=kk: e.matmul(out=ps[4][:T, :], lhsT=hT[:, kk, :T], rhs=w_in_b[:, kk, C_Z:C_Z + 512], start=(kk == 0), stop=(kk == 7)),
                         R=[B_win, B_hT], W=[B_ps[4]])
                for kk in range(8):
                    k.op("pe", lambda e, kk=kk: e.matmul(out=ps[7][:T, :], lhsT=hT[:, kk, :T], rhs=w_in_b[:, kk, C_Q:C_Q + 512], start=(kk == 0), stop=(kk == 7)),
                         R=[B_win, B_hT], W=[B_ps[7]])
            yield "P2"
            DT, A_, ACS, EACS, DEC, EATOT, DTDEC, TMP = [sm8[:, i, :] for i in range(8)]
            k.op("dve", lambda e: e.tensor_tensor(out=DT[:T], in0=ps[3][:T, dtcol:dtcol + 8], in1=smallc[:T, 0, :], op=ALU.add), R=[B_ps[3], B_small], W=[B_sm8])
            k.op("act", lambda e: e.activation(out=DT[:T], in_=DT[:T], func=AF.Exp), R=[B_sm8], W=[B_sm8])
            k.op("act", lambda e: e.activation(out=DT[:T], in_=DT[:T], func=AF.Ln, bias=1.0), R=[B_sm8], W=[B_sm8])
            k.op("dve", lambda e: e.tensor_tensor(out=A_[:T], in0=DT[:T], in1=smallc[:T, 1, :], op=ALU.mult), R=[B_sm8, B_small], W=[B_sm8])
            if kv:
                k.op("act", lambda e: e.copy(out=kvs[:T, :], in_=ps[3][:T, 0:256]), R=[B_ps[3]], W=[B_kvs])
            k.op("pe", lambda e: e.matmul(out=ps[3][:T, 0:8], lhsT=U_ap[:T, :T], rhs=A_[:T], start=True, stop=True), R=[B_cmx, B_sm8], W=[B_ps[3]])
            NB = T if smp else 128
            k.op("pe", lambda e: e.matmul(out=ps[3][:NB, 8:16], lhsT=BD_ap[:T, :NB], rhs=A_[:T], start=True, stop=True), R=[B_cmx, B_sm8], W=[B_ps[3]])
            k.op("dve", lambda e: e.tensor_copy(out=ACS[:T], in_=ps[3][:T, 0:8]), R=[B_ps[3]], W=[B_sm8])
            k.op("act", lambda e: e.activation(out=EACS[:T], in_=ps[3][:T, 0:8], func=AF.Exp), R=[B_ps[3]], W=[B_sm8])
            k.op("dve", lambda e: e.tensor_tensor(out=TMP[:T], in0=ps[3][:T, 8:16], in1=ACS[:T], op=ALU.subtract), R=[B_ps[3], B_sm8], W=[B_sm8])
            k.op("act", lambda e: e.activation(out=DEC[:T], in_=TMP[:T], func=AF.Exp), R=[B_sm8], W=[B_sm8])
            if not smp:
                k.op("act", lambda e: e.activation(out=EATOT, in_=ps[3][:, 8:16], func=AF.Exp), R=[B_ps[3]], W=[B_sm8])
            k.op("dve", lambda e: e.tensor_tensor(out=DTDEC[:T], in0=DT[:T], in1=DEC[:T], op=ALU.mult), R=[B_sm8], W=[B_sm8])
            yield
            if smp:
                def cwb(j):
                    return cw[:, 0:8, j:j + 1].unsqueeze(3).broadcast_to([128, 8, 16, 4])

                def v4(ap3):
                    return ap3.rearrange("p c (s t) -> p c s t", t=4)
                k.op("pool", lambda e: e.tensor_tensor(out=v4(xacc[:, 0:8, :64]), in0=xbcS[:, :, :, 0:4], in1=cwb(0), op=ALU.mult), R=[B_xbcT, B_cw], W=[B_xacc])
                for j in range(1, 4):
                    k.op("pool", lambda e, j=j: e.tensor_tensor(out=v4(big1[:, 0:8, :64]), in0=xbcS[:, :, :, j:j + 4], in1=cwb(j), op=ALU.mult), R=[B_xbcT, B_cw], W=[B_big1])
                    k.op("pool", lambda e: e.tensor_tensor(out=xacc[:, 0:8, :64], in0=xacc[:, 0:8, :64], in1=big1[:, 0:8, :64], op=ALU.add), R=[B_xacc, B_big1], W=[B_xacc])
                k.op("pool", lambda e: e.tensor_tensor(out=v4(xacc[:, 0:8, :64]), in0=v4(xacc[:, 0:8, :64]), in1=cwb(4), op=ALU.add), R=[B_xacc, B_cw], W=[B_xacc])

            if not smp:
                nD = nch - 2
                for b_ in B_xc:
                    b_.w = {}
                    b_.r = {}
                    for dd in (B_xacc.w, B_xacc.r):
                        for s__, v__ in dd.items():
                            if b_.w.get(s__, 0) < v__:
                                b_.w[s__] = v__
                for j in range(4):
                    for c in range(nD):
                        if j == 0:
                            k.op("dve", lambda e, c=c: e.tensor_scalar(out=xacc[:, c, :T], in0=xbcT[:, c, 0:T], scalar1=cw[:, c, 0:1], scalar2=cw[:, c, 4:5], op0=ALU.mult, op1=ALU.add),
                                 R=[B_xbcT, B_cw], W=[B_xc[c]])
                        else:
                            k.op("dve", lambda e, c=c, j=j: e.scalar_tensor_tensor(out=xacc[:, c, :T], in0=xbcT[:, c, j:j + T], scalar=cw[:, c, j:j + 1], in1=xacc[:, c, :T], op0=ALU.mult, op1=ALU.add),
                                 R=[B_xbcT, B_cw], W=[B_xc[c]])

                def cwb(j):
                    return cw[:, nD:nch, j:j + 1].broadcast_to([128, nch - nD, T])
                k.op("pool", lambda e: e.tensor_tensor(out=xacc[:, nD:nch, :T], in0=xbcT[:, nD:nch, 0:T], in1=cwb(0), op=ALU.mult), R=[B_xbcT, B_cw], W=[B_xc[8]])
                for j in range(1, 4):
                    k.op("pool", lambda e, j=j: e.tensor_tensor(out=big1[:, nD:nch, :T], in0=xbcT[:, nD:nch, j:j + T], in1=cwb(j), op=ALU.mult), R=[B_xbcT, B_cw], W=[B_big1])
                    k.op("pool", lambda e: e.tensor_tensor(out=xacc[:, nD:nch, :T], in0=xacc[:, nD:nch, :T], in1=big1[:, nD:nch, :T], op=ALU.add), R=[B_big1], W=[B_xc[8]])
                k.op("pool", lambda e: e.tensor_tensor(out=xacc[:, nD:nch, :T], in0=xacc[:, nD:nch, :T], in1=cwb(4), op=ALU.add), R=[B_cw], W=[B_xc[8]])
            if last_conv_out:
                pass
            if not smp:
                k.op("pool", lambda e: e.tensor_copy(out=xbcT[:, :, 0:3], in_=xbcT[:, :, T:T + 3]), R=[B_xbcT], W=[B_xbcT])
            if full:
                k.op("act", lambda e: e.activation(out=qs[:T, :], in_=ps[7][:T, :], func=AF.Copy, scale=0.125), R=[B_ps[7]], W=[B_qs])
                k.op("act", lambda e: e.activation(out=zs[:T, :], in_=ps[4][:T, :], func=AF.Silu), R=[B_ps[4]], W=[B_zs])
            k.op("act", lambda e: e.activation(out=xcT[:, :, :T], in_=xacc[:, 0:4, :T], func=AF.Silu), R=[B_xacc] + B_xc, W=[B_xcT])
            k.op("act", lambda e: e.activation(out=bcT[:, 0:n2, :T], in_=xacc[:, 4:4 + n2, :T], func=AF.Silu), R=[B_xacc] + B_xc, W=[B_bcT])
            yield
            for c in range(4):
                k.op("pe", lambda e, c=c: e.transpose(out=ps[4][:T, c * 128:(c + 1) * 128], in_=xcT[:, c, :T], identity=identf), R=[B_xcT, B_cm], W=[B_ps[4]])
            k.op("act", lambda e: e.copy(out=xtok[:T, :], in_=ps[4][:T, :]), R=[B_ps[4]], W=[B_xtok])
            for g in range(2):
                k.op("pe", lambda e, g=g: e.transpose(out=psb(5)[:T, g * 128:(g + 1) * 128], in_=bcT[:, g, :T], identity=identb[:, :]), R=[B_bcT, B_cm], W=[B_ps[5]])
            k.op("dve", lambda e: e.tensor_copy(out=btok[:T, :], in_=psb(5)[:T, 0:256]), R=[B_ps[5]], W=[B_btok])
            x3 = xtok[:T, :].rearrange("p (h d) -> p h d", h=8)
            k.op("dve", lambda e: e.tensor_tensor(out=xdec[:T, :].rearrange("p (h d) -> p h d", h=8), in0=x3, in1=bc8(DTDEC[:T], T, 64), op=ALU.mult), R=[B_xtok, B_sm8], W=[B_xdec])
            def chainA():
                if full:
                    k.op("pool", lambda e: e.tensor_tensor(out=xdt[:T, :].rearrange("p (h d) -> p h d", h=8), in0=x3, in1=bc8(DT[:T], T, 64), op=ALU.mult), R=[B_xtok, B_sm8], W=[B_xdt])
                    k.op("dve", lambda e: e.tensor_tensor(out=big1[:T, :, :T], in0=A_[:T].unsqueeze(2).broadcast_to([T, 8, T]),
                                                          in1=SL_ap[:T, :T].unsqueeze(1).broadcast_to([T, 8, T]), op=ALU.mult), R=[B_sm8, B_cmx], W=[B_big1])
                    for h in range(8):
                        pb = 1 + h // 4
                        k.op("pe", lambda e, h=h, pb=pb: e.matmul(out=ps[pb][:T, (h % 4) * 128:(h % 4) * 128 + T], lhsT=big1[:T, h, :T], rhs=U_ap[:T, :T], start=True, stop=True),
                             R=[B_big1, B_cmx], W=[B_ps[pb]])
                    for hh in range(2):
                        k.op("act", lambda e, hh=hh: e.activation(out=big2[:T, hh * 4:(hh + 1) * 4, :T], in_=ps[1 + hh][:T, :].rearrange("p (a b) -> p a b", a=4)[:, :, :T], func=AF.Exp),
                             R=[B_ps[1 + hh]], W=[B_big2])
                    yield
                    for g in range(2):
                        k.op("pe", lambda e, g=g: e.matmul(out=ps[4][:T, g * 128:g * 128 + T], lhsT=bcT[:, g, :T], rhs=bcT[:, 2 + g, :T], start=True, stop=True), R=[B_bcT], W=[B_ps[4]])
                    k.op("dve", lambda e: e.tensor_tensor(out=cbm[:T, :, :T], in0=ps[4][:T, 0:256].rearrange("p (a b) -> p a b", a=2)[:, :, :T],
                                                          in1=CAU_ap[:T, :T].unsqueeze(1).broadcast_to([T, 2, T]), op=ALU.mult), R=[B_ps[4], B_cmx], W=[B_cbm])
                    for g in range(2):
                        k.op("dve" if g == 0 else "pool", lambda e, g=g: e.tensor_tensor(out=Gb[:T, g * 4:(g + 1) * 4, :T], in0=big2[:T, g * 4:(g + 1) * 4, :T],
                                                                                        in1=cbm[:T, g, :T].unsqueeze(1).broadcast_to([T, 4, T]), op=ALU.mult), R=[B_big2, B_cbm], W=[B_G])
                    yield
                    for h in range(8):
                        k.op("pe", lambda e, h=h: e.matmul(out=ps[3][:T, h * 64:(h + 1) * 64], lhsT=Gb[:T, h, :T], rhs=xdt[:T, h * 64:(h + 1) * 64], start=True, stop=True),
                             R=[B_G, B_xdt], W=[B_ps[3]])
                    if smp and 'S' not in DBG:
                        yield from sample_state_loop()
                    for g in range(0 if smp else 2):
                        k.op("pe", lambda e, g=g: e.matmul(out=ps[4][:T, g * 256:(g + 1) * 256], lhsT=bcT[:, 2 + g, :T], rhs=Hbf[:, g * 256:(g + 1) * 256], start=True, stop=True),
                             R=[B_bcT, B_Hb], W=[B_ps[4]])
                    y3 = ysb[:T, :].rearrange("p (h d) -> p h d", h=8)
                    k.op("dve", lambda e: e.tensor_tensor(out=y3, in0=ps[4][:T, :].rearrange("p (h d) -> p h d", h=8), in1=bc8(EACS[:T], T, 64), op=ALU.mult), R=[B_ps[4], B_sm8], W=[B_y])
                    k.op("dve", lambda e: e.tensor_tensor(out=ysb[:T, :], in0=ysb[:T, :], in1=ps[3][:T, :], op=ALU.add), R=[B_y, B_ps[3]], W=[B_y])
                    k.op("pool", lambda e: e.tensor_tensor(out=tmpf[:T, 0:512].rearrange("p (h d) -> p h d", h=8), in0=x3, in1=bc8(smallc[:T, 2, :], T, 64), op=ALU.mult), R=[B_xtok, B_small], W=[B_tmpf])
                    k.op("dve", lambda e: e.tensor_tensor(out=ysb[:T, :], in0=ysb[:T, :], in1=tmpf[:T, 0:512], op=ALU.add), R=[B_y, B_tmpf], W=[B_y])
                yield
                for g in range(0 if smp else 2):
                    k.op("pe", lambda e, g=g: e.matmul(out=ps[1][:, g * 256:(g + 1) * 256], lhsT=btok[:T, g * 128:(g + 1) * 128], rhs=xdec[:T, g * 256:(g + 1) * 256], start=True, stop=True),
                         R=[B_btok, B_xdec], W=[B_ps[1]])
                if not smp:
                    k.op("dve", lambda e: e.tensor_tensor(out=Hst[:, :].rearrange("p (h d) -> p h d", h=8), in0=Hst[:, :].rearrange("p (h d) -> p h d", h=8), in1=bc8(EATOT, 128, 64), op=ALU.mult),
                         R=[B_H, B_sm8], W=[B_H])
                    k.op("dve", lambda e: e.tensor_tensor(out=Hst[:, :], in0=Hst[:, :], in1=ps[1][:, :], op=ALU.add), R=[B_H, B_ps[1]], W=[B_H])
                    k.op("act", lambda e: e.copy(out=Hbf[:, :], in_=Hst[:, :]), R=[B_H], W=[B_Hb])
                if full:
                    yield
                    k.op("dve", lambda e: e.tensor_tensor(out=ysb[:T, :], in0=ysb[:T, :], in1=zs[:T, :], op=ALU.mult), R=[B_y, B_zs], W=[B_y])
                    RS = rs8[:, :]
                    for g in range(2):
                        k.op("act", lambda e, g=g: e.activation(out=tmpf[:T, g * 256:(g + 1) * 256], in_=ysb[:T, g * 256:(g + 1) * 256], func=AF.Square, accum_out=RS[:T, g:g + 1]), R=[B_y], W=[B_tmpf, B_rs8])
                    k.op("dve", lambda e: e.tensor_scalar(out=RS[:T, 2:4], in0=RS[:T, 0:2], scalar1=1.0 / 256.0, scalar2=RMS_EPS, op0=ALU.mult, op1=ALU.add), R=[B_rs8], W=[B_rs8])
                    k.op("act", lambda e: e.activation(out=RS[:T, 2:4], in_=RS[:T, 2:4], func=AF.Ln), R=[B_rs8], W=[B_rs8])
                    k.op("act", lambda e: e.activation(out=RS[:T, 4:6], in_=RS[:T, 2:4], func=AF.Exp, scale=-0.5), R=[B_rs8], W=[B_rs8])
                    for g in range(2):
                        k.op("dve", lambda e, g=g: e.scalar_tensor_tensor(out=ymix[:T, g * 256:(g + 1) * 256], in0=ysb[:T, g * 256:(g + 1) * 256], scalar=RS[:T, 4 + g:5 + g], in1=normw[:T, g * 256:(g + 1) * 256],
                                                                          op0=ALU.mult, op1=ALU.mult), R=[B_y, B_rs8], W=[B_ymix])

            def chainB():
                if kv:
                    cosb = rope[:T, 0:8]; sinb = rope[:T, 8:16]
                    k3 = kvs[:T, 0:128].rearrange("p (h d) -> p h d", h=2)
                    rotary(k3, B_kvs, 2, T, cosb, sinb)
                    k.op("act", lambda e: e.copy(out=qb[:T, 512:640], in_=kvs[:T, 0:128]), R=[B_kvs], W=[B_qb])
                    k.op("dve", lambda e: e.tensor_copy(out=vb[par][:T, :], in_=kvs[:T, 128:256]), R=[B_kvs], W=[B_vb[par]])
                    k.op("pe", lambda e: e.transpose(out=psb(0)[:, 512:512 + T], in_=qb[:T, 512:640], identity=identb[:T, :T]), R=[B_qb, B_cm], W=[B_ps[0]])
                    k.op("act", lambda e: e.copy(out=kTb[par][:, :T], in_=psb(0)[:, 512:512 + T]), R=[B_ps[0]], W=[B_kT[par]])
                    if smp:
                        k.dma("sp", [(k_s[:, 124 + t_, :], kvs[t_:64:4, 0:128]) for t_ in range(4)] + [(v_s[:, 124 + t_, :], kvs[t_:64:4, 128:256]) for t_ in range(4)]
    , R=[B_kvs], W=[B_dummy], dsem=k.site("kv_s"))
                if full:
                    yield
                    q3 = qs[:T, :].rearrange("p (h d) -> p h d", h=8)
                    rotary(q3, B_qs, 8, T, rope[:T, 0:8], rope[:T, 8:16])
                    k.op("act", lambda e: e.copy(out=qb[:T, 0:512], in_=qs[:T, :]), R=[B_qs], W=[B_qb])
                    for c in range(4):
                        k.op("pe", lambda e, c=c: e.transpose(out=psb(0)[:, c * 128:c * 128 + T], in_=qb[:T, c * 128:(c + 1) * 128], identity=identb[:T, :T]), R=[B_qb, B_cm], W=[B_ps[0]])
                    k.op("act", lambda e: e.copy(out=qT[:, :, :T], in_=psb(0)[:, 0:512].rearrange("p (a b) -> p a b", a=4)[:, :, :T]), R=[B_ps[0]], W=[B_qT])
                    if smp and 'A' not in DBG:
                        yield from sample_attention(par)
                    for jg in range(0 if smp else 2):
                        yield
                        rows = slice(jg * 64, (jg + 1) * 64)
                        for c in range(4):
                            pb = (5, 7)[c // 2]
                            o0 = (c % 2) * 256
                            k.op("pe", lambda e, c=c, pb=pb, o0=o0: e.matmul(out=ps[pb][:T, o0:o0 + 128], lhsT=qT[rows, c, :T], rhs=kTb[1 - par][rows, :], start=True, stop=True),
                                 R=[B_qT, B_kT[1 - par]], W=[B_ps[pb]])
                            k.op("pe", lambda e, c=c, pb=pb, o0=o0: e.matmul(out=ps[pb][:T, o0 + 128:o0 + 256], lhsT=qT[rows, c, :T], rhs=kTb[par][rows, :], start=True, stop=True),
                                 R=[B_qT, B_kT[par]], W=[B_ps[pb]])
                        smv = smvb[:T, :, :]
                        for hh in range(2):
                            k.op("dve", lambda e, hh=hh: e.tensor_tensor(out=smv[:, hh * 2:(hh + 1) * 2, :], in0=ps[(5, 7)[hh]][:T, :].rearrange("p (a b) -> p a b", a=2),
                                                                         in1=mask_ap[:T, :].unsqueeze(1).broadcast_to([T, 2, 256]), op=ALU.add), R=[B_ps[(5, 7)[hh]], B_am], W=[B_smvb])
                        RMX = att8[:, 0, :]; M_ = att8[:, 1, :]; NM = att8[:, 2, :]; RSM = att8[:, 3, :]; ES = att8[:, 4, :]; DEN = att8[:, 5, :]; RDEN = att8[:, 6, :]
                        sk = smallc[:T, 3, jg * 4:(jg + 1) * 4]
                        k.op("dve", lambda e: e.tensor_reduce(out=RMX[:T, 0:4], in_=smv, axis=AX.X, op=ALU.max), R=[B_smvb], W=[B_att8])
                        k.op("dve", lambda e: e.tensor_tensor(out=M_[:T, 0:4], in0=RMX[:T, 0:4], in1=sk, op=ALU.max), R=[B_att8, B_small], W=[B_att8])
                        k.op("dve", lambda e: e.tensor_scalar(out=NM[:T, 0:4], in0=M_[:T, 0:4], scalar1=-1.0, scalar2=None, op0=ALU.mult), R=[B_att8], W=[B_att8])
                        Pv = Pb[:T, :, :]
                        for c in range(4):
                            k.op("act", lambda e, c=c: e.activation(out=Pv[:, c, :], in_=smv[:, c, :], func=AF.Exp, bias=NM[:T, c:c + 1], accum_out=RSM[:T, c:c + 1]), R=[B_smvb, B_att8], W=[B_Pb, B_att8])
                        k.op("dve", lambda e: e.tensor_tensor(out=ES[:T, 0:4], in0=sk, in1=M_[:T, 0:4], op=ALU.subtract), R=[B_att8, B_small], W=[B_att8])
                        k.op("act", lambda e: e.activation(out=ES[:T, 0:4], in_=ES[:T, 0:4], func=AF.Exp), R=[B_att8], W=[B_att8])
                        k.op("dve", lambda e: e.tensor_tensor(out=DEN[:T, 0:4], in0=ES[:T, 0:4], in1=RSM[:T, 0:4], op=ALU.add), R=[B_att8], W=[B_att8])
                        k.op("dve", lambda e: e.reciprocal(out=RDEN[:T, jg * 4:(jg + 1) * 4], in_=DEN[:T, 0:4]), R=[B_att8], W=[B_att8])
                        for c in range(4):
                            for kb in range(2):
                                k.op("pe", lambda e, c=c, kb=kb: e.transpose(out=psb(0)[:, (c * 2 + kb) * 128:(c * 2 + kb) * 128 + T], in_=Pv[:, c, kb * 128:(kb + 1) * 128], identity=identb[:T, :T]),
                                     R=[B_Pb, B_cm], W=[B_ps[0]])
                        k.op("act", lambda e: e.copy(out=PT[:, :, :T], in_=psb(0)[:, 0:1024].rearrange("p (a b) -> p a b", a=8)[:, :, :T]), R=[B_ps[0]], W=[B_PT])
                        for c in range(4):
                            head = c + 4 * jg
                            k.op("pe", lambda e, c=c, head=head: e.matmul(out=ps[6][:T, head * 64:(head + 1) * 64], lhsT=PT[:, c * 2, :T], rhs=vb[1 - par][:, jg * 64:(jg + 1) * 64], start=True, stop=False),
                                 R=[B_PT, B_vb[1 - par]], W=[B_ps[6]])
                            k.op("pe", lambda e, c=c, head=head: e.matmul(out=ps[6][:T, head * 64:(head + 1) * 64], lhsT=PT[:, c * 2 + 1, :T], rhs=vb[par][:, jg * 64:(jg + 1) * 64], start=False, stop=True),
                                 R=[B_PT, B_vb[par]], W=[B_ps[6]])
                    if not smp:
                      k.op("dve", lambda e: e.tensor_tensor(out=ymix[:T, 512:1024].rearrange("p (h d) -> p h d", h=8), in0=ps[6][:T, :].rearrange("p (h d) -> p h d", h=8),
                                                          in1=bc8(att8[:T, 6, :], T, 64), op=ALU.mult), R=[B_ps[6], B_att8], W=[B_ymix])
                yield

            yield "AB"
            if full and 'I' not in DBG:
                yield from zip_gens(chainA(), chainB())
            else:
                yield from chainA()
                yield from chainB()
            if not full:
                return
            yield "S"
            pend_o = [oload(c_) for c_ in range(NOS - 1)]
            for kk in range(8):
                k.op("pe", lambda e, kk=kk: e.transpose(out=psb(0)[:, kk * 128:kk * 128 + T], in_=ymix[:T, kk * 128:(kk + 1) * 128], identity=identb[:T, :T]), R=[B_ymix, B_cm], W=[B_ps[0]])
            k.op("act", lambda e: e.copy(out=ymixT[:, :, :T], in_=psb(0)[:, 0:1024].rearrange("p (a b) -> p a b", a=8)[:, :, :T]), R=[B_ps[0]], W=[B_ymixT])
            yield
            for c_ in range(16):
                kk, n = c_ // 2, c_ % 2
                i = pend_o.pop(0)
                if c_ + NOS - 1 < 16:
                    pend_o.append(oload(c_ + NOS - 1))
                k.op("pe", lambda e, n=n, kk=kk, i=i: e.matmul(out=ps[1 + n][:T, :], lhsT=ymixT[:, kk, :T], rhs=oslot[i][:, :], start=(kk == 0), stop=(kk == 7)),
                     R=[B_ymixT, B_oslot[i]], W=[B_ps[1 + n]])
            for n in range(2):
                sl = slice(n * 512, (n + 1) * 512)
                k.op("dve", lambda e, n=n, sl=sl: e.tensor_tensor(out=tmpf[:T, sl], in0=ps[1 + n][:T, :], in1=ada[:T, 2, sl], op=ALU.mult), R=[B_ps[1 + n], B_ada], W=[B_tmpf])
            k.op("dve", lambda e: e.scalar_tensor_tensor(out=X1v[:T, sub, :], in0=xp[:T, :], scalar=ALPHA, in1=tmpf[:T, :], op0=ALU.mult, op1=ALU.add), R=[B_xp, B_tmpf], W=[B_X1v[sub]])
            yield
            layer_norm_rows(X1v[:, sub, :], B_X1v[sub], T, None, X1v[:, sub, :], B_X1v[sub])
            yield
            modulate_T(X1v[:, sub, :], B_X1v[sub], T, 4, h2Tv, B_h2Tv[sub], sub * 128)
            yield
            affine(X1v[:, sub, :], B_X1v[sub], T, 2, X1v[:, sub, :], B_X1v[sub], ("pool", "pool"))

        def sample_state_loop():
            T = 64
            A_ = sm8[:, 1, :]
            k.op("dve", lambda e: e.tensor_copy(out=tmpf[:T, 0:512].rearrange("p (h d) -> p h d", h=8), in_=bc8(A_[:T], T, 64)), R=[B_sm8], W=[B_tmpf])
            for j in range(4):
                k.op("pe", lambda e, j=j: e.matmul(out=ps[2][:, 256 + j * 16:256 + (j + 1) * 16], lhsT=tmpf[:T, j * 128:(j + 1) * 128], rhs=smk[:T, :], start=True, stop=True),
                     R=[B_tmpf, B_smk], W=[B_ps[2]])
            k.op("act", lambda e: e.activation(out=dcol[:, :, :], in_=ps[2][:, 256:320].rearrange("p (a b) -> p a b", a=4), func=AF.Exp), R=[B_ps[2]], W=[B_dcol])
            for sq in range(NSEQ):
                sl_ = sq % 2
                k.dma("sp", [(hn[sl_].rearrange("p (j n) -> p j n", j=4), sssm[sq].rearrange("(j p) n -> p j n", p=128))], R=[], W=[B_hn[sl_]], dsem=D_hn[sl_])
                k.op("act", lambda e, sl_=sl_: e.copy(out=hnb, in_=hn[sl_]), R=[B_hn[sl_]], W=[B_hnb])
                for j in range(4):
                    k.op("pe", lambda e, j=j: e.transpose(out=psb(0)[:, j * 128:(j + 1) * 128], in_=hnb[:, j * 128:(j + 1) * 128], identity=identb[:, :]), R=[B_hnb, B_cm], W=[B_ps[0]])
                k.op("dve", lambda e: e.tensor_copy(out=h0T, in_=psb(0)[:, 0:512]), R=[B_ps[0]], W=[B_h0T])
                for j in range(4):
                    k.op("pe", lambda e, j=j, sq=sq: e.matmul(out=ps[2][:, j * 64 + 4 * sq:j * 64 + 4 * sq + 4], lhsT=h0T[:, j * 128:(j + 1) * 128], rhs=bcT[:, 2 + j // 2, 4 * sq:4 * sq + 4], start=True, stop=True),
                         R=[B_h0T, B_bcT], W=[B_ps[2]])
                k.op("pool", lambda e, sq=sq: e.tensor_scalar(out=xdm[:T, :], in0=xdec[:T, :], scalar1=smk[:T, sq:sq + 1], scalar2=None, op0=ALU.mult), R=[B_xdec, B_smk], W=[B_xdm])
                pb = 1
                for j in range(4):
                    k.op("pe", lambda e, j=j, pb=pb: e.matmul(out=ps[pb][:, j * 128:(j + 1) * 128], lhsT=xdm[:T, j * 128:(j + 1) * 128], rhs=btok[:T, (j // 2) * 128:(j // 2 + 1) * 128], start=True, stop=True),
                         R=[B_xdm, B_btok], W=[B_ps[pb]])
                h3 = hn[sl_].rearrange("p (j n) -> p j n", j=4)
                k.op("dve", lambda e, h3=h3, sq=sq: e.tensor_tensor(out=h3, in0=h3, in1=dcol[:, :, sq:sq + 1].broadcast_to([128, 4, 128]), op=ALU.mult), R=[B_hn[sl_], B_dcol], W=[B_hn[sl_]])
                k.op("dve", lambda e, sl_=sl_, pb=pb: e.tensor_tensor(out=hn[sl_], in0=hn[sl_], in1=ps[pb][:, :], op=ALU.add), R=[B_hn[sl_], B_ps[pb]], W=[B_hn[sl_]])
                k.dma("sp", [(ssm_s[sq].rearrange("(j p) n -> p j n", p=128), h3)], R=[B_hn[sl_]], W=[B_dummy], dsem=D_hn[sl_])
                yield
            k.op("act", lambda e: e.copy(out=yoT, in_=ps[2][:, 0:256]), R=[B_ps[2]], W=[B_yoT])
            for j in range(4):
                k.op("pe", lambda e, j=j: e.transpose(out=ps[4][:T, j * 128:(j + 1) * 128], in_=yoT[:, j * 64:(j + 1) * 64], identity=identf), R=[B_yoT, B_cm], W=[B_ps[4]])

        def sample_attention(par):
            T = 64
            mc = dmk[:16, 0:128]
            for c in range(4):
                k.op("pe", lambda e, c=c: e.transpose(out=psb(0)[:64, 640 + c * 64:640 + (c + 1) * 64], in_=qb[:T, c * 128 + 64:(c + 1) * 128], identity=identb[:T, :T]), R=[B_qb, B_cm], W=[B_ps[0]])
            k.op("pe", lambda e: e.transpose(out=psb(0)[:64, 896:960], in_=qb[:T, 576:640], identity=identb[:T, :T]), R=[B_qb, B_cm], W=[B_ps[0]])
            k.op("dve", lambda e: e.tensor_copy(out=qTs[:, 0, :, :].rearrange("p s (c t) -> p s c t", c=4), in_=qT[0:64, :, 0:64].rearrange("p c (s t) -> p s c t", t=4)), R=[B_qT], W=[B_qTs])
            k.op("dve", lambda e: e.tensor_copy(out=qTs[:, 1, :, :].rearrange("p s (c t) -> p s c t", c=4), in_=psb(0)[:64, 640:896].rearrange("p (c s t) -> p s c t", c=4, t=4)), R=[B_ps[0]], W=[B_qTs])
            k.op("act", lambda e: e.copy(out=knT1[:, :], in_=psb(0)[:64, 896:960]), R=[B_ps[0]], W=[B_qTs])

            for sq in range(NSEQ):
                k.dma("sp", [(ckf, ck[sq]), (cvf, cv[sq])], R=[], W=[B_ckf, B_cvf], dsem=D_ckv)
                if '1' not in DBG:
                    k.dma("sp", [(k_s[sq, 0:124, :], ckf[4:128, :]), (v_s[sq, 0:124, :], cvf[4:128, :])], R=[B_ckf, B_cvf], W=[B_dummy], dsem=k.site("kvc_s"))
                k.op("act", lambda e: e.copy(out=ckb, in_=ckf), R=[B_ckf], W=[B_ckb])
                k.op("pool", lambda e: e.tensor_copy(out=cvb, in_=cvf), R=[B_cvf], W=[B_cvb])
                for j in range(2):
                    k.op("pe", lambda e, j=j: e.transpose(out=psb(0)[:64, j * 128:(j + 1) * 128], in_=ckb[:, j * 64:(j + 1) * 64], identity=identb[:, :]), R=[B_ckb, B_cm], W=[B_ps[0]])
                k.op("act", lambda e: e.copy(out=kcT2[:, :, :].rearrange("p j x -> p (j x)"), in_=psb(0)[:64, 0:256]), R=[B_ps[0]], W=[B_kcT])
                pb = 5 + sq % 2
                if '4' in DBG:
                    continue
                for j in range(2):
                    lq = qTs[:, j, sq, :]
                    knew = kTb[par][0:64, 0:64] if j == 0 else knT1[:, :]
                    k.op("pe", lambda e, j=j, pb=pb, lq=lq: e.matmul(out=ps[pb][:16, j * 192:j * 192 + 128], lhsT=lq, rhs=kcT2[:, j, :], start=True, stop=True), R=[B_qTs, B_kcT], W=[B_ps[pb]])
                    k.op("pe", lambda e, j=j, pb=pb, lq=lq, knew=knew: e.matmul(out=ps[pb][:16, j * 192 + 128:j * 192 + 192], lhsT=lq, rhs=knew, start=True, stop=True), R=[B_qTs, B_kT[par]], W=[B_ps[pb]])
                smv = smvb[:16, 0:2, 0:192]
                psv = ps[pb][:16, 0:384].rearrange("p (j x) -> p j x", j=2)
                k.op("dve", lambda e, psv=psv: e.tensor_tensor(out=smv[:, :, 0:128], in0=psv[:, :, 0:128], in1=mc.unsqueeze(1).broadcast_to([16, 2, 128]), op=ALU.add), R=[B_ps[pb], B_dmk], W=[B_smvb])
                mn = dmk[:16, 128 + sq * 64:128 + (sq + 1) * 64]
                k.op("dve", lambda e, psv=psv, mn=mn: e.tensor_tensor(out=smv[:, :, 128:192], in0=psv[:, :, 128:192], in1=mn.unsqueeze(1).broadcast_to([16, 2, 64]), op=ALU.add), R=[B_ps[pb], B_dmk], W=[B_smvb])
                RMX = att8[:16, 0, 0:2]; M_ = att8[:16, 1, 0:2]; RSM = att8[:16, 3, 0:2]; ES = att8[:16, 4, 0:2]; DEN = att8[:16, 5, 0:2]; RDEN = att8[:16, 6, 0:2]
                k.op("dve", lambda e: e.tensor_reduce(out=RMX, in_=smv, axis=AX.X, op=ALU.max), R=[B_smvb], W=[B_att8])
                k.op("dve", lambda e: e.tensor_tensor(out=M_, in0=RMX, in1=sink16[:, :], op=ALU.max), R=[B_att8, B_smk], W=[B_att8])
                k.op("dve", lambda e: e.tensor_tensor(out=smv, in0=smv, in1=M_.unsqueeze(2).broadcast_to([16, 2, 192]), op=ALU.subtract), R=[B_smvb, B_att8], W=[B_smvb])
                k.op("act", lambda e: e.activation(out=smv, in_=smv, func=AF.Exp), R=[B_smvb], W=[B_smvb])
                k.op("dve", lambda e: e.tensor_reduce(out=RSM, in_=smv, axis=AX.X, op=ALU.add), R=[B_smvb], W=[B_att8])
                k.op("dve", lambda e: e.tensor_tensor(out=ES, in0=sink16[:, :], in1=M_, op=ALU.subtract), R=[B_att8, B_smk], W=[B_att8])
                k.op("act", lambda e: e.activation(out=ES, in_=ES, func=AF.Exp), R=[B_att8], W=[B_att8])
                k.op("dve", lambda e: e.tensor_tensor(out=DEN, in0=ES, in1=RSM, op=ALU.add), R=[B_att8], W=[B_att8])
                k.op("dve", lambda e: e.reciprocal(out=RDEN, in_=DEN), R=[B_att8], W=[B_att8])
                Pv = Pb[:16, 0:2, 0:192]
                k.op("dve", lambda e: e.tensor_tensor(out=Pv, in0=smv, in1=RDEN.unsqueeze(2).broadcast_to([16, 2, 192]), op=ALU.mult), R=[B_smvb, B_att8], W=[B_Pb])
                if '3' in DBG:
                    continue
                for j in range(2):
                    k.op("pe", lambda e, j=j: e.transpose(out=psb(0)[:, (2 * j) * 16:(2 * j) * 16 + 16], in_=Pv[:, j, 0:128], identity=identb[:16, :16]), R=[B_Pb, B_cm], W=[B_ps[0]])
                    k.op("pe", lambda e, j=j: e.transpose(out=psb(0)[:64, (2 * j + 1) * 16:(2 * j + 1) * 16 + 16], in_=Pv[:, j, 128:192], identity=identb[:16, :16]), R=[B_Pb, B_cm], W=[B_ps[0]])
                k.op("act", lambda e: e.copy(out=PT[:, 0, 0:64], in_=psb(0)[:, 0:64]), R=[B_ps[0]], W=[B_PT])
                for j in range(2):
                    k.op("pe", lambda e, j=j: e.matmul(out=ps[7][:16, j * 64:(j + 1) * 64], lhsT=PT[:, 0, (2 * j) * 16:(2 * j) * 16 + 16], rhs=cvb[:, j * 64:(j + 1) * 64], start=True, stop=False), R=[B_PT, B_cvb], W=[B_ps[7]])
                    k.op("pe", lambda e, j=j: e.matmul(out=ps[7][:16, j * 64:(j + 1) * 64], lhsT=PT[:64, 0, (2 * j + 1) * 16:(2 * j + 1) * 16 + 16], rhs=vb[par][:64, j * 64:(j + 1) * 64], start=False, stop=True), R=[B_PT, B_vb[par]], W=[B_ps[7]])
                k.op("act", lambda e, sq=sq: e.copy(out=osb[:16, sq * 128:(sq + 1) * 128], in_=ps[7][:16, 0:128]), R=[B_ps[7]], W=[B_X1[NXB - 1][0]])
                yield
            if '2' in DBG:
                return
            B_osc = Buf("osc")
            k.dma("sp", [(osc[:, :], osb[:16, :])], R=[B_X1[NXB - 1][0]], W=[B_osc], dsem=k.site("osc_w"))
            srcv = osc.rearrange("(c t) (s j d) -> t j s c d", t=4, s=NSEQ, j=2)
            k.dma("sp", [(ymix[t_:64:4, 512 + j * 256:512 + (j + 1) * 256].rearrange("p (c d) -> p c d", c=4), srcv[t_, j]) for t_ in range(4) for j in range(2)],
                  R=[B_osc], W=[B_ymix], dsem=k.site("osc_r"))

        def rotary(v3, B_v, nh, T, cosb, sinb):
            cb = cosb.unsqueeze(1).broadcast_to([T, nh, 8]); sbb = sinb.unsqueeze(1).broadcast_to([T, nh, 8])
            t = [rt[:T, i, 0:nh * 8].rearrange("p (h d) -> p h d", h=nh) for i in range(4)]
            x1 = v3[:, :, 0:8]; x2 = v3[:, :, 8:16]
            k.op("dve", lambda e: e.tensor_tensor(out=t[0], in0=x1, in1=cb, op=ALU.mult), R=[B_v, B_rope], W=[B_rt])
            k.op("pool", lambda e: e.tensor_tensor(out=t[1], in0=x2, in1=sbb, op=ALU.mult), R=[B_v, B_rope], W=[B_rt])
            k.op("dve", lambda e: e.tensor_tensor(out=t[2], in0=x2, in1=cb, op=ALU.mult), R=[B_v, B_rope], W=[B_rt])
            k.op("pool", lambda e: e.tensor_tensor(out=t[3], in0=x1, in1=sbb, op=ALU.mult), R=[B_v, B_rope], W=[B_rt])
            k.op("dve", lambda e: e.tensor_tensor(out=x1, in0=t[0], in1=t[1], op=ALU.subtract), R=[B_rt], W=[B_v])
            k.op("dve", lambda e: e.tensor_tensor(out=x2, in0=t[2], in1=t[3], op=ALU.add), R=[B_rt], W=[B_v])

        wq = {"n": 0}

        def wload(src_ap):
            i = wq["n"] % NSLOT
            wq["n"] += 1
            k.dma("sp", [(wslot[i][:, :], src_ap)], R=[B_wup_s, B_wdn_s, B_wout_s], W=[B_wslot[i]], dsem=D_wslot[i])
            return i

        def mlp(T, nsub, y_rows_fn, xb=0, part="all"):
            xb = xb % NXB
            X1v = X1[:, xb]; B_X1v = B_X1[xb]; h2Tv = h2T[:, xb]; B_h2Tv = B_h2T[xb]
            TTl = (nsub - 1) * 128 + T
            if part == "tail":
                for s_ in range(nsub):
                    Ts = T if s_ == nsub - 1 else 128
                    layer_norm_rows(X1v[:, s_, :], B_X1v[s_], Ts, 4, X1v[:, s_, :], B_X1v[s_])
                    yield
                    k.dma("sp", [(y_rows_fn(s_, Ts), X1v[:Ts, s_, :])], R=[B_X1v[s_]], W=[B_dummy], dsem=k.site(f"yo{s_}"))
                    yield
                return

            def upsrc(g2):
                return wup_s[g2 // 4, :, (g2 % 4) * 1024:(g2 % 4 + 1) * 1024]
            WD = NSLOT - 1
            pend = [wload(upsrc(g_)) for g_ in range(WD)]
            for g in range(32):
                i = pend.pop(0)
                if g + WD < 32:
                    pend.append(wload(upsrc(g + WD)))
                wv = wslot[i][:, :].rearrange("p (fc kk f) -> p fc kk f", fc=1, kk=8)
                for fcl in range(1):
                    fc = g
                    ub = fc % 2
                    for kk in range(8):
                        k.op("pe", lambda e, fcl=fcl, kk=kk, ub=ub, wv=wv: e.matmul(out=ps[6 + ub][:, 0:TTl], lhsT=wv[:, fcl, kk, :], rhs=h2Tv[:, kk, 0:TTl], start=(kk == 0), stop=(kk == 7)),
                             R=[B_wslot[i]] + B_h2Tv[:nsub], W=[B_ps[6 + ub]])
                    k.op("act", lambda e, ub=ub: e.activation(out=urelu[ub][:, 0:TTl], in_=ps[6 + ub][:, 0:TTl], func=AF.Relu), R=[B_ps[6 + ub]], W=[B_urelu[ub]])
                    k.op("pool", lambda e, fc=fc, ub=ub: e.tensor_tensor(out=uT[:, fc, 0:TTl], in0=urelu[ub][:, 0:TTl], in1=urelu[ub][:, 0:TTl], op=ALU.mult), R=[B_urelu[ub]], W=[B_uT[fc], B_stgb[fc // 16]])
                    yield
            for n in range(2):
                def dnsrc(g2, n=n):
                    return wdn_s[n, g2 // 4, :, (g2 % 4) * 1024:(g2 % 4 + 1) * 1024]
                pend = [wload(dnsrc(g_)) for g_ in range(WD)]
                for g in range(16):
                    i = pend.pop(0)
                    if g + WD < 16:
                        pend.append(wload(dnsrc(g + WD)))
                    wv = wslot[i][:, :].rearrange("p (a n) -> p a n", a=2)
                    for s_ in range(nsub):
                        Ts = T if s_ == nsub - 1 else 128
                        for fl in range(2):
                            fk = g * 2 + fl
                            k.op("pe", lambda e, s_=s_, fl=fl, fk=fk, wv=wv, Ts=Ts: e.matmul(out=ps[4 + s_][:Ts, :], lhsT=uT[:, fk, s_ * 128:s_ * 128 + Ts], rhs=wv[:, fl, :], start=(fk == 0), stop=(fk == 31)),
                                 R=[B_wslot[i], B_uT[fk]], W=[B_ps[4 + s_]])
                    yield
                for s_ in range(nsub):
                    Ts = T if s_ == nsub - 1 else 128
                    sl = slice(n * 512, (n + 1) * 512)
                    k.op("dve", lambda e, s_=s_, sl=sl, Ts=Ts: e.tensor_tensor(out=tmpf[:Ts, sl], in0=ps[4 + s_][:Ts, :], in1=ada[:Ts, 5, sl], op=ALU.mult), R=[B_ps[4 + s_], B_ada], W=[B_tmpf])
                    k.op("dve", lambda e, s_=s_, sl=sl, Ts=Ts: e.scalar_tensor_tensor(out=X1v[:Ts, s_, sl], in0=X1v[:Ts, s_, sl], scalar=ALPHA, in1=tmpf[:Ts, sl], op0=ALU.mult, op1=ALU.add),
                         R=[B_X1v[s_], B_tmpf], W=[B_X1v[s_]])
            if part == "main":
                return
            for s_ in range(nsub):
                Ts = T if s_ == nsub - 1 else 128
                layer_norm_rows(X1v[:, s_, :], B_X1v[s_], Ts, 4, X1v[:, s_, :], B_X1v[s_])
                yield
                k.dma("sp", [(y_rows_fn(s_, Ts), X1v[:Ts, s_, :])], R=[B_X1v[s_]], W=[B_dummy], dsem=k.site(f"yo{s_}"))

        def run(g):
            for _ in g:
                pass

        def zip_gens(ga, gb):
            da = db = False
            while not (da and db):
                if not da:
                    try:
                        next(ga)
                    except StopIteration:
                        da = True
                if not db:
                    try:
                        next(gb)
                    except StopIteration:
                        db = True
                yield

        def interleave(ga, gb):
            da = db = False
            while not (da and db):
                if not da:
                    try:
                        next(ga)
                    except StopIteration:
                        da = True
                if not db:
                    try:
                        next(gb)
                    except StopIteration:
                        db = True

        def adv_until(g, mark):
            for v in g:
                if v == mark:
                    return True
            return False

        def zip_until(ga, gb, mark_b):
            da = db = False
            while not (da and db):
                if not da:
                    try:
                        next(ga)
                    except StopIteration:
                        da = True
                if not db:
                    try:
                        if next(gb) == mark_b:
                            db = True
                    except StopIteration:
                        db = True

        k.op("dve", lambda e: e.memset(Hst[:, :], 0.0), R=[], W=[B_H])
        k.op("dve", lambda e: e.memset(Hbf[:, :], 0.0), R=[], W=[B_Hb])
        k.op("pool", lambda e: e.memset(xbcT[:, :, :], 0.0), R=[], W=[B_xbcT])
        for i in range(2):
            k.op("pool", lambda e, i=i: e.memset(kTb[i][:, :], 0.0), R=[], W=[B_kT[i]])
            k.op("pool", lambda e, i=i: e.memset(vb[i][:, :], 0.0), R=[], W=[B_vb[i]])

        if with_sample:
            compute_ada(c_s, NTS)
            late_weight_prep()
            fence()
            k.dma("sp", [(cms[:, :, :], cmask_s), (dmk[:16, :], dmask)], R=[], W=[B_cms, B_dmk], dsem=k.site("cms"))
            k.dma("sp", [(smk[:, :], smk_d), (sink16[:, :], sink16_d)], R=[], W=[B_smk], dsem=k.site("smk"))
            run(mixer_tile(x_s, rope_s, NTS, 0, "full", None, 0, smp=True, xb=1))
            fence()
            run(mlp(NTS, 1, lambda s_, Ts: y_s[0:Ts, :], xb=1))
        if not with_sample:
            late_weight_prep()
        compute_ada(c_p.broadcast_to([128, D]), 128)

        def pre_gen(t):
            return mixer_tile(x_prev[t * 128:(t + 1) * 128, :], rope_prev[:, :], 128, 1, "statekv" if t == NT - 1 else "state", None, 0)

        if 'Q' in DBG:
            for t in range(NT):
                run(pre_gen(t))
        else:
            curp = pre_gen(0)
            adv_until(curp, "P2")
            for t in range(NT):
                nxtp = pre_gen(t + 1) if t + 1 < NT else None
                if nxtp is not None:
                    zip_until(curp, nxtp, "P2")
                else:
                    run(curp)
                curp = nxtp
        k.op("dve", lambda e: e.tensor_scalar(out=Hst[:, :], in0=Hst[:, :], scalar1=flg[:, 0:1], scalar2=None, op0=ALU.mult), R=[B_H, B_am], W=[B_H])
        k.op("act", lambda e: e.copy(out=Hbf[:, :], in_=Hst[:, :]), R=[B_H], W=[B_Hb])
        k.op("pool", lambda e: e.tensor_scalar(out=xbcT[:, :, 0:3], in0=xbcT[:, :, 0:3], scalar1=flg[:, 0:1], scalar2=None, op0=ALU.mult), R=[B_xbcT, B_am], W=[B_xbcT])

        def final_outputs():
            k.dma("sp", [(k_p[:, :], kvs[:, 0:128]), (v_p[:, :], kvs[:, 128:256])], R=[B_kvs], W=[B_dummy], dsem=k.site("kv_p"))
            k.dma("sp", [(conv_p[:, c * 128:(c + 1) * 128].rearrange("j p -> p j"), xbcT[:, c, 0:3]) for c in range(8)], R=[B_xbcT], W=[B_dummy], dsem=k.site("conv_p"))
            for c in range(4):
                k.op("pe", lambda e, c=c: e.transpose(out=ps[4][:, c * 128:(c + 1) * 128], in_=Hst[:, c * 128:(c + 1) * 128], identity=identf), R=[B_H, B_cm], W=[B_ps[4]])
            k.op("act", lambda e: e.copy(out=tmpf[:, 0:512], in_=ps[4][:, :]), R=[B_ps[4]], W=[B_tmpf])
            k.dma("sp", [(ssm_p.rearrange("(c p) n -> p c n", p=128), tmpf[:, 0:512].rearrange("p (c n) -> p c n", c=4))], R=[B_tmpf], W=[B_dummy], dsem=k.site("ssm_p"))

        def tile_gen(t):
            return mixer_tile(x_main[t * 128:(t + 1) * 128, :], rope_main[t * 128:(t + 1) * 128, :], 128, t % 2, "full",
                              am[:, 0, :] if t == 0 else am[:, 1, :], t % NSUB, xb=0)

        def mlp_st(st, part):
            return mlp(128, NSUB, lambda s_, Ts, st=st: y_main[(st * NSUB + s_) * 128:(st * NSUB + s_) * 128 + Ts, :], xb=0, part=part)

        def empty():
            return
            yield

        cur = tile_gen(0)
        adv_until(cur, "S")
        for t in range(NT):
            nxt = tile_gen(t + 1) if t + 1 < NT else None
            if nxt is not None and 'P' not in DBG:
                zip_until(cur, nxt, "AB")
            else:
                run(cur)
                if nxt is not None:
                    adv_until(nxt, "AB")
            if t == NT - 1:
                final_outputs()
            tail = empty()
            if (t + 1) % NSUB == 0:
                st = t // NSUB
                run(mlp_st(st, "main"))
                tail = mlp_st(st, "tail")
            if nxt is not None:
                zip_until(tail, nxt, "S")
            else:
                run(tail)
            cur = nxt

        k.finish("sp")
        build.ninstr = k.ninstr
    return nc


def _rope_table(pos):
    half = 8
    inv = (np.float32(500000.0) ** (-np.arange(half, dtype=np.float32) / np.float32(half))).astype(np.float32)
    ang = (pos.astype(np.float32)[:, None] * inv[None, :]).astype(np.float32)
    return np.concatenate([np.cos(ang), np.sin(ang)], 1).astype(np.float32)


def _band_mask(first_masked):
    i = np.arange(128)[:, None]
    j = np.arange(256)[None, :]
    ok = (j > i) & (j <= i + 128)
    if first_masked:
        ok = ok & (j >= 128)
    return np.where(ok, 0.0, NEG).astype(np.float32)


def _prep_inputs(inp, LH):
    f = np.ascontiguousarray
    xprompt = inp["x_prompt"]
    B = xprompt.shape[0]
    w_in = inp["w_in"][0]
    Z_END = 512; XBC_END = 1536; DT_END = 1544; Q_END = 2056; K_END = 2184
    qcols = []
    for c in range(4):
        for j in range(2):
            h = c + 4 * j
            qcols += list(range(DT_END + h * 64, DT_END + (h + 1) * 64))
    cols = list(range(0, XBC_END)) + qcols + list(range(Q_END, K_END)) + list(range(K_END, K_END + 128)) + list(range(XBC_END, DT_END))
    w_in_r = f(w_in[:, cols])
    sinks = inp["sinks"][0]
    sinks_p = f(np.array([sinks[c + 4 * j] for j in range(2) for c in range(4)], np.float32)[None, :])
    U, SL, BD = _consts(128, 1)
    ident = np.eye(128, dtype=np.float32)
    cau = U.copy()
    cmask = f(np.stack([U, SL, np.ones((128, 128), np.float32), ident, cau], 1))
    common = {
        "ln_in_g": f(inp["ln_in_g"][None, :]), "ln_in_b": f(inp["ln_in_b"][None, :]),
        "w_ada": f(inp["w_ada"][0]), "b_ada": f(inp["b_ada"]), "w_in": w_in_r,
        "conv_w": f(inp["conv_w"][0]), "conv_b": f(inp["conv_b"]), "dt_bias": f(inp["dt_bias"]),
        "A_log": f(inp["A_log"]), "D_skip": f(inp["D_skip"]), "ssm_norm_w": f(inp["ssm_norm_w"]),
        "sinks_p": sinks_p, "w_out": f(inp["w_out"][0]), "ln1_g": f(inp["ln1_g"]), "ln1_b": f(inp["ln1_b"]),
        "w_up": f(inp["w_up"][0]), "w_down": f(inp["w_down"][0]), "ln2_g": f(inp["ln2_g"]), "ln2_b": f(inp["ln2_b"]),
        "cmask": cmask, "m2": _band_mask(False),
    }
    Us, SLs, BDs = _consts(64, 16)
    cms_ = np.zeros((128, 3, 128), np.float32)
    for i_, a_ in enumerate((Us, SLs, BDs)):
        cms_[:64, i_, :64] = a_
    smk_ = np.zeros((128, 16), np.float32)
    smk_[np.arange(64), np.arange(64) // 4] = 1.0
    sink16 = np.zeros((16, 2), np.float32)
    dm = np.full((16, 128 + 16 * 64), NEG, np.float32)
    for c_ in range(4):
        for t_ in range(4):
            r_ = c_ * 4 + t_
            for j_ in range(2):
                sink16[r_, j_] = sinks[c_ + 4 * j_]
            dm[r_, t_ + 1:128] = 0.0
            for s_ in range(16):
                dm[r_, 128 + s_ * 64 + s_ * 4:128 + s_ * 64 + s_ * 4 + t_ + 1] = 0.0
    common.update({"cmask_s": f(cms_), "smk": smk_, "sink16": sink16, "dmask": dm,
                   "rope_s": f(np.tile(_rope_table(PAST + np.arange(4)), (16, 1)))})
    maps = []
    for c in range(8):
        b, half = c // 2, c % 2
        m = dict(common)
        sq = slice(16 * c, 16 * (c + 1))
        m["x_s"] = f(inp["x_sample"][sq].reshape(64, D))
        m["c_s"] = f(np.repeat(inp["c_sample"][sq], 4, axis=0))
        m["sconv"] = f(inp["state_conv"][0, sq].reshape(48, D))
        m["sssm"] = f(inp["state_ssm"][0, sq].reshape(16, 512, 128))
        m["ck"] = f(inp["cache_k"][0, sq].reshape(16, 128, 128))
        m["cv"] = f(inp["cache_v"][0, sq].reshape(16, 128, 128))
        m["x_prev"] = f(xprompt[b, 0:LH])
        m["x_main"] = f(xprompt[b, half * LH:(half + 1) * LH])
        m["c_p"] = f(inp["c_prompt"][b][None, :])
        fl = np.zeros((128, 2), np.float32); fl[:, 0] = float(half)
        m["flag"] = fl
        m["m0"] = _band_mask(half == 0)
        m["rope_main"] = _rope_table(np.arange(half * LH, (half + 1) * LH))
        m["rope_prev"] = _rope_table(np.arange(LH - 128, LH))
        maps.append(m)
    return maps


_CACHE = {}


def kernel(**inputs):
    inp = {k_: np.asarray(v) for k_, v in inputs.items()}
    B, S, _ = inp["x_prompt"].shape
    LH = S // 2
    key = LH
    if key not in _CACHE:
        _CACHE[key] = build(LH)
    nc = _CACHE[key]
    maps = _prep_inputs(inp, LH)
    res = run_bass_kernel_spmd(nc, maps, core_ids=list(range(8)))
    R = res.results
    y_prompt = np.stack([np.concatenate([R[2 * b]["y_main"], R[2 * b + 1]["y_main"]], 0) for b in range(B)], 0)
    conv_p = np.stack([R[2 * b + 1]["conv_p"] for b in range(B)], 0)[None]
    ssm_p = np.stack([R[2 * b + 1]["ssm_p"].reshape(8, 64, 128) for b in range(B)], 0)[None]
    k_p = np.stack([R[2 * b + 1]["k_p"].reshape(128, 2, 64) for b in range(B)], 0)[None]
    v_p = np.stack([R[2 * b + 1]["v_p"].reshape(128, 2, 64) for b in range(B)], 0)[None]
    y_sample = np.concatenate([R[c]["y_s"].reshape(16, 4, D) for c in range(8)], 0)
    conv_s = np.concatenate([R[c]["conv_s"] for c in range(8)], 0)[None]
    ssm_s = np.concatenate([R[c]["ssm_s"].reshape(16, 8, 64, 128) for c in range(8)], 0)[None]
    k_s = np.concatenate([R[c]["k_s"].reshape(16, 128, 2, 64) for c in range(8)], 0)[None]
    v_s = np.concatenate([R[c]["v_s"].reshape(16, 128, 2, 64) for c in range(8)], 0)[None]
    return (y_prompt, y_sample, conv_p, ssm_p, k_p, v_p, conv_s, ssm_s, k_s, v_s)
```
